# Optimizing a Trainium2 kernel written in Bass

```python
import math
import jax, jax.numpy as jnp
from jax import lax
import numpy as np

D_MODEL = 1024
BATCH = 16
SEQ = 2048
DEPTH = 4

CHUNK = 64
N_MIXERS = 2
N_GDN = (DEPTH + 1) // 2
N_HGRN = DEPTH // 2
EPS = 1e-6

GDN_HEADS = 8
GDN_HEAD_DIM = D_MODEL // GDN_HEADS
GDN_KEY_WIDTH = GDN_HEADS * GDN_HEAD_DIM
GDN_VAL_WIDTH = GDN_HEADS * GDN_HEAD_DIM
GDN_CONV_WIDTH = 2 * GDN_KEY_WIDTH + GDN_VAL_WIDTH
GDN_IN_WIDTH = GDN_CONV_WIDTH + GDN_VAL_WIDTH + 2 * GDN_HEADS
CONV_K = 4

HGRN_FORGET_DIM = 128
HGRN_HEADS = D_MODEL // HGRN_FORGET_DIM
HGRN_VALUE_DIM = D_MODEL // HGRN_HEADS
HGRN_WIDTH = HGRN_HEADS * HGRN_FORGET_DIM
HGRN_IN_WIDTH = 4 * HGRN_WIDTH
SUB = 16
N_SUB = CHUNK // SUB

MLP_HIDDEN = 4 * D_MODEL

kernel_name = "hybrid_gdn_hgrn2_stream_encoder"


def _rmsnorm(x, w):
    xf = x.astype(jnp.float32)
    y = xf * lax.rsqrt(jnp.mean(xf * xf, axis=-1, keepdims=True) + EPS)
    return (y * w.astype(jnp.float32)).astype(x.dtype)


def _l2norm(x):
    return x * lax.rsqrt(jnp.sum(x * x, axis=-1, keepdims=True) + EPS)


def _causal_conv(x, w):
    K = w.shape[0]
    T = x.shape[1]
    xp = jnp.pad(x, ((0, 0), (K - 1, 0), (0, 0)))
    y = xp[:, 0:T] * w[0]
    for kk in range(1, K):
        y = y + xp[:, kk:kk + T] * w[kk]
    return y


def _to_chunks(a, nc):
    B = a.shape[0]
    H = a.shape[2]
    a = a.reshape((B, nc, CHUNK, H) + a.shape[3:])
    return jnp.moveaxis(a, (1, 3), (0, 2))


def _from_chunks(o):
    nc, B, H, C, Dv = o.shape
    return jnp.moveaxis(o, (0, 2), (1, 3)).reshape(B, nc * C, H, Dv)


def _gdn_chunked(q, k, v, beta, g):
    B, T, H, DK = q.shape
    DV = v.shape[-1]
    nc = T // CHUNK
    q = _to_chunks(q * (DK ** -0.5), nc)
    k = _to_chunks(k, nc)
    v = _to_chunks(v, nc)
    beta = _to_chunks(beta, nc)
    gc = jnp.cumsum(_to_chunks(g, nc), axis=-1)
    causal = jnp.tril(jnp.ones((CHUNK, CHUNK), dtype=bool))
    strict = jnp.tril(jnp.ones((CHUNK, CHUNK), dtype=bool), k=-1)
    decay = jnp.exp(jnp.where(causal, gc[..., :, None] - gc[..., None, :], -jnp.inf))
    kb = k * beta[..., None]
    L = jnp.where(strict, jnp.einsum('nbhik,nbhjk->nbhij', kb, k) * decay, 0.0)
    rhs = jnp.concatenate([v * beta[..., None], kb * jnp.exp(gc)[..., None]], axis=-1)
    sol = lax.linalg.triangular_solve(L, rhs, left_side=True, lower=True, unit_diagonal=True)
    u = sol[..., :DV]
    w = sol[..., DV:]
    a_qk = jnp.where(causal, jnp.einsum('nbhik,nbhjk->nbhij', q, k) * decay, 0.0)
    q_dec = q * jnp.exp(gc)[..., None]
    k_dec = k * jnp.exp(gc[..., -1:] - gc)[..., None]
    chunk_decay = jnp.exp(gc[..., -1])

    def step(S, xs):
        qd, a_c, u_c, w_c, kd, dl = xs
        v_new = u_c - jnp.einsum('bhck,bhkv->bhcv', w_c, S)
        o = jnp.einsum('bhck,bhkv->bhcv', qd, S) + jnp.einsum('bhij,bhjv->bhiv', a_c, v_new)
        S = S * dl[..., None, None] + jnp.einsum('bhck,bhcv->bhkv', kd, v_new)
        return S, o

    S0 = jnp.zeros((B, H, DK, DV), jnp.float32)
    _, o = lax.scan(step, S0, (q_dec, a_qk, u, w, k_dec, chunk_decay))
    return _from_chunks(o)


def _hgrn2_chunked(q, k, v, g):
    B, T, H, DK = q.shape
    DV = v.shape[-1]
    nc = T // CHUNK
    q = _to_chunks(q * (DK ** -0.5), nc)
    k = _to_chunks(k, nc)
    v = _to_chunks(v, nc)
    g = _to_chunks(g, nc)
    gc = jnp.cumsum(g, axis=-2)
    gb = (gc - g)[:, :, :, ::SUB]
    pos = jnp.arange(CHUNK)
    off_mask = (pos[None, :] // SUB) < jnp.arange(N_SUB)[:, None]
    diag_mask = jnp.tril(jnp.ones((SUB, SUB), dtype=bool))

    def step(S, xs):
        q_c, k_c, v_c, gc_c, gb_c = xs
        qs = q_c.reshape(B, H, N_SUB, SUB, DK)
        ks = k_c.reshape(B, H, N_SUB, SUB, DK)
        vs = v_c.reshape(B, H, N_SUB, SUB, DV)
        gs = gc_c.reshape(B, H, N_SUB, SUB, DK)
        q_off = qs * jnp.exp(gs - gb_c[:, :, :, None])
        k_off = k_c[:, :, None] * jnp.exp(
            jnp.where(off_mask[:, :, None], gb_c[:, :, :, None] - gc_c[:, :, None], -jnp.inf))
        a_off = jnp.einsum('bhsik,bhsjk->bhsij', q_off, k_off)
        dec = jnp.exp(jnp.where(diag_mask[:, :, None],
                                gs[..., :, None, :] - gs[..., None, :, :], -jnp.inf))
        a_diag = jnp.einsum('bhsik,bhsjk,bhsijk->bhsij', qs, ks, dec)
        o = (jnp.einsum('bhsij,bhjv->bhsiv', a_off, v_c)
             + jnp.einsum('bhsij,bhsjv->bhsiv', a_diag, vs)).reshape(B, H, CHUNK, DV)
        o = o + jnp.einsum('bhck,bhkv->bhcv', q_c * jnp.exp(gc_c), S)
        g_last = gc_c[:, :, -1]
        S = S * jnp.exp(g_last)[..., None] + jnp.einsum(
            'bhck,bhcv->bhkv', k_c * jnp.exp(g_last[:, :, None] - gc_c), v_c)
        return S, o

    S0 = jnp.zeros((B, H, DK, DV), jnp.float32)
    _, o = lax.scan(step, S0, (q, k, v, gc, gb))
    return _from_chunks(o)


def _gated_deltanet(y, w_in, conv_w, a_log, dt_bias, onorm_w, w_out):
    B, T, _ = y.shape
    f32 = jnp.float32
    proj = y @ w_in
    qkv = proj[..., :GDN_CONV_WIDTH]
    gate = proj[..., GDN_CONV_WIDTH:GDN_CONV_WIDTH + GDN_VAL_WIDTH]
    a = proj[..., GDN_CONV_WIDTH + GDN_VAL_WIDTH:GDN_CONV_WIDTH + GDN_VAL_WIDTH + GDN_HEADS]
    b = proj[..., GDN_CONV_WIDTH + GDN_VAL_WIDTH + GDN_HEADS:]
    qkv = jax.nn.silu(_causal_conv(qkv, conv_w)).astype(f32)
    q = _l2norm(qkv[..., :GDN_KEY_WIDTH].reshape(B, T, GDN_HEADS, GDN_HEAD_DIM))
    k = _l2norm(qkv[..., GDN_KEY_WIDTH:2 * GDN_KEY_WIDTH].reshape(B, T, GDN_HEADS, GDN_HEAD_DIM))
    v = qkv[..., 2 * GDN_KEY_WIDTH:].reshape(B, T, GDN_HEADS, GDN_HEAD_DIM)
    beta = jax.nn.sigmoid(b.astype(f32))
    g = -jnp.exp(a_log.astype(f32)) * jax.nn.softplus(a.astype(f32) + dt_bias.astype(f32))
    o = _gdn_chunked(q, k, v, beta, g)
    o = _rmsnorm(o, onorm_w) * jax.nn.silu(gate.astype(f32)).reshape(B, T, GDN_HEADS, GDN_HEAD_DIM)
    return o.reshape(B, T, GDN_VAL_WIDTH).astype(y.dtype) @ w_out


def _hgrn2(y, w_in, lb, gnorm_w, w_out):
    B, T, _ = y.shape
    f32 = jnp.float32
    proj = y @ w_in
    q, f, i, gate = jnp.split(proj, 4, axis=-1)
    f = f.astype(f32)
    log_forget = jnp.logaddexp(jnp.log(lb), jnp.log1p(-lb) + jax.nn.log_sigmoid(f))
    k = (1.0 - lb) * jax.nn.sigmoid(-f)
    q = jax.nn.silu(q.astype(f32))
    heads = lambda t: t.reshape(B, T, HGRN_HEADS, -1)
    o = _hgrn2_chunked(heads(q), heads(k), heads(i.astype(f32)), heads(log_forget))
    o = _rmsnorm(o.reshape(B, T, HGRN_WIDTH), gnorm_w) * jax.nn.silu(gate.astype(f32))
    return o.astype(y.dtype) @ w_out


def _sq_relu_mlp(h, w_up, w_down):
    return jnp.square(jax.nn.relu(h @ w_up)) @ w_down


def setup_inputs(seed: int = 0) -> dict:
    key = jax.random.key(seed)
    ks = jax.random.split(key, 17)
    f32 = jnp.float32
    nrm = lambda kk, shape: jax.random.normal(kk, shape, f32)
    x = nrm(ks[0], (BATCH, SEQ, D_MODEL))
    gdn_w_in = nrm(ks[1], (N_GDN, D_MODEL, GDN_IN_WIDTH)) * D_MODEL ** -0.5
    gdn_conv = nrm(ks[2], (N_GDN, CONV_K, GDN_CONV_WIDTH)) * CONV_K ** -0.5
    gdn_a_log = jnp.log(jax.random.uniform(ks[3], (N_GDN, GDN_HEADS), f32, 1.0, 16.0))
    dt = jnp.exp(jax.random.uniform(ks[4], (N_GDN, GDN_HEADS), f32, math.log(1e-3), math.log(1e-1)))
    gdn_dt_bias = dt + jnp.log(-jnp.expm1(-dt))
    gdn_onorm = 1.0 + 0.02 * nrm(ks[5], (N_GDN, GDN_HEAD_DIM))
    gdn_w_out = nrm(ks[6], (N_GDN, GDN_VAL_WIDTH, D_MODEL)) * GDN_VAL_WIDTH ** -0.5
    hgrn_w_in = nrm(ks[7], (N_HGRN, D_MODEL, HGRN_IN_WIDTH)) * D_MODEL ** -0.5
    hgrn_lb_logits = 0.1 * nrm(ks[8], (DEPTH, HGRN_WIDTH))
    hgrn_gnorm = 1.0 + 0.02 * nrm(ks[9], (N_HGRN, HGRN_WIDTH))
    hgrn_w_out = nrm(ks[10], (N_HGRN, HGRN_WIDTH, D_MODEL)) * HGRN_WIDTH ** -0.5
    norm_mix = 1.0 + 0.02 * nrm(ks[11], (DEPTH, D_MODEL))
    norm_mlp = 1.0 + 0.02 * nrm(ks[12], (DEPTH, D_MODEL))
    mlp_w_up = nrm(ks[13], (DEPTH, D_MODEL, MLP_HIDDEN)) * D_MODEL ** -0.5
    mlp_w_down = nrm(ks[14], (DEPTH, MLP_HIDDEN, D_MODEL)) * MLP_HIDDEN ** -0.5
    norm_final = 1.0 + 0.02 * nrm(ks[15], (D_MODEL,))
    return {"x": x, "gdn_w_in": gdn_w_in, "gdn_conv": gdn_conv, "gdn_a_log": gdn_a_log,
            "gdn_dt_bias": gdn_dt_bias, "gdn_onorm": gdn_onorm, "gdn_w_out": gdn_w_out,
            "hgrn_w_in": hgrn_w_in, "hgrn_lb_logits": hgrn_lb_logits, "hgrn_gnorm": hgrn_gnorm,
            "hgrn_w_out": hgrn_w_out, "norm_mix": norm_mix, "norm_mlp": norm_mlp,
            "mlp_w_up": mlp_w_up, "mlp_w_down": mlp_w_down, "norm_final": norm_final}


def reference(x, gdn_w_in, gdn_conv, gdn_a_log, gdn_dt_bias, gdn_onorm, gdn_w_out,
              hgrn_w_in, hgrn_lb_logits, hgrn_gnorm, hgrn_w_out, norm_mix, norm_mlp,
              mlp_w_up, mlp_w_down, norm_final):
    sm = jax.nn.softmax(hgrn_lb_logits.astype(jnp.float32), axis=0)
    lower_bounds = jnp.cumsum(sm, axis=0) - sm[0]
    h = x
    for i in range(DEPTH):
        j = i // N_MIXERS
        y = _rmsnorm(h, norm_mix[i])
        if i % N_MIXERS == 0:
            y = _gated_deltanet(y, gdn_w_in[j], gdn_conv[j], gdn_a_log[j], gdn_dt_bias[j],
                                gdn_onorm[j], gdn_w_out[j])
        else:
            y = _hgrn2(y, hgrn_w_in[j], lower_bounds[i], hgrn_gnorm[j], hgrn_w_out[j])
        h = h + y.astype(h.dtype)
        h = h + _sq_relu_mlp(_rmsnorm(h, norm_mlp[i]), mlp_w_up[i], mlp_w_down[i]).astype(h.dtype)
    return _rmsnorm(h, norm_final)
```

```python
import numpy as np
from contextlib import ExitStack
import concourse.bass as bass
import concourse.mybir as mybir
from concourse.bass_utils import run_bass_kernel_spmd

F32 = mybir.dt.float32
BF16 = mybir.dt.bfloat16
AF = mybir.ActivationFunctionType
ALU = mybir.AluOpType

D = 1024
NCH = 8
H = 8
C = 128
EPS = 1e-6
GDN_IN = 4112
MAXV = 30000


class _Op:
    __slots__ = ("eng", "fn", "deps", "kind", "stream", "sig", "upg")

    def __init__(self, eng, fn, deps, kind, stream):
        self.eng, self.fn, self.deps, self.kind, self.stream, self.sig = eng, fn, deps, kind, stream, None


class Sched:
    ENGS = ("pe", "act", "dve", "pool", "sp")

    def __init__(self):
        self.ops = []
        self.lastw = {}
        self.readers = {}
        self.stream_last = {}

    def add(self, eng, fn, r=(), w=(), kind="c", stream=None):
        i = len(self.ops)
        deps = {}

        def upd(d, t):
            o = deps.get(d)
            if o is None or (t == "raw") or (t == "waw" and o == "war"):
                deps[d] = t
        for k in r:
            if k in self.lastw:
                upd(self.lastw[k], "raw")
        for k in w:
            if k in self.lastw:
                upd(self.lastw[k], "waw")
            for d in self.readers.get(k, ()):
                if d != i:
                    upd(d, "war")
        op = _Op(eng, fn, deps, kind, stream)
        op.upg = dict(self.stream_last)
        self.ops.append(op)
        if kind == "d":
            self.stream_last[stream] = i
        for k in r:
            self.readers.setdefault(k, []).append(i)
        for k in w:
            self.lastw[k] = i
            self.readers[k] = []
        return i

    def dma(self, eng, fn, r, w, stream):
        return self.add(eng, fn, r, w, kind="d", stream=stream)

    def _needs_sync(self, op, D, t):
        if D.kind == "d":
            return True
        if D.eng != op.eng:
            return True
        if op.kind == "d":
            return True
        if op.eng == "pe":
            return False
        return True

    def emit(self, nc, es):
        ops = self.ops
        need = [False] * len(ops)
        for op in ops:
            for d, t in op.deps.items():
                if self._needs_sync(op, ops[d], t):
                    need[d] = True
        cnt = {}
        semkeys = []

        def nxt(key, inc):
            st = cnt.get(key)
            if st is None:
                st = cnt[key] = [0, 0]
                semkeys.append((key, 0))
            if st[1] + inc > MAXV:
                st[0] += 1
                st[1] = 0
                semkeys.append((key, st[0]))
            st[1] += inc
            return (key, st[0]), st[1]
        for i, op in enumerate(ops):
            if op.kind == "d":
                op.sig = nxt(("dma", op.stream), 16)
            elif need[i]:
                op.sig = nxt(("eng", op.eng), 1)
        sems = {}
        for sk in semkeys:
            nm = "s_%s_%s_%d" % (sk[0][0], str(sk[0][1]), sk[1])
            sems[sk] = es.enter_context(nc.semaphore(nm))
        self.nsems = len(sems)
        block = es.enter_context(nc.Block())
        per = {e: [] for e in self.ENGS}
        for i, op in enumerate(ops):
            per[op.eng].append(i)

        def run(eng_name, engine):
            waited = {}
            for i in per[eng_name]:
                op = ops[i]
                for d, t in sorted(op.deps.items()):
                    Dop = ops[d]
                    if not self._needs_sync(op, Dop, t):
                        continue
                    if Dop.kind == "d":
                        Dop = ops[op.upg[Dop.stream]]
                    sk, val = Dop.sig
                    if waited.get(sk, 0) >= val:
                        continue
                    engine.wait_ge(sems[sk], val)
                    waited[sk] = val
                ins = op.fn(engine)
                if op.sig is not None:
                    assert ins is not None
                    ins.then_inc(sems[op.sig[0]], 16 if op.kind == "d" else 1)

        @block.tensor
        def _(e):
            run("pe", e)

        @block.scalar
        def _(e):
            run("act", e)

        @block.vector
        def _(e):
            run("dve", e)

        @block.gpsimd
        def _(e):
            run("pool", e)

        @block.sync
        def _(e):
            run("sp", e)


def _consts():
    k = np.arange(128)[:, None]
    j = np.arange(128)[None, :]
    f = np.float32
    LT = (k <= j).astype(f)
    SUP = (k > j).astype(f)
    ID = (k == j).astype(f)
    ONES = np.ones((128, 128), f)
    NEG = -30000.0
    NEGS = np.where(k > j, 0.0, NEG).astype(f)
    NEGCT = np.where(j >= k, 0.0, NEG).astype(f)
    BD = ((k // 64) == (j // 64)).astype(f)
    OD = 1.0 - BD
    MASKT = (k <= j).astype(f)
    Ms = []
    for r in range(4):
        lo, hi = 32 * r, 32 * (r + 1)
        M = np.zeros((128, 128), f)
        M += ((j < lo) & (k > j) & (k < lo)) * 1.0
        M -= ((j >= lo) & (j < hi) & (k >= lo) & (k <= j)) * 1.0
        Ms.append(M.astype(f))
    MQ = ((k >= (j // 32) * 32) & (k <= j)).astype(f)
    RH = np.concatenate(Ms + [MQ, LT, ID], axis=1)
    cf = np.concatenate([LT, SUP, ID, ONES, RH, ID, ONES, NEGS, NEGCT, BD, OD, MASKT], axis=1)
    return np.ascontiguousarray(cf.astype(f))


CF_OFF = {"LT": 0, "SUP": 128, "ID": 256, "ONES": 384, "RH": 512}
CF_W = 512 + 896
CB_OFF = {"ID": 0, "ONES": 128, "NEGS": 256, "NEGCT": 384, "BD": 512, "OD": 640, "MASKT": 768}
CB_W = 896


def _cvec_layout():
    off = {}
    n = 0
    for nm, w in (("norm_mix", 32), ("norm_mlp", 32), ("norm_final", 8), ("conv", 2 * 24 * 4),
                  ("onorm", 2), ("gnorm", 16)):
        off[nm] = n
        n += w
    return off, n


CV_OFF, CV_W = _cvec_layout()


def _make_cvec(inp):
    cv = np.zeros((128, CV_W), np.float32)
    cv[:, CV_OFF["norm_mix"]:CV_OFF["norm_mix"] + 32] = inp["norm_mix"].reshape(4, 8, 128).transpose(2, 0, 1).reshape(128, 32)
    cv[:, CV_OFF["norm_mlp"]:CV_OFF["norm_mlp"] + 32] = inp["norm_mlp"].reshape(4, 8, 128).transpose(2, 0, 1).reshape(128, 32)
    cv[:, CV_OFF["norm_final"]:CV_OFF["norm_final"] + 8] = inp["norm_final"].reshape(8, 128).T
    cw = inp["gdn_conv"].reshape(2, 4, 24, 128).transpose(3, 0, 2, 1).reshape(128, 2 * 24 * 4)
    cv[:, CV_OFF["conv"]:CV_OFF["conv"] + 192] = cw
    cv[:, CV_OFF["onorm"]:CV_OFF["onorm"] + 2] = inp["gdn_onorm"].T
    cv[:, CV_OFF["gnorm"]:CV_OFF["gnorm"] + 16] = inp["hgrn_gnorm"].reshape(2, 8, 128).transpose(2, 0, 1).reshape(128, 16)
    return cv


RW_W = 32


def _make_rows(inp):
    r = np.zeros((RW_W,), np.float32)
    r[0:16] = inp["gdn_a_log"].reshape(-1)
    r[16:32] = inp["gdn_dt_bias"].reshape(-1)
    return np.ascontiguousarray(np.broadcast_to(r[None, :], (128, RW_W)))


def _make_lbrows(inp):
    r = inp["hgrn_lb_logits"].reshape(-1).astype(np.float32)
    return np.ascontiguousarray(np.broadcast_to(r[None, :], (128, 4096)))


class Prog:
    def __init__(self, nseq, T, TB, layers, final_norm):
        self.nseq, self.T, self.TB, self.layers, self.final_norm = nseq, T, TB, list(layers), final_norm
        self.NTOK = nseq * T
        self.NCI = TB // 128
        self.S = Sched()
        self.nc = bass.Bass("TRN2", target_bir_lowering=False)
        self.slab_i = 0
        self.pi = 0
        self.skip_mixer = False
        self.dbg_stop = 0
        self.dbg2 = 9
        self.dbg3 = 0

    def dram_in(self, name, shape, dt=F32):
        return self.nc.dram_tensor(name, list(shape), dt, kind="ExternalInput").ap()

    def sb(self, name, shape, dt):
        return self.es.enter_context(self.nc.sbuf_tensor(name, list(shape), dt))

    def op(self, eng, meth, r, w, *a, **kw):
        return self.S.add(eng, lambda e: getattr(e, meth)(*a, **kw), r, w)

    def mm(self, out, lhsT, rhs, start, stop, r, w):
        return self.op("pe", "matmul", r, w, out, lhsT=lhsT, rhs=rhs, start=start, stop=stop)

    def tr(self, out, in_, r, w):
        return self.op("pe", "transpose", r + ["cb"], w, out, in_, self.cba("ID"))

    def actv(self, out, in_, func, r, w, **kw):
        return self.op("act", "activation", r, w, out=out, in_=in_, func=func, **kw)

    def cp(self, out, in_, r, w):
        return self.op("dve", "tensor_copy", r, w, out=out, in_=in_)

    def tt(self, eng, out, in0, in1, op, r, w):
        return self.op(eng, "tensor_tensor", r, w, out=out, in0=in0, in1=in1, op=op)

    def ts(self, eng, out, in0, s1, s2, op0, op1, r, w):
        return self.op(eng, "tensor_scalar", r, w, out=out, in0=in0, scalar1=s1, scalar2=s2, op0=op0, op1=op1)

    def stt(self, out, in0, scalar, in1, op0, op1, r, w):
        return self.op("dve", "scalar_tensor_tensor", r, w, out=out, in0=in0, scalar=scalar, in1=in1, op0=op0, op1=op1)

    def dma(self, eng, out, in_, r, w, stream):
        return self.S.dma(eng, lambda e: e.dma_start(out=out, in_=in_), r, w, stream)

    def cfa(self, name, w=128):
        o = CF_OFF[name]
        return self.cf[:, o:o + w]

    def cba(self, name):
        o = CB_OFF[name]
        return self.cb[:, o:o + 128]

    def psk(self, bank, c0=0, c1=512):
        return [("ps", bank)]

    def bigk(self, lo, hi):
        return [("big", i) for i in range(lo // 512, (hi + 511) // 512)]

    def load_slab(self, src_ap, nc_, ncols):
        i = self.slab_i % 2
        self.slab_i += 1
        t = self.slabs[i]
        view = t[:, 0:nc_ * ncols].rearrange("p (c n) -> p c n", c=nc_)
        self.dma("pool", view, src_ap, [], [("slab", i)], "slab%d" % i)
        return ("slab", i), view

    def build(self):
        nc = self.nc
        NTOK, T, TB = self.NTOK, self.T, self.TB
        with ExitStack() as es:
            self.es = es
            self.hin = self.dram_in("hT_in", [D, NTOK])
            self.hout = nc.dram_tensor("hT_out", [D, NTOK], F32, kind="ExternalOutput").ap()
            self.cf_d = self.dram_in("cf", [128, CF_W + CB_W])
            self.cv_d = self.dram_in("cvec", [128, CV_W])
            self.rw_d = self.dram_in("rows", [128, RW_W])
            need_g = (not self.skip_mixer) and any(l % 2 == 0 for l in self.layers)
            need_h = (not self.skip_mixer) and any(l % 2 == 1 for l in self.layers)
            self.need_g, self.need_h = need_g, need_h
            if need_g:
                self.gdn_w_in = self.dram_in("gdn_w_in", [2, D, GDN_IN])
                self.gdn_w_out = self.dram_in("gdn_w_out", [2, D, D])
            if need_h:
                self.hgrn_w_in = self.dram_in("hgrn_w_in", [2, D, 4 * D])
                self.hgrn_w_out = self.dram_in("hgrn_w_out", [2, D, D])
                self.lb_d = self.dram_in("lbrows", [128, 4096])
            if self.layers:
                self.mlp_w_up = self.dram_in("mlp_w_up", [4, D, 4 * D])
                self.mlp_w_down = self.dram_in("mlp_w_down", [4, 4 * D, D])
            self.hT = self.sb("hT", [128, NCH, T], F32)
            self.yT = self.sb("yT", [128, NCH, TB], BF16)
            self.slabs = [self.sb("slab%d" % i, [128, 4096], BF16) for i in range(2)]
            self.cv = self.sb("cv", [128, CV_W], F32)
            self.rw = self.sb("rw", [128, RW_W], F32)
            self.cf = self.sb("cff", [128, CF_W], F32)
            self.cb = self.sb("cfb", [128, CB_W], BF16)
            self.rstd = self.sb("rstd", [128, 512], F32)
            self.lnt = self.sb("lnt", [128, 512], F32)
            self.sqs = self.sb("sqs", [128, 2, 512], BF16)
            self.big = self.sb("big", [128, 32 * TB], BF16)
            self.gate = self.sb("gate", [128, H, TB], BF16)
            self.OT = self.sb("OT", [128, H, TB], BF16)
            self.tmpf = self.sb("tmpf", [128, 1024], F32)
            self.tmpf2 = self.sb("tmpf2", [128, 1024], F32)
            self.Sst = self.sb("Sst", [128, H, 128], F32)
            self.Sbf = self.sb("Sbf", [128, H, 128], BF16)
            self.ps = [es.enter_context(nc.psum_tensor("ps%d" % i, [128, 512], F32)) for i in range(8)]
            if need_g or need_h:
                self.mxf = self.sb("mxf", [128, 2, 1024], F32)
                self.mxb = self.sb("mxb", [128, 2, 12, 128], BF16)
                self.lbt = self.sb("lbt", [128, 2, 1024], F32)
                self.kdt = self.sb("kdt", [128, 1024], BF16)
                self.smallf = self.sb("smallf", [128, 4, 64], F32)
            if need_g:
                self.pre = self.sb("pre", [128, 1, 4, 3 + TB], BF16)
                self.halo = self.sb("halo", [128, 24, 3], BF16)
                self.Gs = self.sb("Gs", [128, self.NCI, 16], F32)
                import os as _os
                if int(_os.environ.get("PADT", "0")):
                    self.padt = self.sb("padt", [128, int(_os.environ.get("PADT"))], BF16)
                self.nr = self.sb("nr", [128, 2, 2, 256], F32)
                self.ysb = self.sb("ysb", [128, 2, 2, 256], F32)
            self.emit_all()
            self.S.emit(nc, es)
        return nc

    def emit_all(self):
        T = self.T
        self.dma("sp", self.cf[:], self.cf_d[:, 0:CF_W], [], ["cf"], "c0")
        self.dma("pool", self.cb[:], self.cf_d[:, CF_W:CF_W + CB_W], [], ["cb"], "c3")
        self.dma("sp", self.cv[:], self.cv_d[:, :], [], ["cv"], "c1")
        self.dma("sp", self.rw[:], self.rw_d[:, :], [], ["rw"], "c2")
        for s in range(self.nseq):
            t0 = s * T
            for c in range(NCH):
                self.dma("sp", self.hT[:, c, :], self.hin[c * 128:(c + 1) * 128, t0:t0 + T], [], [("h", c)], "hin")
            for l in self.layers:
                if self.skip_mixer:
                    pass
                elif l % 2 == 0:
                    self.gdn_layer(l)
                else:
                    self.hgrn_layer(l)
                self.mlp_layer(l)
            if self.final_norm:
                for b in range(T // self.TB):
                    self.rmsnorm_block(b, CV_OFF["norm_final"], out_f32=True)
            for c in range(NCH):
                self.dma("sp", self.hout[c * 128:(c + 1) * 128, t0:t0 + T], self.hT[:, c, :], [("h", c)], [("hout", s, c)], "hout")
        self.S.add("sp", lambda e: None, r=[("hout", s, c) for s in range(self.nseq) for c in range(NCH)], w=["done"])

    def sumsq_rstd(self, srcs, keys, n, scale, bias, pk=0):
        P0 = self.ps[pk]
        pkey = self.psk(pk, 0, n)
        ns = len(srcs)
        for i, (sap, k) in enumerate(zip(srcs, keys)):
            sqv = self.sqs[:, i % 2, 0:n]
            self.actv(sqv, sap, AF.Square, [k], [("sqs", i % 2)])
            self.mm(P0[:, 0:n], self.cba("ONES"), sqv, i == 0, i == ns - 1, [("sqs", i % 2), "cb"], pkey)
        self.actv(self.lnt[:, 0:n], P0[:, 0:n], AF.Ln, pkey, ["lnt"], scale=scale, bias=bias)

    def rmsnorm_block(self, b, cvoff, out_f32=False):
        TB = self.TB
        tsl = slice(b * TB, (b + 1) * TB)
        self.sumsq_rstd([self.hT[:, c, tsl] for c in range(NCH)], [("h", c) for c in range(NCH)], TB, 1.0 / D, EPS)
        self.actv(self.rstd[:, 0:TB], self.lnt[:, 0:TB], AF.Exp, ["lnt"], ["rstd"], scale=-0.5)
        for c in range(NCH):
            if out_f32:
                self.stt(self.hT[:, c, tsl], self.hT[:, c, tsl], self.cv[:, cvoff + c:cvoff + c + 1], self.rstd[:, 0:TB],
                         ALU.mult, ALU.mult, [("h", c), "rstd", "cv"], [("h", c)])
            else:
                self.stt(self.yT[:, c, :], self.hT[:, c, tsl], self.cv[:, cvoff + c:cvoff + c + 1], self.rstd[:, 0:TB],
                         ALU.mult, ALU.mult, [("h", c), "rstd", "cv"], [("y", c)])

    def proj_fm(self, W, col0, ncols, evac):
        TB = self.TB
        yk = [("y", c) for c in range(NCH)]
        Wv = W.rearrange("(c p) n -> p c n", p=128)
        for s0 in range(0, ncols, 512):
            w_ = min(512, ncols - s0)
            sk, wv = self.load_slab(Wv[:, :, col0 + s0:col0 + s0 + w_], 8, w_)
            for m in range(w_ // 128):
                pi = self.pi
                self.pi += 1
                P = self.ps[1 + (pi % 2)]
                pk = self.psk(1 + (pi % 2))
                for c in range(NCH):
                    self.mm(P[:, 0:TB], wv[:, c, m * 128:(m + 1) * 128], self.yT[:, c, :], c == 0, c == NCH - 1, [sk] + yk, pk)
                evac((s0 // 128) + m, P, pk)

    def proj_tm(self, W, col0, ncols, evac):
        TB = self.TB
        yk = [("y", c) for c in range(NCH)]
        Wv = W.rearrange("(c p) n -> p c n", p=128)
        for s0 in range(0, ncols, 512):
            w_ = min(512, ncols - s0)
            sk, wv = self.load_slab(Wv[:, :, col0 + s0:col0 + s0 + w_], 8, w_)
            for tt_ in range(TB // 128):
                pi = self.pi
                self.pi += 1
                P = self.ps[1 + (pi % 2)]
                pk = self.psk(1 + (pi % 2))
                for c in range(NCH):
                    self.mm(P[:, 0:w_], self.yT[:, c, tt_ * 128:(tt_ + 1) * 128], wv[:, c, :], c == 0, c == NCH - 1, [sk] + yk, pk)
                evac(s0 // 512, tt_, P, pk, w_)

    def out_proj(self, W, src, srck, tsl):
        TB = self.TB
        Wv = W.rearrange("(c p) n -> p c n", p=128)
        for s0 in range(0, D, 512):
            sk, wv = self.load_slab(Wv[:, :, s0:s0 + 512], 8, 512)
            for m in range(4):
                pi = self.pi
                self.pi += 1
                P = self.ps[1 + (pi % 2)]
                pk = self.psk(1 + (pi % 2))
                for c in range(NCH):
                    self.mm(P[:, 0:TB], wv[:, c, m * 128:(m + 1) * 128], src[:, c, :], c == 0, c == NCH - 1, [sk, srck(c)], pk)
                mo = s0 // 128 + m
                self.tt("dve", self.hT[:, mo, tsl], P[:, 0:TB], self.hT[:, mo, tsl], ALU.add, pk + [("h", mo)], [("h", mo)])

    def mlp_layer(self, l):
        TB = self.TB
        hid = self.big[:, 0:32 * TB].rearrange("p (c t) -> p c t", c=32)
        hk = lambda c: self.bigk(c * TB * 2, (c + 1) * TB * 2)
        for b in range(self.T // TB):
            tsl = slice(b * TB, (b + 1) * TB)
            self.rmsnorm_block(b, CV_OFF["norm_mlp"] + 8 * l)

            def evac(j, P, pk):
                tf = self.tmpf if (j % 2 == 0) else self.tmpf2
                tk = "tmpf" if (j % 2 == 0) else "tmpf2"
                self.actv(tf[:, 0:TB], P[:, 0:TB], AF.Relu, pk, [tk])
                self.tt("pool", hid[:, j, :], tf[:, 0:TB], tf[:, 0:TB], ALU.mult, [tk], hk(j))
            self.proj_fm(self.mlp_w_up[l], 0, 4 * D, evac)
            Wd = self.mlp_w_down[l].rearrange("(c p) n -> p c n", p=128)
            for m in range(NCH):
                sk, wv = self.load_slab(Wd[:, :, m * 128:(m + 1) * 128], 32, 128)
                pi = self.pi
                self.pi += 1
                P = self.ps[1 + (pi % 2)]
                pk = self.psk(1 + (pi % 2))
                for kc in range(32):
                    self.mm(P[:, 0:TB], wv[:, kc, :], hid[:, kc, :], kc == 0, kc == 31, [sk] + hk(kc), pk)
                self.tt("dve", self.hT[:, m, tsl], P[:, 0:TB], self.hT[:, m, tsl], ALU.add, pk + [("h", m)], [("h", m)])

    def mxfk(self, hp, c0, c1):
        bounds = [0, 128, 256, 896, 1024]
        return [("mxf", hp, q) for q in range(4) if c0 < bounds[q + 1] and c1 > bounds[q]]

    def mbk(self, hp, s0, s1=None):
        s1 = s0 + 1 if s1 is None else s1
        return [("mxb", hp, s) for s in range(s0, s1)]

    def kdk(self, c0, c1):
        return [("kdt", q) for q in range(c0 // 128, (c1 + 127) // 128)]

    def hgrn_layer(self, l):
        TB, NCI = self.TB, self.NCI
        j = l // 2
        W = self.hgrn_w_in[j]
        mb = self.mxb
        lg = self.big[:, 0:8192].bitcast(F32)
        lgk = self.bigk(0, 16384)
        self.dma("sp", lg, self.lb_d[:, :], [], lgk, "lb")
        self.actv(lg, lg, AF.Exp, lgk, lgk)
        t1 = self.mxf[:, 0, :]
        t2 = self.mxf[:, 1, :]
        k1 = self.mxfk(0, 0, 1024)
        k2 = self.mxfk(1, 0, 1024)
        self.tt("dve", t1, lg[:, 0:1024], lg[:, 1024:2048], ALU.add, lgk, k1)
        self.tt("dve", t1, t1, lg[:, 2048:3072], ALU.add, lgk + k1, k1)
        self.tt("dve", t1, t1, lg[:, 3072:4096], ALU.add, lgk + k1, k1)
        self.op("dve", "reciprocal", k1, k1, out=t1, in_=t1)
        if l == 1:
            self.op("dve", "tensor_copy", lgk, k2, out=t2, in_=lg[:, 1024:2048])
        else:
            self.tt("dve", t2, lg[:, 1024:2048], lg[:, 2048:3072], ALU.add, lgk, k2)
            self.tt("dve", t2, t2, lg[:, 3072:4096], ALU.add, lgk + k2, k2)
        LB = self.lbt[:, 0, :]
        OML = self.lbt[:, 1, :]
        self.tt("dve", LB, t2, t1, ALU.mult, k1 + k2, ["lbt"])
        self.ts("dve", OML, LB, -1.0, 1.0, ALU.mult, ALU.add, ["lbt"], ["oml"])
        self.op("dve", "memset", [], [("Sst", h) for h in range(H)], self.Sst[:], 0.0)
        self.op("pool", "memset", [], [("Sbf", h) for h in range(H)], self.Sbf[:], 0.0)
        gbytes = NCI * 4096
        G = self.big[:, 0:gbytes // 2].bitcast(F32).rearrange("p (c n) -> p c n", c=NCI)
        V = self.big[:, gbytes // 2:gbytes // 2 + NCI * 1024].rearrange("p (c n) -> p c n", c=NCI)
        q0 = gbytes + NCI * 2048
        qT = self.big[:, q0 // 2:q0 // 2 + H * TB].rearrange("p (h t) -> p h t", h=H)
        Gk = lambda ci, half: self.bigk(ci * 4096 + half * 2048, ci * 4096 + (half + 1) * 2048)
        Vk = lambda ci, half: self.bigk(gbytes + ci * 2048 + half * 1024, gbytes + ci * 2048 + (half + 1) * 1024)
        qk = lambda h: self.bigk(q0 + h * TB * 2, q0 + (h + 1) * TB * 2)
        qsc = float(128 ** -0.5)
        for b in range(self.T // TB):
            tsl = slice(b * TB, (b + 1) * TB)
            self.rmsnorm_block(b, CV_OFF["norm_mix"] + 8 * l)

            def evac_q(jj, P, pk):
                self.actv(qT[:, jj, :], P[:, 0:TB], AF.Silu, pk, qk(jj))
                self.ts("pool", qT[:, jj, :], qT[:, jj, :], qsc, None, ALU.mult, ALU.bypass, qk(jj), qk(jj))
            self.proj_fm(W, 0, 1024, evac_q)

            def evac_f(si, tt_, P, pk, w_):
                cs = slice(si * 512, (si + 1) * 512)
                a = self.tmpf[:, 0:512]
                self.actv(a, P[:, 0:512], AF.Exp, pk, ["tmpf"], scale=-1.0)
                self.ts("dve", a, a, 1.0, None, ALU.add, ALU.bypass, ["tmpf"], ["tmpf"])
                self.op("dve", "reciprocal", ["tmpf"], ["tmpf"], out=a, in_=a)
                self.tt("dve", a, a, OML[:, cs], ALU.mult, ["tmpf", "oml"], ["tmpf"])
                self.tt("dve", a, a, LB[:, cs], ALU.add, ["tmpf", "lbt"], ["tmpf"])
                self.actv(G[:, tt_, cs], a, AF.Ln, ["tmpf"], Gk(tt_, si))
            self.proj_tm(W, 1024, 1024, evac_f)
            self.proj_tm(W, 2048, 1024, lambda si, tt_, P, pk, w_: self.actv(V[:, tt_, si * 512:(si + 1) * 512], P[:, 0:512], AF.Copy, pk, Vk(tt_, si)))
            self.proj_fm(W, 3072, 1024, lambda jj, P, pk: self.actv(self.gate[:, jj, :], P[:, 0:TB], AF.Silu, pk, [("gate", jj)]))
            for ci in range(NCI):
                csl = slice(ci * 128, (ci + 1) * 128)
                for half in range(2):
                    hs = slice(half * 512, (half + 1) * 512)
                    P = self.ps[3]
                    p3 = self.psk(3)
                    self.mm(P[:, :], self.cfa("SUP"), G[:, ci, hs], True, True, Gk(ci, half) + ["cf"], p3)
                    e1 = self.tmpf[:, 0:512]
                    e2 = self.tmpf2[:, 0:512]
                    self.actv(e1, G[:, ci, hs], AF.Exp, Gk(ci, half), ["tmpf"])
                    self.actv(e2, P[:, :], AF.Exp, p3, ["tmpf2"])
                    self.ts("dve", e1, e1, -1.0, 1.0, ALU.mult, ALU.add, ["tmpf"], ["tmpf"])
                    self.tt("dve", self.kdt[:, hs], e1, e2, ALU.mult, ["tmpf", "tmpf2"], self.kdk(half * 512, (half + 1) * 512))
                for h in range(H):
                    hp = h % 2
                    hsl = slice(h * 128, (h + 1) * 128)
                    PE0, PE1 = self.ps[4 + 2 * hp], self.ps[5 + 2 * hp]
                    pk0, pk1 = self.psk(4 + 2 * hp), self.psk(5 + 2 * hp, 0, 384)
                    RH = self.cfa("RH", 896)
                    self.mm(PE0[:, :], G[:, ci, hsl], RH[:, 0:512], True, True, Gk(ci, h // 4) + ["cf"], pk0)
                    self.mm(PE1[:, 0:384], G[:, ci, hsl], RH[:, 512:896], True, True, Gk(ci, h // 4) + ["cf"], pk1)
                    E = self.mxf[:, hp, 0:896]
                    ek = self.mxfk(hp, 0, 896)
                    self.actv(E[:, 0:512], PE0[:, :], AF.Exp, pk0, self.mxfk(hp, 0, 512))
                    self.actv(E[:, 512:896], PE1[:, 0:384], AF.Exp, pk1, self.mxfk(hp, 512, 896))
                    kTf = self.mxf[:, hp, 896:1024]
                    kfk = self.mxfk(hp, 896, 1024)
                    self.ts("dve", kTf, E[:, 768:896], -1.0, 1.0, ALU.mult, ALU.add, ek, kfk)
                    KT4 = mb[:, hp, 0:4, :]
                    self.tt("dve", KT4, E[:, 0:512].rearrange("p (r n) -> p r n", r=4),
                            kTf.unsqueeze(1).to_broadcast([128, 4, 128]), ALU.mult, ek + kfk, self.mbk(hp, 0, 4))
                    QQ = mb[:, hp, 4:6, :]
                    self.tt("dve", QQ, E[:, 512:768].rearrange("p (r n) -> p r n", r=2),
                            qT[:, h, csl].unsqueeze(1).to_broadcast([128, 2, 128]), ALU.mult, ek + qk(h), self.mbk(hp, 4, 6))
                    PA = self.ps[3]
                    pa = self.psk(3, 0, 128)
                    for r4 in range(4):
                        self.mm(PA[:, r4 * 32:(r4 + 1) * 32], KT4[:, r4, :], QQ[:, 0, r4 * 32:(r4 + 1) * 32], True, True,
                                self.mbk(hp, 0, 6), pa)
                    AT = mb[:, hp, 6, :]
                    self.tt("dve", AT, PA[:, 0:128], self.cba("MASKT"), ALU.mult, pa + ["cb"], self.mbk(hp, 6))
                    PO = self.ps[0]
                    po0, po1 = self.psk(0, 0, 128), self.psk(0, 128, 256)
                    self.mm(PO[:, 0:128], V[:, ci, hsl], AT, True, False, Vk(ci, h // 4) + self.mbk(hp, 6), po0)
                    self.mm(PO[:, 0:128], self.Sbf[:, h, :], QQ[:, 1, :], False, True, [("Sbf", h)] + self.mbk(hp, 5), po0)
                    self.actv(self.OT[:, h, csl], PO[:, 0:128], AF.Copy, po0, [("OT", h)])
                    self.mm(PO[:, 128:256], self.kdt[:, hsl], V[:, ci, hsl], True, True, self.kdk(h * 128, (h + 1) * 128) + Vk(ci, h // 4), po1)
                    self.stt(self.Sst[:, h, :], self.Sst[:, h, :], E[:, 767:768], PO[:, 128:256], ALU.mult, ALU.add,
                             [("Sst", h)] + ek + po1, [("Sst", h)])
                    self.actv(self.Sbf[:, h, :], self.Sst[:, h, :], AF.Copy, [("Sst", h)], [("Sbf", h)])
            self.sumsq_rstd([self.OT[:, h, :] for h in range(H)], [("OT", h) for h in range(H)], TB, 1.0 / D, EPS)
            self.actv(self.rstd[:, 0:TB], self.lnt[:, 0:TB], AF.Exp, ["lnt"], ["rstd"], scale=-0.5)
            go = CV_OFF["gnorm"] + 8 * j
            for h in range(H):
                self.stt(self.OT[:, h, :], self.OT[:, h, :], self.cv[:, go + h:go + h + 1], self.rstd[:, 0:TB], ALU.mult, ALU.mult,
                         [("OT", h), "cv", "rstd"], [("OT", h)])
                self.tt("pool", self.OT[:, h, :], self.OT[:, h, :], self.gate[:, h, :], ALU.mult, [("OT", h), ("gate", h)], [("OT", h)])
            self.out_proj(self.hgrn_w_out[j], self.OT, lambda c: ("OT", c), tsl)

    def gdn_layer(self, l):
        TB, NCI = self.TB, self.NCI
        j = l // 2
        W = self.gdn_w_in[j]
        QKV = self.big[:, 0:24 * TB].rearrange("p (c t) -> p c t", c=24)
        ck = lambda c: self.bigk(c * TB * 2, (c + 1) * TB * 2)
        nega = self.smallf[:, 0, 0:8]
        self.actv(nega, self.rw[:, 8 * j:8 * j + 8], AF.Exp, ["rw"], ["nega"])
        self.ts("dve", nega, nega, -1.0, None, ALU.mult, ALU.bypass, ["nega"], ["nega"])
        self.op("dve", "memset", [], [("Sst", h) for h in range(H)], self.Sst[:], 0.0)
        self.op("pool", "memset", [], [("Sbf", h) for h in range(H)], self.Sbf[:], 0.0)
        self.op("pool", "memset", [], [("halo", c) for c in range(24)], self.halo[:], 0.0)
        cvo = CV_OFF["conv"] + j * 96
        for b in range(self.T // TB):
            tsl = slice(b * TB, (b + 1) * TB)
            self.rmsnorm_block(b, CV_OFF["norm_mix"] + 8 * l)

            def evac_qkv(jj, P, pk):
                pp = 0
                slot = jj % 4
                pv = self.pre[:, pp, slot, :]
                pkk = ("pre", pp, slot)
                self.actv(pv[:, 3:3 + TB], P[:, 0:TB], AF.Copy, pk, [pkk])
                self.op("pool", "tensor_copy", [("halo", jj)], [pkk], out=pv[:, 0:3], in_=self.halo[:, jj, :])
                acc = self.tmpf[:, 0:TB] if jj % 2 == 0 else self.tmpf2[:, 0:TB]
                ak = "tmpf" if jj % 2 == 0 else "tmpf2"
                wc = cvo + jj * 4
                self.ts("dve", acc, pv[:, 0:TB], self.cv[:, wc:wc + 1], None, ALU.mult, ALU.bypass, [pkk, "cv"], [ak])
                for tap in range(1, 4):
                    self.stt(acc, pv[:, tap:tap + TB], self.cv[:, wc + tap:wc + tap + 1], acc, ALU.mult, ALU.add, [pkk, "cv", ak], [ak])
                self.op("pool", "tensor_copy", [pkk], [("halo", jj)], out=self.halo[:, jj, :], in_=pv[:, TB:TB + 3])
                self.actv(QKV[:, jj, :], acc, AF.Silu, [ak], ck(jj))
            self.proj_fm(W, 0, 3072, evac_qkv)
            self.proj_fm(W, 3072, 1024, lambda jj, P, pk: self.actv(self.gate[:, jj, :], P[:, 0:TB], AF.Silu, pk, [("gate", jj)]))
            if self.dbg_stop == 1:
                continue

            def evac_ab(si, tt_, P, pk, w_):
                x1 = self.smallf[:, 1, 0:16]
                self.tt("dve", x1[:, 0:8], P[:, 0:8], self.rw[:, 16 + 8 * j:24 + 8 * j], ALU.add, pk + ["rw"], ["x1"])
                self.ts("dve", x1[:, 8:16], P[:, 8:16], -1.0, None, ALU.mult, ALU.bypass, pk, ["x1b"])
                self.actv(x1, x1, AF.Exp, ["x1", "x1b"], ["x1", "x1b"])
                self.actv(x1, x1, AF.Ln, ["x1", "x1b"], ["x1", "x1b"], bias=1.0)
                self.tt("dve", self.Gs[:, tt_, 0:8], x1[:, 0:8], nega, ALU.mult, ["x1", "nega"], [("Gs", tt_)])
                self.ts("dve", self.Gs[:, tt_, 8:16], x1[:, 8:16], -1.0, None, ALU.mult, ALU.bypass, ["x1b"], [("Gsb", tt_)])
            self.proj_tm(W, 4096, 16, evac_ab)
            if self.dbg_stop == 2:
                continue
            for c in range(16):
                self.sumsq_rstd([QKV[:, c, :]], [ck(c)[0]], TB, 1.0, EPS, pk=0)
                self.actv(self.rstd[:, 0:TB], self.lnt[:, 0:TB], AF.Exp, ["lnt"], ["rstd"], scale=-0.5)
                if c < 8:
                    self.stt(QKV[:, c, :], QKV[:, c, :], float(128 ** -0.5), self.rstd[:, 0:TB], ALU.mult, ALU.mult, ck(c) + ["rstd"], ck(c))
                else:
                    self.tt("dve", QKV[:, c, :], QKV[:, c, :], self.rstd[:, 0:TB], ALU.mult, ck(c) + ["rstd"], ck(c))
            if self.dbg_stop == 3:
                continue
            for ci in range(NCI):
                self.gdn_chunk(ci, QKV, ck)
            if self.dbg_stop >= 4:
                continue
            oo = CV_OFF["onorm"] + j
            for h in range(H):
                self.sumsq_rstd([self.OT[:, h, :]], [("OT", h)], TB, 1.0 / 128, EPS, pk=0)
                self.actv(self.rstd[:, 0:TB], self.lnt[:, 0:TB], AF.Exp, ["lnt"], ["rstd"], scale=-0.5)
                self.stt(self.OT[:, h, :], self.OT[:, h, :], self.cv[:, oo:oo + 1], self.rstd[:, 0:TB], ALU.mult, ALU.mult,
                         [("OT", h), "cv", "rstd"], [("OT", h)])
                self.tt("pool", self.OT[:, h, :], self.OT[:, h, :], self.gate[:, h, :], ALU.mult, [("OT", h), ("gate", h)], [("OT", h)])
            self.out_proj(self.gdn_w_out[j], self.OT, lambda c: ("OT", c), tsl)

    def gdn_chunk(self, ci, QKV, ck):
        mb = self.mxb
        csl = slice(ci * 128, (ci + 1) * 128)
        g = self.Gs[:, ci, 0:8]
        lnb = self.Gs[:, ci, 8:16]
        gk, lk = ("Gs", ci), ("Gsb", ci)
        PSm = self.ps[3]
        p3a = self.psk(3, 0, 128)
        self.mm(PSm[:, 0:8], self.cfa("LT"), g, True, False, [gk, "cf"], p3a)
        self.mm(PSm[:, 0:8], self.cfa("ID"), lnb, False, True, [lk, "cf"], p3a)
        self.mm(PSm[:, 8:16], self.cfa("SUP"), g, True, True, [gk, "cf"], p3a)
        self.mm(PSm[:, 16:24], self.cfa("ONES"), g, True, True, [gk, "cf"], p3a)
        self.mm(PSm[:, 24:32], self.cfa("ID"), lnb, True, True, [lk, "cf"], p3a)
        EG = self.smallf[:, 2, 0:32]
        self.actv(EG, PSm[:, 0:32], AF.Exp, p3a, ["EG"])
        negb = self.smallf[:, 3, 0:8]
        self.ts("dve", negb, EG[:, 24:32], -1.0, None, ALU.mult, ALU.bypass, ["EG"], ["negb"])
        if self.dbg_stop == 4:
            return
        PT = self.ps[3][:, :].bitcast(BF16)
        p3b, p3c, p3d = self.psk(3, 128, 256), self.psk(3, 256, 384), self.psk(3, 384, 512)
        nr, ys = self.nr, self.ysb
        for h in range(H):
            hp = h % 2
            qTh, kTh, vTh = QKV[:, h, csl], QKV[:, 8 + h, csl], QKV[:, 16 + h, csl]
            qk_, kk_, vk_ = ck(h), ck(8 + h), ck(16 + h)
            Gb = self.mxf[:, hp, 0:128]
            Gb2 = self.mxf[:, hp, 128:256]
            gbk, gb2k = self.mxfk(hp, 0, 128), self.mxfk(hp, 128, 256)
            self.ts("dve", Gb, self.cfa("SUP"), g[:, h:h + 1], None, ALU.mult, ALU.bypass, ["cf", gk], gbk)
            self.ts("dve", Gb2, self.cfa("LT"), g[:, h:h + 1], None, ALU.mult, ALU.bypass, ["cf", gk], gb2k)
            PD = self.ps[4 + hp]
            pd0, pd1, pd2 = self.psk(4 + hp, 0, 128), self.psk(4 + hp, 128, 256), self.psk(4 + hp, 256, 384)
            self.mm(PD[:, 0:128], self.cfa("LT"), Gb, True, False, ["cf"] + gbk, pd0)
            self.mm(PD[:, 0:128], self.cba("ID"), self.cba("NEGS"), False, True, ["cb"], pd0)
            self.mm(PD[:, 128:256], self.cfa("SUP"), Gb2, True, False, ["cf"] + gb2k, pd1)
            self.mm(PD[:, 128:256], self.cba("ID"), self.cba("NEGCT"), False, True, ["cb"], pd1)
            self.mm(PD[:, 256:384], self.cfa("ONES"), Gb2, True, True, ["cf"] + gb2k, pd2)
            DEC = mb[:, hp, 0:3, :]
            self.actv(DEC[:, 1:3, :], PD[:, 128:384].rearrange("p (a n) -> p a n", a=2), AF.Exp, pd1 + pd2, self.mbk(hp, 1, 3))
            fz = self.mxf[:, hp, :]
            zk = self.mxfk(hp, 256, 896)
            decS, N, No, U64, T64 = fz[:, 256:384], fz[:, 384:512], fz[:, 512:640], fz[:, 640:768], fz[:, 768:896]
            Pm = fz[:, 896:1024]
            pmk = self.mxfk(hp, 896, 1024)
            self.actv(decS, PD[:, 0:128], AF.Exp, pd0, zk)
            PK = self.ps[6 + hp]
            pq0, pq1, pq2, pq3 = (self.psk(6 + hp, 0, 128), self.psk(6 + hp, 128, 256), self.psk(6 + hp, 256, 384),
                                  self.psk(6 + hp, 384, 512))
            self.mm(PK[:, 0:128], kTh, kTh, True, True, kk_, pq0)
            self.mm(PK[:, 128:256], kTh, qTh, True, True, kk_ + qk_, pq1)
            self.stt(N, PK[:, 0:128], negb[:, h:h + 1], decS, ALU.mult, ALU.mult, pq0 + ["negb"] + zk, zk)
            AT = mb[:, hp, 4, :]
            self.tt("dve", AT, PK[:, 128:256], DEC[:, 1, :], ALU.mult, pq1 + self.mbk(hp, 1), self.mbk(hp, 4))
            Nd = nr[:, hp, 0, 0:128]
            self.tt("pool", Nd, N, self.cba("BD"), ALU.mult, zk + ["cb"], [("nrN", hp, 0)])
            self.tt("pool", No, N, self.cba("OD"), ALU.mult, zk + ["cb"], zk)
            P3 = self.ps[3]
            self.op("pe", "transpose", [("nrN", hp, 0), "cf"], p3b, P3[:, 128:256], Nd, self.cfa("ID"))
            Yd = ys[:, hp, 0, 0:128]
            self.cp(Yd, P3[:, 128:256], p3b, [("ysY", hp, 0)])
            self.mm(PK[:, 256:384], Nd, Yd, True, True, [("nrN", hp, 0), ("ysY", hp, 0)], pq2)
            self.mm(PK[:, 384:512], Yd, Nd, True, True, [("nrN", hp, 0), ("ysY", hp, 0)], pq3)
            self.cp(ys[:, hp, 1, 0:128], PK[:, 256:384], pq2, [("ysY", hp, 1)])
            self.tt("pool", ys[:, hp, 1, 128:256], Yd, self.cfa("ID"), ALU.add, [("ysY", hp, 0), "cf"], [("ysS", hp, 1)])
            self.cp(nr[:, hp, 1, 0:128], PK[:, 384:512], pq3, [("nrN", hp, 1)])
            self.tt("pool", nr[:, hp, 1, 128:256], Nd, self.cfa("ID"), ALU.add, [("nrN", hp, 0), "cf"], [("nrR", hp, 1)])
            cur = 1
            PA_, PB_ = self.ps[4 + hp], self.ps[6 + hp]
            pa01, pb01 = pd0 + pd1, pq0 + pq1
            for lev in range(1, 5):
                nx = 1 - cur
                rk = [("nrN", hp, cur), ("nrR", hp, cur), ("ysY", hp, cur), ("ysS", hp, cur)]
                self.mm(PA_[:, 0:256], nr[:, hp, cur, 0:128], ys[:, hp, cur, :], True, True, rk, pa01)
                self.mm(PB_[:, 0:256], ys[:, hp, cur, 0:128], nr[:, hp, cur, :], True, True, rk, pb01)
                self.cp(ys[:, hp, nx, 0:128], PA_[:, 0:128], pd0, [("ysY", hp, nx)])
                self.tt("dve", ys[:, hp, nx, 128:256], PA_[:, 128:256], ys[:, hp, cur, 128:256], ALU.add, pd1 + [("ysS", hp, cur)], [("ysS", hp, nx)])
                self.cp(nr[:, hp, nx, 0:128], PB_[:, 0:128], pq0, [("nrN", hp, nx)])
                self.tt("dve", nr[:, hp, nx, 128:256], PB_[:, 128:256], nr[:, hp, cur, 128:256], ALU.add, pq1 + [("nrR", hp, cur)], [("nrR", hp, nx)])
                cur = nx
            rk = [("nrN", hp, cur), ("nrR", hp, cur), ("ysY", hp, cur), ("ysS", hp, cur)]
            self.mm(PA_[:, 0:128], nr[:, hp, cur, 0:128], ys[:, hp, cur, 128:256], True, True, rk, pd0)
            self.mm(PB_[:, 0:128], ys[:, hp, cur, 0:128], nr[:, hp, cur, 128:256], True, True, rk, pq0)
            self.tt("dve", U64, PA_[:, 0:128], ys[:, hp, cur, 128:256], ALU.add, pd0 + [("ysS", hp, cur)], zk)
            self.tt("dve", T64, PB_[:, 0:128], nr[:, hp, cur, 128:256], ALU.add, pq0 + [("nrR", hp, cur)], zk)
            self.mm(PA_[:, 0:128], No, U64, True, True, zk, pd0)
            self.cp(Pm, PA_[:, 0:128], pd0, pmk)
            self.mm(PB_[:, 0:128], T64, Pm, True, True, zk + pmk, pq0)
            U = mb[:, hp, 9, :]
            self.tt("dve", U, PB_[:, 0:128], U64, ALU.add, pq0 + zk, self.mbk(hp, 9))
            if self.dbg_stop == 8:
                continue
            self.tr(PT[:, 512:640], kTh, kk_, p3c)
            kbgn = mb[:, hp, 10, :]
            kdec = mb[:, hp, 11, :]
            self.ts("dve", kbgn, PT[:, 512:640], EG[:, h:h + 1], -1.0, ALU.mult, ALU.mult, p3c + ["EG"], self.mbk(hp, 10))
            self.ts("dve", kdec, PT[:, 512:640], EG[:, 8 + h:9 + h], None, ALU.mult, ALU.bypass, p3c + ["EG"], self.mbk(hp, 11))
            self.tr(PT[:, 768:896], vTh, vk_, p3d)
            c_vb, c_w, c_vn, c_qd = hp * 128, 256 + hp * 128, 512 + hp * 128, 768 + hp * 128
            vb = self.kdt[:, c_vb:c_vb + 128]
            self.ts("dve", vb, PT[:, 768:896], EG[:, 24 + h:25 + h], None, ALU.mult, ALU.bypass, p3d + ["EG"], self.kdk(c_vb, c_vb + 128))
            if self.dbg_stop == 9:
                continue
            PO = self.ps[0]
            po = [self.psk(0, q * 128, (q + 1) * 128) for q in range(4)]
            self.mm(PO[:, 0:128], kbgn, U, True, True, self.mbk(hp, 10) + self.mbk(hp, 9), po[0])
            wTn = self.kdt[:, c_w:c_w + 128]
            self.cp(wTn, PO[:, 0:128], po[0], self.kdk(c_w, c_w + 128))
            self.mm(PO[:, 128:256], U, vb, True, False, self.mbk(hp, 9) + self.kdk(c_vb, c_vb + 128), po[1])
            self.mm(PO[:, 128:256], wTn, self.Sbf[:, h, :], False, True, self.kdk(c_w, c_w + 128) + [("Sbf", h)], po[1])
            vnew = self.kdt[:, c_vn:c_vn + 128]
            vnk = self.kdk(c_vn, c_vn + 128)
            self.cp(vnew, PO[:, 128:256], po[1], vnk)
            qd = self.kdt[:, c_qd:c_qd + 128]
            qdk = self.kdk(c_qd, c_qd + 128)
            self.tt("pool", qd, qTh, DEC[:, 2, :], ALU.mult, qk_ + self.mbk(hp, 2), qdk)
            self.mm(PO[:, 256:384], self.Sbf[:, h, :], qd, True, False, [("Sbf", h)] + qdk, po[2])
            self.mm(PO[:, 256:384], vnew, AT, False, True, vnk + self.mbk(hp, 4), po[2])
            self.cp(self.OT[:, h, csl], PO[:, 256:384], po[2], [("OT", h)])
            self.mm(PO[:, 384:512], kdec, vnew, True, True, self.mbk(hp, 11) + vnk, po[3])
            self.stt(self.Sst[:, h, :], self.Sst[:, h, :], EG[:, 16 + h:17 + h], PO[:, 384:512], ALU.mult, ALU.add,
                     [("Sst", h), "EG"] + po[3], [("Sst", h)])
            self.cp(self.Sbf[:, h, :], self.Sst[:, h, :], [("Sst", h)], [("Sbf", h)])


def make_in_map(inp, hT, prog):
    m = {"hT_in": np.ascontiguousarray(hT, dtype=np.float32), "cf": _consts(), "cvec": _make_cvec(inp), "rows": _make_rows(inp)}
    need_g, need_h = prog.need_g, prog.need_h
    if need_g:
        m["gdn_w_in"] = inp["gdn_w_in"]
        m["gdn_w_out"] = inp["gdn_w_out"]
    if need_h:
        m["hgrn_w_in"] = inp["hgrn_w_in"]
        m["hgrn_w_out"] = inp["hgrn_w_out"]
        m["lbrows"] = _make_lbrows(inp)
    if prog.layers:
        m["mlp_w_up"] = inp["mlp_w_up"]
        m["mlp_w_down"] = inp["mlp_w_down"]
    return m


N_CORES = 8
_CFG = {"T": 2048, "TB": 512, "nseq": 2, "groups": [[0, 1, 2, 3]], "skip_mixer": False}


def kernel(**inp):
    inp = {k: np.asarray(v) for k, v in inp.items()}
    x = inp["x"]
    B, T, Dm = x.shape
    nseq = B // N_CORES
    hT = [np.ascontiguousarray(x[c * nseq:(c + 1) * nseq].reshape(nseq * T, Dm).T) for c in range(N_CORES)]
    groups = _CFG["groups"]
    for gi, layers in enumerate(groups):
        p = Prog(nseq, T, _CFG["TB"], layers, final_norm=(gi == len(groups) - 1))
        p.skip_mixer = _CFG["skip_mixer"]
        nc = p.build()
        base = make_in_map(inp, hT[0], p)
        in_maps = []
        for c in range(N_CORES):
            m = dict(base)
            m["hT_in"] = hT[c]
            in_maps.append(m)
        res = run_bass_kernel_spmd(nc, in_maps, core_ids=list(range(N_CORES)))
        hT = [np.ascontiguousarray(r["hT_out"]) for r in res.results]
    out = np.stack([h.T.reshape(nseq, T, Dm) for h in hT], axis=0).reshape(B, T, Dm)
    return out.astype(np.float32)
```

```python
import numpy as np
from contextlib import ExitStack
import concourse.bass as bass
import concourse.mybir as mybir
from concourse.bass_utils import run_bass_kernel_spmd

F32 = mybir.dt.float32
BF16 = mybir.dt.bfloat16
AF = mybir.ActivationFunctionType
ALU = mybir.AluOpType

D = 1024
NCH = 8
H = 8
C = 128
EPS = 1e-6
GDN_IN = 4112
MAXV = 30000


class _Op:
    __slots__ = ("eng", "fn", "deps", "kind", "stream", "sig", "upg")

    def __init__(self, eng, fn, deps, kind, stream):
        self.eng, self.fn, self.deps, self.kind, self.stream, self.sig = eng, fn, deps, kind, stream, None


class Sched:
    ENGS = ("pe", "act", "dve", "pool", "sp")

    def __init__(self):
        self.ops = []
        self.lastw = {}
        self.readers = {}
        self.stream_last = {}
        self.recording = None

    def interleave(self, bodies):
        lists = []
        for b in bodies:
            self.recording = []
            b()
            lists.append(self.recording)
            self.recording = None
        n = max(len(l) for l in lists)
        for i in range(n):
            for l in lists:
                if i < len(l):
                    self.add(*l[i])

    def add(self, eng, fn, r=(), w=(), kind="c", stream=None):
        if self.recording is not None:
            self.recording.append((eng, fn, tuple(r), tuple(w), kind, stream))
            return -1
        i = len(self.ops)
        deps = {}

        def upd(d, t):
            o = deps.get(d)
            if o is None or (t == "raw") or (t == "waw" and o == "war"):
                deps[d] = t
        for k in r:
            if k in self.lastw:
                upd(self.lastw[k], "raw")
        for k in w:
            if k in self.lastw:
                upd(self.lastw[k], "waw")
            for d in self.readers.get(k, ()):
                if d != i:
                    upd(d, "war")
        op = _Op(eng, fn, deps, kind, stream)
        op.upg = dict(self.stream_last)
        self.ops.append(op)
        if kind == "d":
            self.stream_last[stream] = i
        for k in r:
            self.readers.setdefault(k, []).append(i)
        for k in w:
            self.lastw[k] = i
            self.readers[k] = []
        return i

    def dma(self, eng, fn, r, w, stream):
        return self.add(eng, fn, r, w, kind="d", stream=stream)

    def _needs_sync(self, op, D, t):
        if D.kind == "d":
            return True
        if D.eng != op.eng:
            return True
        if op.kind == "d":
            return True
        if op.eng == "pe":
            return False
        return True

    def emit(self, nc, es):
        ops = self.ops
        need = [False] * len(ops)
        for op in ops:
            for d, t in op.deps.items():
                if self._needs_sync(op, ops[d], t):
                    need[d] = True
        cnt = {}
        semkeys = []

        def nxt(key, inc):
            st = cnt.get(key)
            if st is None:
                st = cnt[key] = [0, 0]
                semkeys.append((key, 0))
            if st[1] + inc > MAXV:
                st[0] += 1
                st[1] = 0
                semkeys.append((key, st[0]))
            st[1] += inc
            return (key, st[0]), st[1]
        for i, op in enumerate(ops):
            if op.kind == "d":
                op.sig = nxt(("dma", op.stream), 16)
            elif need[i]:
                op.sig = nxt(("eng", op.eng), 1)
        sems = {}
        for sk in semkeys:
            nm = "s_%s_%s_%d" % (sk[0][0], str(sk[0][1]), sk[1])
            sems[sk] = es.enter_context(nc.semaphore(nm))
        self.nsems = len(sems)
        block = es.enter_context(nc.Block())
        per = {e: [] for e in self.ENGS}
        for i, op in enumerate(ops):
            per[op.eng].append(i)

        def run(eng_name, engine):
            waited = {}
            for i in per[eng_name]:
                op = ops[i]
                for d, t in sorted(op.deps.items()):
                    Dop = ops[d]
                    if not self._needs_sync(op, Dop, t):
                        continue
                    if Dop.kind == "d":
                        Dop = ops[op.upg[Dop.stream]]
                    sk, val = Dop.sig
                    if waited.get(sk, 0) >= val:
                        continue
                    engine.wait_ge(sems[sk], val)
                    waited[sk] = val
                ins = op.fn(engine)
                if op.sig is not None:
                    assert ins is not None
                    ins.then_inc(sems[op.sig[0]], 16 if op.kind == "d" else 1)

        @block.tensor
        def _(e):
            run("pe", e)

        @block.scalar
        def _(e):
            run("act", e)

        @block.vector
        def _(e):
            run("dve", e)

        @block.gpsimd
        def _(e):
            run("pool", e)

        @block.sync
        def _(e):
            run("sp", e)


def _consts():
    k = np.arange(128)[:, None]
    j = np.arange(128)[None, :]
    f = np.float32
    LT = (k <= j).astype(f)
    SUP = (k > j).astype(f)
    ID = (k == j).astype(f)
    ONES = np.ones((128, 128), f)
    NEG = -30000.0
    NEGS = np.where(k > j, 0.0, NEG).astype(f)
    NEGCT = np.where(j >= k, 0.0, NEG).astype(f)
    BD = ((k // 64) == (j // 64)).astype(f)
    OD = 1.0 - BD
    MASKT = (k <= j).astype(f)
    Ms = []
    for r in range(4):
        lo, hi = 32 * r, 32 * (r + 1)
        M = np.zeros((128, 128), f)
        M += ((j < lo) & (k > j) & (k < lo)) * 1.0
        M -= ((j >= lo) & (j < hi) & (k >= lo) & (k <= j)) * 1.0
        Ms.append(M.astype(f))
    MQ = ((k >= (j // 32) * 32) & (k <= j)).astype(f)
    RH = np.concatenate(Ms + [MQ, LT, ID], axis=1)
    cf = np.concatenate([LT, SUP, ID, ONES, RH, ID, ONES, NEGS, NEGCT, BD, OD, MASKT], axis=1)
    return np.ascontiguousarray(cf.astype(f))


CF_OFF = {"LT": 0, "SUP": 128, "ID": 256, "ONES": 384, "RH": 512}
CF_W = 512 + 896
CB_OFF = {"ID": 0, "ONES": 128, "NEGS": 256, "NEGCT": 384, "BD": 512, "OD": 640, "MASKT": 768}
CB_W = 896


def _cvec_layout():
    off = {}
    n = 0
    for nm, w in (("norm_mix", 32), ("norm_mlp", 32), ("norm_final", 8), ("conv", 2 * 24 * 4),
                  ("onorm", 2), ("gnorm", 16)):
        off[nm] = n
        n += w
    return off, n


CV_OFF, CV_W = _cvec_layout()


def _make_cvec(inp):
    cv = np.zeros((128, CV_W), np.float32)
    cv[:, CV_OFF["norm_mix"]:CV_OFF["norm_mix"] + 32] = inp["norm_mix"].reshape(4, 8, 128).transpose(2, 0, 1).reshape(128, 32)
    cv[:, CV_OFF["norm_mlp"]:CV_OFF["norm_mlp"] + 32] = inp["norm_mlp"].reshape(4, 8, 128).transpose(2, 0, 1).reshape(128, 32)
    cv[:, CV_OFF["norm_final"]:CV_OFF["norm_final"] + 8] = inp["norm_final"].reshape(8, 128).T
    cw = inp["gdn_conv"].reshape(2, 4, 24, 128).transpose(3, 0, 2, 1).reshape(128, 2 * 24 * 4)
    cv[:, CV_OFF["conv"]:CV_OFF["conv"] + 192] = cw
    cv[:, CV_OFF["onorm"]:CV_OFF["onorm"] + 2] = inp["gdn_onorm"].T
    cv[:, CV_OFF["gnorm"]:CV_OFF["gnorm"] + 16] = inp["hgrn_gnorm"].reshape(2, 8, 128).transpose(2, 0, 1).reshape(128, 16)
    return cv


RW_W = 32


def _make_rows(inp):
    r = np.zeros((RW_W,), np.float32)
    r[0:16] = inp["gdn_a_log"].reshape(-1)
    r[16:32] = inp["gdn_dt_bias"].reshape(-1)
    return np.ascontiguousarray(np.broadcast_to(r[None, :], (128, RW_W)))


def _make_lbrows(inp):
    r = inp["hgrn_lb_logits"].reshape(-1).astype(np.float32)
    return np.ascontiguousarray(np.broadcast_to(r[None, :], (128, 4096)))


class Prog:
    def __init__(self, nseq, T, TB, layers, final_norm):
        self.nseq, self.T, self.TB, self.layers, self.final_norm = nseq, T, TB, list(layers), final_norm
        self.NTOK = nseq * T
        self.NCI = TB // 128
        self.S = Sched()
        self.nc = bass.Bass("TRN2", target_bir_lowering=False)
        self.slab_i = 0
        self.pi = 0
        self.skip_mixer = False
        self.dbg_stop = 0
        self.dbg2 = 9
        self.dbg3 = 0

    def dram_in(self, name, shape, dt=F32):
        return self.nc.dram_tensor(name, list(shape), dt, kind="ExternalInput").ap()

    def sb(self, name, shape, dt):
        return self.es.enter_context(self.nc.sbuf_tensor(name, list(shape), dt))

    def op(self, eng, meth, r, w, *a, **kw):
        return self.S.add(eng, lambda e: getattr(e, meth)(*a, **kw), r, w)

    def mm(self, out, lhsT, rhs, start, stop, r, w):
        return self.op("pe", "matmul", r, w, out, lhsT=lhsT, rhs=rhs, start=start, stop=stop)

    def tr(self, out, in_, r, w):
        return self.op("pe", "transpose", r + ["cb"], w, out, in_, self.cba("ID"))

    def actv(self, out, in_, func, r, w, **kw):
        return self.op("act", "activation", r, w, out=out, in_=in_, func=func, **kw)

    def cp(self, out, in_, r, w):
        return self.op("dve", "tensor_copy", r, w, out=out, in_=in_)

    def tt(self, eng, out, in0, in1, op, r, w):
        return self.op(eng, "tensor_tensor", r, w, out=out, in0=in0, in1=in1, op=op)

    def ts(self, eng, out, in0, s1, s2, op0, op1, r, w):
        return self.op(eng, "tensor_scalar", r, w, out=out, in0=in0, scalar1=s1, scalar2=s2, op0=op0, op1=op1)

    def stt(self, out, in0, scalar, in1, op0, op1, r, w):
        return self.op("dve", "scalar_tensor_tensor", r, w, out=out, in0=in0, scalar=scalar, in1=in1, op0=op0, op1=op1)

    def dma(self, eng, out, in_, r, w, stream):
        return self.S.dma(eng, lambda e: e.dma_start(out=out, in_=in_), r, w, stream)

    def cfa(self, name, w=128):
        o = CF_OFF[name]
        return self.cf[:, o:o + w]

    def cba(self, name):
        o = CB_OFF[name]
        return self.cb[:, o:o + 128]

    def psk(self, bank, c0=0, c1=512):
        return [("ps", bank)]

    def bigk(self, lo, hi):
        return [("big", i) for i in range(lo // 512, (hi + 511) // 512)]

    def load_slab(self, src_ap, nc_, ncols):
        i = self.slab_i % 2
        self.slab_i += 1
        t = self.slabs[i]
        view = t[:, 0:nc_ * ncols].rearrange("p (c n) -> p c n", c=nc_)
        self.dma("pool", view, src_ap, [], [("slab", i)], "slab%d" % i)
        return ("slab", i), view

    def build(self):
        nc = self.nc
        NTOK, T, TB = self.NTOK, self.T, self.TB
        with ExitStack() as es:
            self.es = es
            self.hin = self.dram_in("hT_in", [D, NTOK])
            self.hout = nc.dram_tensor("hT_out", [D, NTOK], F32, kind="ExternalOutput").ap()
            self.cf_d = self.dram_in("cf", [128, CF_W + CB_W])
            self.cv_d = self.dram_in("cvec", [128, CV_W])
            self.rw_d = self.dram_in("rows", [128, RW_W])
            need_g = (not self.skip_mixer) and any(l % 2 == 0 for l in self.layers)
            need_h = (not self.skip_mixer) and any(l % 2 == 1 for l in self.layers)
            self.need_g, self.need_h = need_g, need_h
            if need_g:
                self.gdn_w_in = self.dram_in("gdn_w_in", [2, D, GDN_IN])
                self.gdn_w_out = self.dram_in("gdn_w_out", [2, D, D])
            if need_h:
                self.hgrn_w_in = self.dram_in("hgrn_w_in", [2, D, 4 * D])
                self.hgrn_w_out = self.dram_in("hgrn_w_out", [2, D, D])
                self.lb_d = self.dram_in("lbrows", [128, 4096])
            if self.layers:
                self.mlp_w_up = self.dram_in("mlp_w_up", [4, D, 4 * D])
                self.mlp_w_down = self.dram_in("mlp_w_down", [4, 4 * D, D])
            self.hT = self.sb("hT", [128, NCH, T], F32)
            self.yT = self.sb("yT", [128, NCH, TB], BF16)
            self.slabs = [self.sb("slab%d" % i, [128, 4096], BF16) for i in range(2)]
            self.cv = self.sb("cv", [128, CV_W], F32)
            self.rw = self.sb("rw", [128, RW_W], F32)
            self.cf = self.sb("cff", [128, CF_W], F32)
            self.cb = self.sb("cfb", [128, CB_W], BF16)
            self.rstd = self.sb("rstd", [128, 512], F32)
            self.lnt = self.sb("lnt", [128, 512], F32)
            self.sqs = self.sb("sqs", [128, 2, 512], BF16)
            self.big = self.sb("big", [128, 32 * TB], BF16)
            self.gate = self.sb("gate", [128, H, TB], BF16)
            self.OT = self.sb("OT", [128, H, TB], BF16)
            self.tmpf = self.sb("tmpf", [128, 1024], F32)
            self.tmpf2 = self.sb("tmpf2", [128, 1024], F32)
            self.Sst = self.sb("Sst", [128, H, 128], F32)
            self.Sbf = self.sb("Sbf", [128, H, 128], BF16)
            self.ps = [es.enter_context(nc.psum_tensor("ps%d" % i, [128, 512], F32)) for i in range(8)]
            if need_g or need_h:
                self.mxf = self.sb("mxf", [128, 2, 1024], F32)
                self.mxb = self.sb("mxb", [128, 2, 12, 128], BF16)
                self.lbt = self.sb("lbt", [128, 2, 1024], F32)
                self.kdt = self.sb("kdt", [128, 1024], BF16)
                self.smallf = self.sb("smallf", [128, 4, 64], F32)
            if need_g:
                self.pre = self.sb("pre", [128, 1, 4, 3 + TB], BF16)
                self.halo = self.sb("halo", [128, 24, 3], BF16)
                self.Gs = self.sb("Gs", [128, self.NCI, 16], F32)
                import os as _os
                if int(_os.environ.get("PADT", "0")):
                    self.padt = self.sb("padt", [128, int(_os.environ.get("PADT"))], BF16)
                self.nr = self.sb("nr", [128, 2, 2, 256], F32)
                self.ysb = self.sb("ysb", [128, 2, 2, 256], F32)
            self.emit_all()
            self.S.emit(nc, es)
        return nc

    def emit_all(self):
        T = self.T
        self.dma("sp", self.cf[:], self.cf_d[:, 0:CF_W], [], ["cf"], "c0")
        self.dma("pool", self.cb[:], self.cf_d[:, CF_W:CF_W + CB_W], [], ["cb"], "c3")
        self.dma("sp", self.cv[:], self.cv_d[:, :], [], ["cv"], "c1")
        self.dma("sp", self.rw[:], self.rw_d[:, :], [], ["rw"], "c2")
        for s in range(self.nseq):
            t0 = s * T
            for c in range(NCH):
                self.dma("sp", self.hT[:, c, :], self.hin[c * 128:(c + 1) * 128, t0:t0 + T], [], [("h", c)], "hin")
            for l in self.layers:
                if self.skip_mixer:
                    pass
                elif l % 2 == 0:
                    self.gdn_layer(l)
                else:
                    self.hgrn_layer(l)
                self.mlp_layer(l)
            if self.final_norm:
                for b in range(T // self.TB):
                    self.rmsnorm_block(b, CV_OFF["norm_final"], out_f32=True)
            for c in range(NCH):
                self.dma("sp", self.hout[c * 128:(c + 1) * 128, t0:t0 + T], self.hT[:, c, :], [("h", c)], [("hout", s, c)], "hout")
        self.S.add("sp", lambda e: None, r=[("hout", s, c) for s in range(self.nseq) for c in range(NCH)], w=["done"])

    def sumsq_rstd(self, srcs, keys, n, scale, bias, pk=0):
        P0 = self.ps[pk]
        pkey = self.psk(pk, 0, n)
        ns = len(srcs)
        for i, (sap, k) in enumerate(zip(srcs, keys)):
            sqv = self.sqs[:, i % 2, 0:n]
            self.actv(sqv, sap, AF.Square, [k], [("sqs", i % 2)])
            self.mm(P0[:, 0:n], self.cba("ONES"), sqv, i == 0, i == ns - 1, [("sqs", i % 2), "cb"], pkey)
        self.actv(self.lnt[:, 0:n], P0[:, 0:n], AF.Ln, pkey, ["lnt"], scale=scale, bias=bias)

    def rmsnorm_block(self, b, cvoff, out_f32=False):
        TB = self.TB
        tsl = slice(b * TB, (b + 1) * TB)
        self.sumsq_rstd([self.hT[:, c, tsl] for c in range(NCH)], [("h", c) for c in range(NCH)], TB, 1.0 / D, EPS)
        self.actv(self.rstd[:, 0:TB], self.lnt[:, 0:TB], AF.Exp, ["lnt"], ["rstd"], scale=-0.5)
        for c in range(NCH):
            if out_f32:
                self.stt(self.hT[:, c, tsl], self.hT[:, c, tsl], self.cv[:, cvoff + c:cvoff + c + 1], self.rstd[:, 0:TB],
                         ALU.mult, ALU.mult, [("h", c), "rstd", "cv"], [("h", c)])
            else:
                self.stt(self.yT[:, c, :], self.hT[:, c, tsl], self.cv[:, cvoff + c:cvoff + c + 1], self.rstd[:, 0:TB],
                         ALU.mult, ALU.mult, [("h", c), "rstd", "cv"], [("y", c)])

    def proj_fm(self, W, col0, ncols, evac):
        TB = self.TB
        yk = [("y", c) for c in range(NCH)]
        Wv = W.rearrange("(c p) n -> p c n", p=128)
        for s0 in range(0, ncols, 512):
            w_ = min(512, ncols - s0)
            sk, wv = self.load_slab(Wv[:, :, col0 + s0:col0 + s0 + w_], 8, w_)
            for m in range(w_ // 128):
                pi = self.pi
                self.pi += 1
                P = self.ps[1 + (pi % 2)]
                pk = self.psk(1 + (pi % 2))
                for c in range(NCH):
                    self.mm(P[:, 0:TB], wv[:, c, m * 128:(m + 1) * 128], self.yT[:, c, :], c == 0, c == NCH - 1, [sk] + yk, pk)
                evac((s0 // 128) + m, P, pk)

    def proj_tm(self, W, col0, ncols, evac):
        TB = self.TB
        yk = [("y", c) for c in range(NCH)]
        Wv = W.rearrange("(c p) n -> p c n", p=128)
        for s0 in range(0, ncols, 512):
            w_ = min(512, ncols - s0)
            sk, wv = self.load_slab(Wv[:, :, col0 + s0:col0 + s0 + w_], 8, w_)
            for tt_ in range(TB // 128):
                pi = self.pi
                self.pi += 1
                P = self.ps[1 + (pi % 2)]
                pk = self.psk(1 + (pi % 2))
                for c in range(NCH):
                    self.mm(P[:, 0:w_], self.yT[:, c, tt_ * 128:(tt_ + 1) * 128], wv[:, c, :], c == 0, c == NCH - 1, [sk] + yk, pk)
                evac(s0 // 512, tt_, P, pk, w_)

    def out_proj(self, W, src, srck, tsl):
        TB = self.TB
        Wv = W.rearrange("(c p) n -> p c n", p=128)
        for s0 in range(0, D, 512):
            sk, wv = self.load_slab(Wv[:, :, s0:s0 + 512], 8, 512)
            for m in range(4):
                pi = self.pi
                self.pi += 1
                P = self.ps[1 + (pi % 2)]
                pk = self.psk(1 + (pi % 2))
                for c in range(NCH):
                    self.mm(P[:, 0:TB], wv[:, c, m * 128:(m + 1) * 128], src[:, c, :], c == 0, c == NCH - 1, [sk, srck(c)], pk)
                mo = s0 // 128 + m
                self.tt("dve", self.hT[:, mo, tsl], P[:, 0:TB], self.hT[:, mo, tsl], ALU.add, pk + [("h", mo)], [("h", mo)])

    def mlp_layer(self, l):
        TB = self.TB
        hid = self.big[:, 0:32 * TB].rearrange("p (c t) -> p c t", c=32)
        hk = lambda c: self.bigk(c * TB * 2, (c + 1) * TB * 2)
        for b in range(self.T // TB):
            tsl = slice(b * TB, (b + 1) * TB)
            self.rmsnorm_block(b, CV_OFF["norm_mlp"] + 8 * l)

            def evac(j, P, pk):
                tf = self.tmpf if (j % 2 == 0) else self.tmpf2
                tk = "tmpf" if (j % 2 == 0) else "tmpf2"
                self.actv(tf[:, 0:TB], P[:, 0:TB], AF.Relu, pk, [tk])
                self.tt("pool", hid[:, j, :], tf[:, 0:TB], tf[:, 0:TB], ALU.mult, [tk], hk(j))
            self.proj_fm(self.mlp_w_up[l], 0, 4 * D, evac)
            Wd = self.mlp_w_down[l].rearrange("(c p) n -> p c n", p=128)
            for m in range(NCH):
                sk, wv = self.load_slab(Wd[:, :, m * 128:(m + 1) * 128], 32, 128)
                pi = self.pi
                self.pi += 1
                P = self.ps[1 + (pi % 2)]
                pk = self.psk(1 + (pi % 2))
                for kc in range(32):
                    self.mm(P[:, 0:TB], wv[:, kc, :], hid[:, kc, :], kc == 0, kc == 31, [sk] + hk(kc), pk)
                self.tt("dve", self.hT[:, m, tsl], P[:, 0:TB], self.hT[:, m, tsl], ALU.add, pk + [("h", m)], [("h", m)])

    def mxfk(self, hp, c0, c1):
        bounds = [0, 128, 256, 896, 1024]
        return [("mxf", hp, q) for q in range(4) if c0 < bounds[q + 1] and c1 > bounds[q]]

    def mbk(self, hp, s0, s1=None):
        s1 = s0 + 1 if s1 is None else s1
        return [("mxb", hp, s) for s in range(s0, s1)]

    def kdk(self, c0, c1):
        return [("kdt", q) for q in range(c0 // 128, (c1 + 127) // 128)]

    def hgrn_layer(self, l):
        TB, NCI = self.TB, self.NCI
        j = l // 2
        W = self.hgrn_w_in[j]
        mb = self.mxb
        lg = self.big[:, 0:8192].bitcast(F32)
        lgk = self.bigk(0, 16384)
        self.dma("sp", lg, self.lb_d[:, :], [], lgk, "lb")
        self.actv(lg, lg, AF.Exp, lgk, lgk)
        t1 = self.mxf[:, 0, :]
        t2 = self.mxf[:, 1, :]
        k1 = self.mxfk(0, 0, 1024)
        k2 = self.mxfk(1, 0, 1024)
        self.tt("dve", t1, lg[:, 0:1024], lg[:, 1024:2048], ALU.add, lgk, k1)
        self.tt("dve", t1, t1, lg[:, 2048:3072], ALU.add, lgk + k1, k1)
        self.tt("dve", t1, t1, lg[:, 3072:4096], ALU.add, lgk + k1, k1)
        self.op("dve", "reciprocal", k1, k1, out=t1, in_=t1)
        if l == 1:
            self.op("dve", "tensor_copy", lgk, k2, out=t2, in_=lg[:, 1024:2048])
        else:
            self.tt("dve", t2, lg[:, 1024:2048], lg[:, 2048:3072], ALU.add, lgk, k2)
            self.tt("dve", t2, t2, lg[:, 3072:4096], ALU.add, lgk + k2, k2)
        LB = self.lbt[:, 0, :]
        OML = self.lbt[:, 1, :]
        self.tt("dve", LB, t2, t1, ALU.mult, k1 + k2, ["lbt"])
        self.ts("dve", OML, LB, -1.0, 1.0, ALU.mult, ALU.add, ["lbt"], ["oml"])
        self.op("dve", "memset", [], [("Sst", h) for h in range(H)], self.Sst[:], 0.0)
        self.op("pool", "memset", [], [("Sbf", h) for h in range(H)], self.Sbf[:], 0.0)
        gbytes = NCI * 4096
        G = self.big[:, 0:gbytes // 2].bitcast(F32).rearrange("p (c n) -> p c n", c=NCI)
        V = self.big[:, gbytes // 2:gbytes // 2 + NCI * 1024].rearrange("p (c n) -> p c n", c=NCI)
        q0 = gbytes + NCI * 2048
        qT = self.big[:, q0 // 2:q0 // 2 + H * TB].rearrange("p (h t) -> p h t", h=H)
        Gk = lambda ci, half: self.bigk(ci * 4096 + half * 2048, ci * 4096 + (half + 1) * 2048)
        Vk = lambda ci, half: self.bigk(gbytes + ci * 2048 + half * 1024, gbytes + ci * 2048 + (half + 1) * 1024)
        qk = lambda h: self.bigk(q0 + h * TB * 2, q0 + (h + 1) * TB * 2)
        qsc = float(128 ** -0.5)
        for b in range(self.T // TB):
            tsl = slice(b * TB, (b + 1) * TB)
            self.rmsnorm_block(b, CV_OFF["norm_mix"] + 8 * l)

            def evac_q(jj, P, pk):
                self.actv(qT[:, jj, :], P[:, 0:TB], AF.Silu, pk, qk(jj))
                self.ts("pool", qT[:, jj, :], qT[:, jj, :], qsc, None, ALU.mult, ALU.bypass, qk(jj), qk(jj))
            self.proj_fm(W, 0, 1024, evac_q)

            def evac_f(si, tt_, P, pk, w_):
                cs = slice(si * 512, (si + 1) * 512)
                a = self.tmpf[:, 0:512]
                self.actv(a, P[:, 0:512], AF.Exp, pk, ["tmpf"], scale=-1.0)
                self.ts("dve", a, a, 1.0, None, ALU.add, ALU.bypass, ["tmpf"], ["tmpf"])
                self.op("dve", "reciprocal", ["tmpf"], ["tmpf"], out=a, in_=a)
                self.tt("dve", a, a, OML[:, cs], ALU.mult, ["tmpf", "oml"], ["tmpf"])
                self.tt("dve", a, a, LB[:, cs], ALU.add, ["tmpf", "lbt"], ["tmpf"])
                self.actv(G[:, tt_, cs], a, AF.Ln, ["tmpf"], Gk(tt_, si))
            self.proj_tm(W, 1024, 1024, evac_f)
            self.proj_tm(W, 2048, 1024, lambda si, tt_, P, pk, w_: self.actv(V[:, tt_, si * 512:(si + 1) * 512], P[:, 0:512], AF.Copy, pk, Vk(tt_, si)))
            self.proj_fm(W, 3072, 1024, lambda jj, P, pk: self.actv(self.gate[:, jj, :], P[:, 0:TB], AF.Silu, pk, [("gate", jj)]))
            for ci in range(NCI):
                csl = slice(ci * 128, (ci + 1) * 128)
                for half in range(2):
                    hs = slice(half * 512, (half + 1) * 512)
                    P = self.ps[3]
                    p3 = self.psk(3)
                    self.mm(P[:, :], self.cfa("SUP"), G[:, ci, hs], True, True, Gk(ci, half) + ["cf"], p3)
                    e1 = self.tmpf[:, 0:512]
                    e2 = self.tmpf2[:, 0:512]
                    self.actv(e1, G[:, ci, hs], AF.Exp, Gk(ci, half), ["tmpf"])
                    self.actv(e2, P[:, :], AF.Exp, p3, ["tmpf2"])
                    self.ts("dve", e1, e1, -1.0, 1.0, ALU.mult, ALU.add, ["tmpf"], ["tmpf"])
                    self.tt("dve", self.kdt[:, hs], e1, e2, ALU.mult, ["tmpf", "tmpf2"], self.kdk(half * 512, (half + 1) * 512))
                def head(h, ci=ci, csl=csl):
                    hp = h % 2
                    hsl = slice(h * 128, (h + 1) * 128)
                    PE0, PE1 = self.ps[4 + 2 * hp], self.ps[5 + 2 * hp]
                    pk0, pk1 = self.psk(4 + 2 * hp), self.psk(5 + 2 * hp, 0, 384)
                    RH = self.cfa("RH", 896)
                    self.mm(PE0[:, :], G[:, ci, hsl], RH[:, 0:512], True, True, Gk(ci, h // 4) + ["cf"], pk0)
                    self.mm(PE1[:, 0:384], G[:, ci, hsl], RH[:, 512:896], True, True, Gk(ci, h // 4) + ["cf"], pk1)
                    E = self.mxf[:, hp, 0:896]
                    ek = self.mxfk(hp, 0, 896)
                    self.actv(E[:, 0:512], PE0[:, :], AF.Exp, pk0, self.mxfk(hp, 0, 512))
                    self.actv(E[:, 512:896], PE1[:, 0:384], AF.Exp, pk1, self.mxfk(hp, 512, 896))
                    kTf = self.mxf[:, hp, 896:1024]
                    kfk = self.mxfk(hp, 896, 1024)
                    self.ts("dve", kTf, E[:, 768:896], -1.0, 1.0, ALU.mult, ALU.add, ek, kfk)
                    KT4 = mb[:, hp, 0:4, :]
                    self.tt("dve", KT4, E[:, 0:512].rearrange("p (r n) -> p r n", r=4),
                            kTf.unsqueeze(1).to_broadcast([128, 4, 128]), ALU.mult, ek + kfk, self.mbk(hp, 0, 4))
                    QQ = mb[:, hp, 4:6, :]
                    self.tt("dve", QQ, E[:, 512:768].rearrange("p (r n) -> p r n", r=2),
                            qT[:, h, csl].unsqueeze(1).to_broadcast([128, 2, 128]), ALU.mult, ek + qk(h), self.mbk(hp, 4, 6))
                    PA = self.ps[1 + hp]
                    pa = self.psk(1 + hp)
                    for r4 in range(4):
                        self.mm(PA[:, r4 * 32:(r4 + 1) * 32], KT4[:, r4, :], QQ[:, 0, r4 * 32:(r4 + 1) * 32], True, True,
                                self.mbk(hp, 0, 6), pa)
                    AT = mb[:, hp, 6, :]
                    self.tt("dve", AT, PA[:, 0:128], self.cba("MASKT"), ALU.mult, pa + ["cb"], self.mbk(hp, 6))
                    PO = self.ps[0 if hp == 0 else 3]
                    po0 = po1 = self.psk(0 if hp == 0 else 3)
                    self.mm(PO[:, 0:128], V[:, ci, hsl], AT, True, False, Vk(ci, h // 4) + self.mbk(hp, 6), po0)
                    self.mm(PO[:, 0:128], self.Sbf[:, h, :], QQ[:, 1, :], False, True, [("Sbf", h)] + self.mbk(hp, 5), po0)
                    self.actv(self.OT[:, h, csl], PO[:, 0:128], AF.Copy, po0, [("OT", h)])
                    self.mm(PO[:, 128:256], self.kdt[:, hsl], V[:, ci, hsl], True, True, self.kdk(h * 128, (h + 1) * 128) + Vk(ci, h // 4), po1)
                    self.stt(self.Sst[:, h, :], self.Sst[:, h, :], E[:, 767:768], PO[:, 128:256], ALU.mult, ALU.add,
                             [("Sst", h)] + ek + po1, [("Sst", h)])
                    self.actv(self.Sbf[:, h, :], self.Sst[:, h, :], AF.Copy, [("Sst", h)], [("Sbf", h)])

                for h0 in range(0, H, 2):
                    self.S.interleave([lambda h0=h0: head(h0), lambda h0=h0: head(h0 + 1)])
            self.sumsq_rstd([self.OT[:, h, :] for h in range(H)], [("OT", h) for h in range(H)], TB, 1.0 / D, EPS)
            self.actv(self.rstd[:, 0:TB], self.lnt[:, 0:TB], AF.Exp, ["lnt"], ["rstd"], scale=-0.5)
            go = CV_OFF["gnorm"] + 8 * j
            for h in range(H):
                self.stt(self.OT[:, h, :], self.OT[:, h, :], self.cv[:, go + h:go + h + 1], self.rstd[:, 0:TB], ALU.mult, ALU.mult,
                         [("OT", h), "cv", "rstd"], [("OT", h)])
                self.tt("pool", self.OT[:, h, :], self.OT[:, h, :], self.gate[:, h, :], ALU.mult, [("OT", h), ("gate", h)], [("OT", h)])
            self.out_proj(self.hgrn_w_out[j], self.OT, lambda c: ("OT", c), tsl)

    def gdn_layer(self, l):
        TB, NCI = self.TB, self.NCI
        j = l // 2
        W = self.gdn_w_in[j]
        QKV = self.big[:, 0:24 * TB].rearrange("p (c t) -> p c t", c=24)
        ck = lambda c: self.bigk(c * TB * 2, (c + 1) * TB * 2)
        nega = self.smallf[:, 0, 0:8]
        self.actv(nega, self.rw[:, 8 * j:8 * j + 8], AF.Exp, ["rw"], ["nega"])
        self.ts("dve", nega, nega, -1.0, None, ALU.mult, ALU.bypass, ["nega"], ["nega"])
        self.op("dve", "memset", [], [("Sst", h) for h in range(H)], self.Sst[:], 0.0)
        self.op("pool", "memset", [], [("Sbf", h) for h in range(H)], self.Sbf[:], 0.0)
        self.op("pool", "memset", [], [("halo", c) for c in range(24)], self.halo[:], 0.0)
        cvo = CV_OFF["conv"] + j * 96
        for b in range(self.T // TB):
            tsl = slice(b * TB, (b + 1) * TB)
            self.rmsnorm_block(b, CV_OFF["norm_mix"] + 8 * l)

            def evac_qkv(jj, P, pk):
                pp = 0
                slot = jj % 4
                pv = self.pre[:, pp, slot, :]
                pkk = ("pre", pp, slot)
                self.actv(pv[:, 3:3 + TB], P[:, 0:TB], AF.Copy, pk, [pkk])
                self.op("pool", "tensor_copy", [("halo", jj)], [pkk], out=pv[:, 0:3], in_=self.halo[:, jj, :])
                acc = self.tmpf[:, 0:TB] if jj % 2 == 0 else self.tmpf2[:, 0:TB]
                ak = "tmpf" if jj % 2 == 0 else "tmpf2"
                wc = cvo + jj * 4
                self.ts("dve", acc, pv[:, 0:TB], self.cv[:, wc:wc + 1], None, ALU.mult, ALU.bypass, [pkk, "cv"], [ak])
                for tap in range(1, 4):
                    self.stt(acc, pv[:, tap:tap + TB], self.cv[:, wc + tap:wc + tap + 1], acc, ALU.mult, ALU.add, [pkk, "cv", ak], [ak])
                self.op("pool", "tensor_copy", [pkk], [("halo", jj)], out=self.halo[:, jj, :], in_=pv[:, TB:TB + 3])
                self.actv(QKV[:, jj, :], acc, AF.Silu, [ak], ck(jj))
            self.proj_fm(W, 0, 3072, evac_qkv)
            self.proj_fm(W, 3072, 1024, lambda jj, P, pk: self.actv(self.gate[:, jj, :], P[:, 0:TB], AF.Silu, pk, [("gate", jj)]))
            if self.dbg_stop == 1:
                continue

            def evac_ab(si, tt_, P, pk, w_):
                x1 = self.smallf[:, 1, 0:16]
                self.tt("dve", x1[:, 0:8], P[:, 0:8], self.rw[:, 16 + 8 * j:24 + 8 * j], ALU.add, pk + ["rw"], ["x1"])
                self.ts("dve", x1[:, 8:16], P[:, 8:16], -1.0, None, ALU.mult, ALU.bypass, pk, ["x1b"])
                self.actv(x1, x1, AF.Exp, ["x1", "x1b"], ["x1", "x1b"])
                self.actv(x1, x1, AF.Ln, ["x1", "x1b"], ["x1", "x1b"], bias=1.0)
                self.tt("dve", self.Gs[:, tt_, 0:8], x1[:, 0:8], nega, ALU.mult, ["x1", "nega"], [("Gs", tt_)])
                self.ts("dve", self.Gs[:, tt_, 8:16], x1[:, 8:16], -1.0, None, ALU.mult, ALU.bypass, ["x1b"], [("Gsb", tt_)])
            self.proj_tm(W, 4096, 16, evac_ab)
            if self.dbg_stop == 2:
                continue
            for c in range(16):
                self.sumsq_rstd([QKV[:, c, :]], [ck(c)[0]], TB, 1.0, EPS, pk=0)
                self.actv(self.rstd[:, 0:TB], self.lnt[:, 0:TB], AF.Exp, ["lnt"], ["rstd"], scale=-0.5)
                if c < 8:
                    self.stt(QKV[:, c, :], QKV[:, c, :], float(128 ** -0.5), self.rstd[:, 0:TB], ALU.mult, ALU.mult, ck(c) + ["rstd"], ck(c))
                else:
                    self.tt("dve", QKV[:, c, :], QKV[:, c, :], self.rstd[:, 0:TB], ALU.mult, ck(c) + ["rstd"], ck(c))
            if self.dbg_stop == 3:
                continue
            for ci in range(NCI):
                self.gdn_chunk(ci, QKV, ck)
            if self.dbg_stop >= 4:
                continue
            oo = CV_OFF["onorm"] + j
            for h in range(H):
                self.sumsq_rstd([self.OT[:, h, :]], [("OT", h)], TB, 1.0 / 128, EPS, pk=0)
                self.actv(self.rstd[:, 0:TB], self.lnt[:, 0:TB], AF.Exp, ["lnt"], ["rstd"], scale=-0.5)
                self.stt(self.OT[:, h, :], self.OT[:, h, :], self.cv[:, oo:oo + 1], self.rstd[:, 0:TB], ALU.mult, ALU.mult,
                         [("OT", h), "cv", "rstd"], [("OT", h)])
                self.tt("pool", self.OT[:, h, :], self.OT[:, h, :], self.gate[:, h, :], ALU.mult, [("OT", h), ("gate", h)], [("OT", h)])
            self.out_proj(self.gdn_w_out[j], self.OT, lambda c: ("OT", c), tsl)

    def gdn_chunk(self, ci, QKV, ck):
        mb = self.mxb
        csl = slice(ci * 128, (ci + 1) * 128)
        g = self.Gs[:, ci, 0:8]
        lnb = self.Gs[:, ci, 8:16]
        gk, lk = ("Gs", ci), ("Gsb", ci)
        PSm = self.ps[3]
        p3a = self.psk(3, 0, 128)
        self.mm(PSm[:, 0:8], self.cfa("LT"), g, True, False, [gk, "cf"], p3a)
        self.mm(PSm[:, 0:8], self.cfa("ID"), lnb, False, True, [lk, "cf"], p3a)
        self.mm(PSm[:, 8:16], self.cfa("SUP"), g, True, True, [gk, "cf"], p3a)
        self.mm(PSm[:, 16:24], self.cfa("ONES"), g, True, True, [gk, "cf"], p3a)
        self.mm(PSm[:, 24:32], self.cfa("ID"), lnb, True, True, [lk, "cf"], p3a)
        EG = self.smallf[:, 2, 0:32]
        self.actv(EG, PSm[:, 0:32], AF.Exp, p3a, ["EG"])
        negb = self.smallf[:, 3, 0:8]
        self.ts("dve", negb, EG[:, 24:32], -1.0, None, ALU.mult, ALU.bypass, ["EG"], ["negb"])
        if self.dbg_stop == 4:
            return
        nr, ys = self.nr, self.ysb
        def head(h):
            hp = h % 2
            qTh, kTh, vTh = QKV[:, h, csl], QKV[:, 8 + h, csl], QKV[:, 16 + h, csl]
            qk_, kk_, vk_ = ck(h), ck(8 + h), ck(16 + h)
            b3 = 0 if hp == 0 else 3
            PT = self.ps[b3][:, :].bitcast(BF16)
            p3b = p3c = p3d = self.psk(b3)
            Gb = self.mxf[:, hp, 0:128]
            Gb2 = self.mxf[:, hp, 128:256]
            gbk, gb2k = self.mxfk(hp, 0, 128), self.mxfk(hp, 128, 256)
            self.ts("dve", Gb, self.cfa("SUP"), g[:, h:h + 1], None, ALU.mult, ALU.bypass, ["cf", gk], gbk)
            self.ts("dve", Gb2, self.cfa("LT"), g[:, h:h + 1], None, ALU.mult, ALU.bypass, ["cf", gk], gb2k)
            PD = self.ps[4 + hp]
            pd0, pd1, pd2 = self.psk(4 + hp, 0, 128), self.psk(4 + hp, 128, 256), self.psk(4 + hp, 256, 384)
            self.mm(PD[:, 0:128], self.cfa("LT"), Gb, True, False, ["cf"] + gbk, pd0)
            self.mm(PD[:, 0:128], self.cba("ID"), self.cba("NEGS"), False, True, ["cb"], pd0)
            self.mm(PD[:, 128:256], self.cfa("SUP"), Gb2, True, False, ["cf"] + gb2k, pd1)
            self.mm(PD[:, 128:256], self.cba("ID"), self.cba("NEGCT"), False, True, ["cb"], pd1)
            self.mm(PD[:, 256:384], self.cfa("ONES"), Gb2, True, True, ["cf"] + gb2k, pd2)
            DEC = mb[:, hp, 0:3, :]
            self.actv(DEC[:, 1:3, :], PD[:, 128:384].rearrange("p (a n) -> p a n", a=2), AF.Exp, pd1 + pd2, self.mbk(hp, 1, 3))
            fz = self.mxf[:, hp, :]
            zk = self.mxfk(hp, 256, 896)
            decS, N, No, U64, T64 = fz[:, 256:384], fz[:, 384:512], fz[:, 512:640], fz[:, 640:768], fz[:, 768:896]
            Pm = fz[:, 896:1024]
            pmk = self.mxfk(hp, 896, 1024)
            self.actv(decS, PD[:, 0:128], AF.Exp, pd0, zk)
            PK = self.ps[6 + hp]
            pq0, pq1, pq2, pq3 = (self.psk(6 + hp, 0, 128), self.psk(6 + hp, 128, 256), self.psk(6 + hp, 256, 384),
                                  self.psk(6 + hp, 384, 512))
            self.mm(PK[:, 0:128], kTh, kTh, True, True, kk_, pq0)
            self.mm(PK[:, 128:256], kTh, qTh, True, True, kk_ + qk_, pq1)
            self.stt(N, PK[:, 0:128], negb[:, h:h + 1], decS, ALU.mult, ALU.mult, pq0 + ["negb"] + zk, zk)
            AT = mb[:, hp, 4, :]
            self.tt("dve", AT, PK[:, 128:256], DEC[:, 1, :], ALU.mult, pq1 + self.mbk(hp, 1), self.mbk(hp, 4))
            Nd = nr[:, hp, 0, 0:128]
            self.tt("pool", Nd, N, self.cba("BD"), ALU.mult, zk + ["cb"], [("nrN", hp, 0)])
            self.tt("pool", No, N, self.cba("OD"), ALU.mult, zk + ["cb"], zk)
            P3 = self.ps[b3]
            self.op("pe", "transpose", [("nrN", hp, 0), "cf"], p3b, P3[:, 128:256], Nd, self.cfa("ID"))
            Yd = ys[:, hp, 0, 0:128]
            self.cp(Yd, P3[:, 128:256], p3b, [("ysY", hp, 0)])
            self.mm(PK[:, 256:384], Nd, Yd, True, True, [("nrN", hp, 0), ("ysY", hp, 0)], pq2)
            self.mm(PK[:, 384:512], Yd, Nd, True, True, [("nrN", hp, 0), ("ysY", hp, 0)], pq3)
            self.cp(ys[:, hp, 1, 0:128], PK[:, 256:384], pq2, [("ysY", hp, 1)])
            self.tt("pool", ys[:, hp, 1, 128:256], Yd, self.cfa("ID"), ALU.add, [("ysY", hp, 0), "cf"], [("ysS", hp, 1)])
            self.cp(nr[:, hp, 1, 0:128], PK[:, 384:512], pq3, [("nrN", hp, 1)])
            self.tt("pool", nr[:, hp, 1, 128:256], Nd, self.cfa("ID"), ALU.add, [("nrN", hp, 0), "cf"], [("nrR", hp, 1)])
            cur = 1
            PA_, PB_ = self.ps[4 + hp], self.ps[6 + hp]
            pa01, pb01 = pd0 + pd1, pq0 + pq1
            for lev in range(1, 5):
                nx = 1 - cur
                rk = [("nrN", hp, cur), ("nrR", hp, cur), ("ysY", hp, cur), ("ysS", hp, cur)]
                self.mm(PA_[:, 0:256], nr[:, hp, cur, 0:128], ys[:, hp, cur, :], True, True, rk, pa01)
                self.mm(PB_[:, 0:256], ys[:, hp, cur, 0:128], nr[:, hp, cur, :], True, True, rk, pb01)
                self.cp(ys[:, hp, nx, 0:128], PA_[:, 0:128], pd0, [("ysY", hp, nx)])
                self.tt("dve", ys[:, hp, nx, 128:256], PA_[:, 128:256], ys[:, hp, cur, 128:256], ALU.add, pd1 + [("ysS", hp, cur)], [("ysS", hp, nx)])
                self.cp(nr[:, hp, nx, 0:128], PB_[:, 0:128], pq0, [("nrN", hp, nx)])
                self.tt("dve", nr[:, hp, nx, 128:256], PB_[:, 128:256], nr[:, hp, cur, 128:256], ALU.add, pq1 + [("nrR", hp, cur)], [("nrR", hp, nx)])
                cur = nx
            rk = [("nrN", hp, cur), ("nrR", hp, cur), ("ysY", hp, cur), ("ysS", hp, cur)]
            self.mm(PA_[:, 0:128], nr[:, hp, cur, 0:128], ys[:, hp, cur, 128:256], True, True, rk, pd0)
            self.mm(PB_[:, 0:128], ys[:, hp, cur, 0:128], nr[:, hp, cur, 128:256], True, True, rk, pq0)
            self.tt("dve", U64, PA_[:, 0:128], ys[:, hp, cur, 128:256], ALU.add, pd0 + [("ysS", hp, cur)], zk)
            self.tt("dve", T64, PB_[:, 0:128], nr[:, hp, cur, 128:256], ALU.add, pq0 + [("nrR", hp, cur)], zk)
            self.mm(PA_[:, 0:128], No, U64, True, True, zk, pd0)
            self.cp(Pm, PA_[:, 0:128], pd0, pmk)
            self.mm(PB_[:, 0:128], T64, Pm, True, True, zk + pmk, pq0)
            U = mb[:, hp, 9, :]
            self.tt("dve", U, PB_[:, 0:128], U64, ALU.add, pq0 + zk, self.mbk(hp, 9))
            if self.dbg_stop == 8:
                return
            self.tr(PT[:, 512:640], kTh, kk_, p3c)
            kbgn = mb[:, hp, 10, :]
            kdec = mb[:, hp, 11, :]
            self.ts("dve", kbgn, PT[:, 512:640], EG[:, h:h + 1], -1.0, ALU.mult, ALU.mult, p3c + ["EG"], self.mbk(hp, 10))
            self.ts("dve", kdec, PT[:, 512:640], EG[:, 8 + h:9 + h], None, ALU.mult, ALU.bypass, p3c + ["EG"], self.mbk(hp, 11))
            self.tr(PT[:, 768:896], vTh, vk_, p3d)
            c_vb, c_w, c_vn, c_qd = hp * 128, 256 + hp * 128, 512 + hp * 128, 768 + hp * 128
            vb = self.kdt[:, c_vb:c_vb + 128]
            self.ts("dve", vb, PT[:, 768:896], EG[:, 24 + h:25 + h], None, ALU.mult, ALU.bypass, p3d + ["EG"], self.kdk(c_vb, c_vb + 128))
            if self.dbg_stop == 9:
                return
            PO = self.ps[1 + hp]
            po = [self.psk(1 + hp) for q in range(4)]
            self.mm(PO[:, 0:128], kbgn, U, True, True, self.mbk(hp, 10) + self.mbk(hp, 9), po[0])
            wTn = self.kdt[:, c_w:c_w + 128]
            self.cp(wTn, PO[:, 0:128], po[0], self.kdk(c_w, c_w + 128))
            self.mm(PO[:, 128:256], U, vb, True, False, self.mbk(hp, 9) + self.kdk(c_vb, c_vb + 128), po[1])
            self.mm(PO[:, 128:256], wTn, self.Sbf[:, h, :], False, True, self.kdk(c_w, c_w + 128) + [("Sbf", h)], po[1])
            vnew = self.kdt[:, c_vn:c_vn + 128]
            vnk = self.kdk(c_vn, c_vn + 128)
            self.cp(vnew, PO[:, 128:256], po[1], vnk)
            qd = self.kdt[:, c_qd:c_qd + 128]
            qdk = self.kdk(c_qd, c_qd + 128)
            self.tt("pool", qd, qTh, DEC[:, 2, :], ALU.mult, qk_ + self.mbk(hp, 2), qdk)
            self.mm(PO[:, 256:384], self.Sbf[:, h, :], qd, True, False, [("Sbf", h)] + qdk, po[2])
            self.mm(PO[:, 256:384], vnew, AT, False, True, vnk + self.mbk(hp, 4), po[2])
            self.cp(self.OT[:, h, csl], PO[:, 256:384], po[2], [("OT", h)])
            self.mm(PO[:, 384:512], kdec, vnew, True, True, self.mbk(hp, 11) + vnk, po[3])
            self.stt(self.Sst[:, h, :], self.Sst[:, h, :], EG[:, 16 + h:17 + h], PO[:, 384:512], ALU.mult, ALU.add,
                     [("Sst", h), "EG"] + po[3], [("Sst", h)])
            self.cp(self.Sbf[:, h, :], self.Sst[:, h, :], [("Sst", h)], [("Sbf", h)])

        for h0 in range(0, H, 2):
            self.S.interleave([lambda h0=h0: head(h0), lambda h0=h0: head(h0 + 1)])


def make_in_map(inp, hT, prog):
    m = {"hT_in": np.ascontiguousarray(hT, dtype=np.float32), "cf": _consts(), "cvec": _make_cvec(inp), "rows": _make_rows(inp)}
    need_g, need_h = prog.need_g, prog.need_h
    if need_g:
        m["gdn_w_in"] = inp["gdn_w_in"]
        m["gdn_w_out"] = inp["gdn_w_out"]
    if need_h:
        m["hgrn_w_in"] = inp["hgrn_w_in"]
        m["hgrn_w_out"] = inp["hgrn_w_out"]
        m["lbrows"] = _make_lbrows(inp)
    if prog.layers:
        m["mlp_w_up"] = inp["mlp_w_up"]
        m["mlp_w_down"] = inp["mlp_w_down"]
    return m


N_CORES = 8
_CFG = {"T": 2048, "TB": 512, "nseq": 2, "groups": [[0, 1, 2, 3]], "skip_mixer": False}


def kernel(**inp):
    inp = {k: np.asarray(v) for k, v in inp.items()}
    x = inp["x"]
    B, T, Dm = x.shape
    nseq = B // N_CORES
    hT = [np.ascontiguousarray(x[c * nseq:(c + 1) * nseq].reshape(nseq * T, Dm).T) for c in range(N_CORES)]
    groups = _CFG["groups"]
    for gi, layers in enumerate(groups):
        p = Prog(nseq, T, _CFG["TB"], layers, final_norm=(gi == len(groups) - 1))
        p.skip_mixer = _CFG["skip_mixer"]
        nc = p.build()
        base = make_in_map(inp, hT[0], p)
        in_maps = []
        for c in range(N_CORES):
            m = dict(base)
            m["hT_in"] = hT[c]
            in_maps.append(m)
        res = run_bass_kernel_spmd(nc, in_maps, core_ids=list(range(N_CORES)))
        hT = [np.ascontiguousarray(r["hT_out"]) for r in res.results]
    out = np.stack([h.T.reshape(nseq, T, Dm) for h in hT], axis=0).reshape(B, T, Dm)
    return out.astype(np.float32)
```

```python
import numpy as np
from contextlib import ExitStack
import concourse.bass as bass
import concourse.mybir as mybir
from concourse.bass_utils import run_bass_kernel_spmd

F32 = mybir.dt.float32
BF16 = mybir.dt.bfloat16
AF = mybir.ActivationFunctionType
ALU = mybir.AluOpType

D = 1024
NCH = 8
H = 8
C = 128
EPS = 1e-6
GDN_IN = 4112
MAXV = 30000


class _Op:
    __slots__ = ("eng", "fn", "deps", "kind", "stream", "sig", "upg")

    def __init__(self, eng, fn, deps, kind, stream):
        self.eng, self.fn, self.deps, self.kind, self.stream, self.sig = eng, fn, deps, kind, stream, None


class Sched:
    ENGS = ("pe", "act", "dve", "pool", "sp")

    def __init__(self):
        self.ops = []
        self.lastw = {}
        self.readers = {}
        self.stream_last = {}
        self.recording = None

    def interleave(self, bodies):
        lists = []
        for b in bodies:
            self.recording = []
            b()
            lists.append(self.recording)
            self.recording = None
        n = max(len(l) for l in lists)
        for i in range(n):
            for l in lists:
                if i < len(l):
                    self.add(*l[i])

    def add(self, eng, fn, r=(), w=(), kind="c", stream=None):
        if self.recording is not None:
            self.recording.append((eng, fn, tuple(r), tuple(w), kind, stream))
            return -1
        i = len(self.ops)
        deps = {}

        def upd(d, t):
            o = deps.get(d)
            if o is None or (t == "raw") or (t == "waw" and o == "war"):
                deps[d] = t
        for k in r:
            if k in self.lastw:
                upd(self.lastw[k], "raw")
        for k in w:
            if k in self.lastw:
                upd(self.lastw[k], "waw")
            for d in self.readers.get(k, ()):
                if d != i:
                    upd(d, "war")
        op = _Op(eng, fn, deps, kind, stream)
        op.upg = dict(self.stream_last)
        self.ops.append(op)
        if kind == "d":
            self.stream_last[stream] = i
        for k in r:
            self.readers.setdefault(k, []).append(i)
        for k in w:
            self.lastw[k] = i
            self.readers[k] = []
        return i

    def dma(self, eng, fn, r, w, stream):
        return self.add(eng, fn, r, w, kind="d", stream=stream)

    def _needs_sync(self, op, D, t):
        if D.kind == "d":
            return True
        if D.eng != op.eng:
            return True
        if op.kind == "d":
            return True
        if op.eng == "pe":
            return False
        return True

    def emit(self, nc, es):
        ops = self.ops
        need = [False] * len(ops)
        for op in ops:
            for d, t in op.deps.items():
                if self._needs_sync(op, ops[d], t):
                    need[d] = True
        cnt = {}
        semkeys = []

        def nxt(key, inc):
            st = cnt.get(key)
            if st is None:
                st = cnt[key] = [0, 0]
                semkeys.append((key, 0))
            if st[1] + inc > MAXV:
                st[0] += 1
                st[1] = 0
                semkeys.append((key, st[0]))
            st[1] += inc
            return (key, st[0]), st[1]
        for i, op in enumerate(ops):
            if op.kind == "d":
                op.sig = nxt(("dma", op.stream), 16)
            elif need[i]:
                op.sig = nxt(("eng", op.eng), 1)
        sems = {}
        for sk in semkeys:
            nm = "s_%s_%s_%d" % (sk[0][0], str(sk[0][1]), sk[1])
            sems[sk] = es.enter_context(nc.semaphore(nm))
        self.nsems = len(sems)
        block = es.enter_context(nc.Block())
        per = {e: [] for e in self.ENGS}
        for i, op in enumerate(ops):
            per[op.eng].append(i)

        def run(eng_name, engine):
            waited = {}
            for i in per[eng_name]:
                op = ops[i]
                for d, t in sorted(op.deps.items()):
                    Dop = ops[d]
                    if not self._needs_sync(op, Dop, t):
                        continue
                    if Dop.kind == "d":
                        Dop = ops[op.upg[Dop.stream]]
                    sk, val = Dop.sig
                    if waited.get(sk, 0) >= val:
                        continue
                    engine.wait_ge(sems[sk], val)
                    waited[sk] = val
                ins = op.fn(engine)
                if op.sig is not None:
                    assert ins is not None
                    ins.then_inc(sems[op.sig[0]], 16 if op.kind == "d" else 1)

        @block.tensor
        def _(e):
            run("pe", e)

        @block.scalar
        def _(e):
            run("act", e)

        @block.vector
        def _(e):
            run("dve", e)

        @block.gpsimd
        def _(e):
            run("pool", e)

        @block.sync
        def _(e):
            run("sp", e)


def _consts():
    k = np.arange(128)[:, None]
    j = np.arange(128)[None, :]
    f = np.float32
    LT = (k <= j).astype(f)
    SUP = (k > j).astype(f)
    ID = (k == j).astype(f)
    ONES = np.ones((128, 128), f)
    NEG = -30000.0
    NEGS = np.where(k > j, 0.0, NEG).astype(f)
    NEGCT = np.where(j >= k, 0.0, NEG).astype(f)
    BD = ((k // 64) == (j // 64)).astype(f)
    OD = 1.0 - BD
    MASKT = (k <= j).astype(f)
    Ms = []
    for r in range(4):
        lo, hi = 32 * r, 32 * (r + 1)
        M = np.zeros((128, 128), f)
        M += ((j < lo) & (k > j) & (k < lo)) * 1.0
        M -= ((j >= lo) & (j < hi) & (k >= lo) & (k <= j)) * 1.0
        Ms.append(M.astype(f))
    MQ = ((k >= (j // 32) * 32) & (k <= j)).astype(f)
    RH = np.concatenate(Ms + [MQ, LT, ID], axis=1)
    cf = np.concatenate([LT, SUP, ID, ONES, RH, ID, ONES, NEGS, NEGCT, BD, OD, MASKT], axis=1)
    return np.ascontiguousarray(cf.astype(f))


CF_OFF = {"LT": 0, "SUP": 128, "ID": 256, "ONES": 384, "RH": 512}
CF_W = 512 + 896
CB_OFF = {"ID": 0, "ONES": 128, "NEGS": 256, "NEGCT": 384, "BD": 512, "OD": 640, "MASKT": 768}
CB_W = 896


def _cvec_layout():
    off = {}
    n = 0
    for nm, w in (("norm_mix", 32), ("norm_mlp", 32), ("norm_final", 8), ("conv", 2 * 24 * 4),
                  ("onorm", 2), ("gnorm", 16)):
        off[nm] = n
        n += w
    return off, n


CV_OFF, CV_W = _cvec_layout()


def _make_cvec(inp):
    cv = np.zeros((128, CV_W), np.float32)
    cv[:, CV_OFF["norm_mix"]:CV_OFF["norm_mix"] + 32] = inp["norm_mix"].reshape(4, 8, 128).transpose(2, 0, 1).reshape(128, 32)
    cv[:, CV_OFF["norm_mlp"]:CV_OFF["norm_mlp"] + 32] = inp["norm_mlp"].reshape(4, 8, 128).transpose(2, 0, 1).reshape(128, 32)
    cv[:, CV_OFF["norm_final"]:CV_OFF["norm_final"] + 8] = inp["norm_final"].reshape(8, 128).T
    cw = inp["gdn_conv"].reshape(2, 4, 24, 128).transpose(3, 0, 2, 1).reshape(128, 2 * 24 * 4)
    cv[:, CV_OFF["conv"]:CV_OFF["conv"] + 192] = cw
    cv[:, CV_OFF["onorm"]:CV_OFF["onorm"] + 2] = inp["gdn_onorm"].T
    cv[:, CV_OFF["gnorm"]:CV_OFF["gnorm"] + 16] = inp["hgrn_gnorm"].reshape(2, 8, 128).transpose(2, 0, 1).reshape(128, 16)
    return cv


RW_W = 32


def _make_rows(inp):
    r = np.zeros((RW_W,), np.float32)
    r[0:16] = inp["gdn_a_log"].reshape(-1)
    r[16:32] = inp["gdn_dt_bias"].reshape(-1)
    return np.ascontiguousarray(np.broadcast_to(r[None, :], (128, RW_W)))


def _make_lbrows(inp):
    r = inp["hgrn_lb_logits"].reshape(-1).astype(np.float32)
    return np.ascontiguousarray(np.broadcast_to(r[None, :], (128, 4096)))


class Prog:
    def __init__(self, nseq, T, TB, layers, final_norm):
        self.nseq, self.T, self.TB, self.layers, self.final_norm = nseq, T, TB, list(layers), final_norm
        self.NTOK = nseq * T
        self.NCI = TB // 128
        self.S = Sched()
        self.nc = bass.Bass("TRN2", target_bir_lowering=False)
        self.slab_i = 0
        self.pi = 0
        self.skip_mixer = False
        self.dbg_stop = 0
        self.dbg2 = 9
        self.dbg3 = 0

    def dram_in(self, name, shape, dt=F32):
        return self.nc.dram_tensor(name, list(shape), dt, kind="ExternalInput").ap()

    def sb(self, name, shape, dt):
        return self.es.enter_context(self.nc.sbuf_tensor(name, list(shape), dt))

    def op(self, eng, meth, r, w, *a, **kw):
        return self.S.add(eng, lambda e: getattr(e, meth)(*a, **kw), r, w)

    def mm(self, out, lhsT, rhs, start, stop, r, w):
        return self.op("pe", "matmul", r, w, out, lhsT=lhsT, rhs=rhs, start=start, stop=stop)

    def tr(self, out, in_, r, w):
        return self.op("pe", "transpose", r + ["cb"], w, out, in_, self.cba("ID"))

    def actv(self, out, in_, func, r, w, **kw):
        return self.op("act", "activation", r, w, out=out, in_=in_, func=func, **kw)

    def acp(self, out, in_, r, w):
        return self.actv(out, in_, AF.Copy, r, w)

    def cp(self, out, in_, r, w):
        return self.op("dve", "tensor_copy", r, w, out=out, in_=in_)

    def tt(self, eng, out, in0, in1, op, r, w):
        return self.op(eng, "tensor_tensor", r, w, out=out, in0=in0, in1=in1, op=op)

    def ts(self, eng, out, in0, s1, s2, op0, op1, r, w):
        return self.op(eng, "tensor_scalar", r, w, out=out, in0=in0, scalar1=s1, scalar2=s2, op0=op0, op1=op1)

    def stt(self, out, in0, scalar, in1, op0, op1, r, w):
        return self.op("dve", "scalar_tensor_tensor", r, w, out=out, in0=in0, scalar=scalar, in1=in1, op0=op0, op1=op1)

    def dma(self, eng, out, in_, r, w, stream):
        return self.S.dma(eng, lambda e: e.dma_start(out=out, in_=in_), r, w, stream)

    def cfa(self, name, w=128):
        o = CF_OFF[name]
        return self.cf[:, o:o + w]

    def cba(self, name):
        o = CB_OFF[name]
        return self.cb[:, o:o + 128]

    def psk(self, bank, c0=0, c1=512):
        return [("ps", bank)]

    def bigk(self, lo, hi):
        return [("big", i) for i in range(lo // 512, (hi + 511) // 512)]

    def load_slab(self, src_ap, nc_, ncols):
        i = self.slab_i % 2
        self.slab_i += 1
        t = self.slabs[i]
        view = t[:, 0:nc_ * ncols].rearrange("p (c n) -> p c n", c=nc_)
        self.dma("pool", view, src_ap, [], [("slab", i)], "slab%d" % i)
        return ("slab", i), view

    def build(self):
        nc = self.nc
        NTOK, T, TB = self.NTOK, self.T, self.TB
        with ExitStack() as es:
            self.es = es
            self.hin = self.dram_in("hT_in", [D, NTOK])
            self.hout = nc.dram_tensor("hT_out", [D, NTOK], F32, kind="ExternalOutput").ap()
            self.cf_d = self.dram_in("cf", [128, CF_W + CB_W])
            self.cv_d = self.dram_in("cvec", [128, CV_W])
            self.rw_d = self.dram_in("rows", [128, RW_W])
            need_g = (not self.skip_mixer) and any(l % 2 == 0 for l in self.layers)
            need_h = (not self.skip_mixer) and any(l % 2 == 1 for l in self.layers)
            self.need_g, self.need_h = need_g, need_h
            if need_g:
                self.gdn_w_in = self.dram_in("gdn_w_in", [2, D, GDN_IN])
                self.gdn_w_out = self.dram_in("gdn_w_out", [2, D, D])
            if need_h:
                self.hgrn_w_in = self.dram_in("hgrn_w_in", [2, D, 4 * D])
                self.hgrn_w_out = self.dram_in("hgrn_w_out", [2, D, D])
                self.lb_d = self.dram_in("lbrows", [128, 4096])
            if self.layers:
                self.mlp_w_up = self.dram_in("mlp_w_up", [4, D, 4 * D])
                self.mlp_w_down = self.dram_in("mlp_w_down", [4, 4 * D, D])
            self.hT = self.sb("hT", [128, NCH, T], F32)
            self.yT = self.sb("yT", [128, NCH, TB], BF16)
            self.slabs = [self.sb("slab%d" % i, [128, 4096], BF16) for i in range(2)]
            self.cv = self.sb("cv", [128, CV_W], F32)
            self.rw = self.sb("rw", [128, RW_W], F32)
            self.cf = self.sb("cff", [128, CF_W], F32)
            self.cb = self.sb("cfb", [128, CB_W], BF16)
            self.rstd = self.sb("rstd", [128, 512], F32)
            self.lnt = self.sb("lnt", [128, 512], F32)
            self.sqs = self.sb("sqs", [128, 2, 512], BF16)
            self.big = self.sb("big", [128, 32 * TB], BF16)
            self.gate = self.sb("gate", [128, H, TB], BF16)
            self.OT = self.sb("OT", [128, H, TB], BF16)
            self.tmpf = self.sb("tmpf", [128, 1024], F32)
            self.tmpf2 = self.sb("tmpf2", [128, 1024], F32)
            self.Sst = self.sb("Sst", [128, H, 128], F32)
            self.Sbf = self.sb("Sbf", [128, H, 128], BF16)
            self.ps = [es.enter_context(nc.psum_tensor("ps%d" % i, [128, 512], F32)) for i in range(8)]
            if need_g or need_h:
                self.mxf = self.sb("mxf", [128, 2, 1024], F32)
                self.mxb = self.sb("mxb", [128, 2, 12, 128], BF16)
                self.lbt = self.sb("lbt", [128, 2, 1024], F32)
                self.kdt = self.sb("kdt", [128, 1024], BF16)
                self.smallf = self.sb("smallf", [128, 4, 64], F32)
            if need_g:
                self.pre = self.sb("pre", [128, 1, 4, 3 + TB], BF16)
                self.halo = self.sb("halo", [128, 24, 3], BF16)
                self.Gs = self.sb("Gs", [128, self.NCI, 16], F32)
                import os as _os
                if int(_os.environ.get("PADT", "0")):
                    self.padt = self.sb("padt", [128, int(_os.environ.get("PADT"))], BF16)
                self.nr = self.sb("nr", [128, 2, 2, 256], F32)
                self.ysb = self.sb("ysb", [128, 2, 2, 256], F32)
            self.emit_all()
            self.S.emit(nc, es)
        return nc

    def emit_all(self):
        T = self.T
        self.dma("sp", self.cf[:], self.cf_d[:, 0:CF_W], [], ["cf"], "c0")
        self.dma("pool", self.cb[:], self.cf_d[:, CF_W:CF_W + CB_W], [], ["cb"], "c3")
        self.dma("sp", self.cv[:], self.cv_d[:, :], [], ["cv"], "c1")
        self.dma("sp", self.rw[:], self.rw_d[:, :], [], ["rw"], "c2")
        for s in range(self.nseq):
            t0 = s * T
            for c in range(NCH):
                self.dma("sp", self.hT[:, c, :], self.hin[c * 128:(c + 1) * 128, t0:t0 + T], [], [("h", c)], "hin")
            for l in self.layers:
                if self.skip_mixer:
                    pass
                elif l % 2 == 0:
                    self.gdn_layer(l)
                else:
                    self.hgrn_layer(l)
                self.mlp_layer(l)
            if self.final_norm:
                for b in range(T // self.TB):
                    self.rmsnorm_block(b, CV_OFF["norm_final"], out_f32=True)
            for c in range(NCH):
                self.dma("sp", self.hout[c * 128:(c + 1) * 128, t0:t0 + T], self.hT[:, c, :], [("h", c)], [("hout", s, c)], "hout")
        self.S.add("sp", lambda e: None, r=[("hout", s, c) for s in range(self.nseq) for c in range(NCH)], w=["done"])

    def sumsq_rstd(self, srcs, keys, n, scale, bias, pk=0):
        P0 = self.ps[pk]
        pkey = self.psk(pk, 0, n)
        ns = len(srcs)
        for i, (sap, k) in enumerate(zip(srcs, keys)):
            sqv = self.sqs[:, i % 2, 0:n]
            self.actv(sqv, sap, AF.Square, [k], [("sqs", i % 2)])
            self.mm(P0[:, 0:n], self.cba("ONES"), sqv, i == 0, i == ns - 1, [("sqs", i % 2), "cb"], pkey)
        self.actv(self.lnt[:, 0:n], P0[:, 0:n], AF.Ln, pkey, ["lnt"], scale=scale, bias=bias)

    def rmsnorm_block(self, b, cvoff, out_f32=False):
        TB = self.TB
        tsl = slice(b * TB, (b + 1) * TB)
        self.sumsq_rstd([self.hT[:, c, tsl] for c in range(NCH)], [("h", c) for c in range(NCH)], TB, 1.0 / D, EPS)
        self.actv(self.rstd[:, 0:TB], self.lnt[:, 0:TB], AF.Exp, ["lnt"], ["rstd"], scale=-0.5)
        for c in range(NCH):
            if out_f32:
                self.stt(self.hT[:, c, tsl], self.hT[:, c, tsl], self.cv[:, cvoff + c:cvoff + c + 1], self.rstd[:, 0:TB],
                         ALU.mult, ALU.mult, [("h", c), "rstd", "cv"], [("h", c)])
            else:
                self.stt(self.yT[:, c, :], self.hT[:, c, tsl], self.cv[:, cvoff + c:cvoff + c + 1], self.rstd[:, 0:TB],
                         ALU.mult, ALU.mult, [("h", c), "rstd", "cv"], [("y", c)])

    def proj_fm(self, W, col0, ncols, evac):
        TB = self.TB
        yk = [("y", c) for c in range(NCH)]
        Wv = W.rearrange("(c p) n -> p c n", p=128)
        for s0 in range(0, ncols, 512):
            w_ = min(512, ncols - s0)
            sk, wv = self.load_slab(Wv[:, :, col0 + s0:col0 + s0 + w_], 8, w_)
            for m in range(w_ // 128):
                pi = self.pi
                self.pi += 1
                P = self.ps[1 + (pi % 2)]
                pk = self.psk(1 + (pi % 2))
                for c in range(NCH):
                    self.mm(P[:, 0:TB], wv[:, c, m * 128:(m + 1) * 128], self.yT[:, c, :], c == 0, c == NCH - 1, [sk] + yk, pk)
                evac((s0 // 128) + m, P, pk)

    def proj_tm(self, W, col0, ncols, evac):
        TB = self.TB
        yk = [("y", c) for c in range(NCH)]
        Wv = W.rearrange("(c p) n -> p c n", p=128)
        for s0 in range(0, ncols, 512):
            w_ = min(512, ncols - s0)
            sk, wv = self.load_slab(Wv[:, :, col0 + s0:col0 + s0 + w_], 8, w_)
            for tt_ in range(TB // 128):
                pi = self.pi
                self.pi += 1
                P = self.ps[1 + (pi % 2)]
                pk = self.psk(1 + (pi % 2))
                for c in range(NCH):
                    self.mm(P[:, 0:w_], self.yT[:, c, tt_ * 128:(tt_ + 1) * 128], wv[:, c, :], c == 0, c == NCH - 1, [sk] + yk, pk)
                evac(s0 // 512, tt_, P, pk, w_)

    def out_proj(self, W, src, srck, tsl):
        TB = self.TB
        Wv = W.rearrange("(c p) n -> p c n", p=128)
        for s0 in range(0, D, 512):
            sk, wv = self.load_slab(Wv[:, :, s0:s0 + 512], 8, 512)
            for m in range(4):
                pi = self.pi
                self.pi += 1
                P = self.ps[1 + (pi % 2)]
                pk = self.psk(1 + (pi % 2))
                for c in range(NCH):
                    self.mm(P[:, 0:TB], wv[:, c, m * 128:(m + 1) * 128], src[:, c, :], c == 0, c == NCH - 1, [sk, srck(c)], pk)
                mo = s0 // 128 + m
                self.tt("dve", self.hT[:, mo, tsl], P[:, 0:TB], self.hT[:, mo, tsl], ALU.add, pk + [("h", mo)], [("h", mo)])

    def mlp_layer(self, l):
        TB = self.TB
        hid = self.big[:, 0:32 * TB].rearrange("p (c t) -> p c t", c=32)
        hk = lambda c: self.bigk(c * TB * 2, (c + 1) * TB * 2)
        for b in range(self.T // TB):
            tsl = slice(b * TB, (b + 1) * TB)
            self.rmsnorm_block(b, CV_OFF["norm_mlp"] + 8 * l)

            def evac(j, P, pk):
                tf = self.tmpf if (j % 2 == 0) else self.tmpf2
                tk = "tmpf" if (j % 2 == 0) else "tmpf2"
                self.actv(tf[:, 0:TB], P[:, 0:TB], AF.Relu, pk, [tk])
                self.tt("pool", hid[:, j, :], tf[:, 0:TB], tf[:, 0:TB], ALU.mult, [tk], hk(j))
            self.proj_fm(self.mlp_w_up[l], 0, 4 * D, evac)
            Wd = self.mlp_w_down[l].rearrange("(c p) n -> p c n", p=128)
            for m in range(NCH):
                sk, wv = self.load_slab(Wd[:, :, m * 128:(m + 1) * 128], 32, 128)
                pi = self.pi
                self.pi += 1
                P = self.ps[1 + (pi % 2)]
                pk = self.psk(1 + (pi % 2))
                for kc in range(32):
                    self.mm(P[:, 0:TB], wv[:, kc, :], hid[:, kc, :], kc == 0, kc == 31, [sk] + hk(kc), pk)
                self.tt("dve", self.hT[:, m, tsl], P[:, 0:TB], self.hT[:, m, tsl], ALU.add, pk + [("h", m)], [("h", m)])

    def mxfk(self, hp, c0, c1):
        bounds = [0, 128, 256, 896, 1024]
        return [("mxf", hp, q) for q in range(4) if c0 < bounds[q + 1] and c1 > bounds[q]]

    def mbk(self, hp, s0, s1=None):
        s1 = s0 + 1 if s1 is None else s1
        return [("mxb", hp, s) for s in range(s0, s1)]

    def kdk(self, c0, c1):
        return [("kdt", q) for q in range(c0 // 128, (c1 + 127) // 128)]

    def hgrn_layer(self, l):
        TB, NCI = self.TB, self.NCI
        j = l // 2
        W = self.hgrn_w_in[j]
        mb = self.mxb
        lg = self.big[:, 0:8192].bitcast(F32)
        lgk = self.bigk(0, 16384)
        self.dma("sp", lg, self.lb_d[:, :], [], lgk, "lb")
        self.actv(lg, lg, AF.Exp, lgk, lgk)
        t1 = self.mxf[:, 0, :]
        t2 = self.mxf[:, 1, :]
        k1 = self.mxfk(0, 0, 1024)
        k2 = self.mxfk(1, 0, 1024)
        self.tt("dve", t1, lg[:, 0:1024], lg[:, 1024:2048], ALU.add, lgk, k1)
        self.tt("dve", t1, t1, lg[:, 2048:3072], ALU.add, lgk + k1, k1)
        self.tt("dve", t1, t1, lg[:, 3072:4096], ALU.add, lgk + k1, k1)
        self.op("dve", "reciprocal", k1, k1, out=t1, in_=t1)
        if l == 1:
            self.op("dve", "tensor_copy", lgk, k2, out=t2, in_=lg[:, 1024:2048])
        else:
            self.tt("dve", t2, lg[:, 1024:2048], lg[:, 2048:3072], ALU.add, lgk, k2)
            self.tt("dve", t2, t2, lg[:, 3072:4096], ALU.add, lgk + k2, k2)
        LB = self.lbt[:, 0, :]
        OML = self.lbt[:, 1, :]
        self.tt("dve", LB, t2, t1, ALU.mult, k1 + k2, ["lbt"])
        self.ts("dve", OML, LB, -1.0, 1.0, ALU.mult, ALU.add, ["lbt"], ["oml"])
        self.op("dve", "memset", [], [("Sst", h) for h in range(H)], self.Sst[:], 0.0)
        self.op("pool", "memset", [], [("Sbf", h) for h in range(H)], self.Sbf[:], 0.0)
        gbytes = NCI * 4096
        G = self.big[:, 0:gbytes // 2].bitcast(F32).rearrange("p (c n) -> p c n", c=NCI)
        V = self.big[:, gbytes // 2:gbytes // 2 + NCI * 1024].rearrange("p (c n) -> p c n", c=NCI)
        q0 = gbytes + NCI * 2048
        qT = self.big[:, q0 // 2:q0 // 2 + H * TB].rearrange("p (h t) -> p h t", h=H)
        Gk = lambda ci, half: self.bigk(ci * 4096 + half * 2048, ci * 4096 + (half + 1) * 2048)
        Vk = lambda ci, half: self.bigk(gbytes + ci * 2048 + half * 1024, gbytes + ci * 2048 + (half + 1) * 1024)
        qk = lambda h: self.bigk(q0 + h * TB * 2, q0 + (h + 1) * TB * 2)
        qsc = float(128 ** -0.5)
        for b in range(self.T // TB):
            tsl = slice(b * TB, (b + 1) * TB)
            self.rmsnorm_block(b, CV_OFF["norm_mix"] + 8 * l)

            def evac_q(jj, P, pk):
                self.actv(qT[:, jj, :], P[:, 0:TB], AF.Silu, pk, qk(jj))
                self.ts("pool", qT[:, jj, :], qT[:, jj, :], qsc, None, ALU.mult, ALU.bypass, qk(jj), qk(jj))
            self.proj_fm(W, 0, 1024, evac_q)

            def evac_f(si, tt_, P, pk, w_):
                cs = slice(si * 512, (si + 1) * 512)
                a = self.tmpf[:, 0:512]
                self.actv(a, P[:, 0:512], AF.Exp, pk, ["tmpf"], scale=-1.0)
                self.ts("dve", a, a, 1.0, None, ALU.add, ALU.bypass, ["tmpf"], ["tmpf"])
                self.op("dve", "reciprocal", ["tmpf"], ["tmpf"], out=a, in_=a)
                self.tt("dve", a, a, OML[:, cs], ALU.mult, ["tmpf", "oml"], ["tmpf"])
                self.tt("dve", a, a, LB[:, cs], ALU.add, ["tmpf", "lbt"], ["tmpf"])
                self.actv(G[:, tt_, cs], a, AF.Ln, ["tmpf"], Gk(tt_, si))
            self.proj_tm(W, 1024, 1024, evac_f)
            self.proj_tm(W, 2048, 1024, lambda si, tt_, P, pk, w_: self.actv(V[:, tt_, si * 512:(si + 1) * 512], P[:, 0:512], AF.Copy, pk, Vk(tt_, si)))
            self.proj_fm(W, 3072, 1024, lambda jj, P, pk: self.actv(self.gate[:, jj, :], P[:, 0:TB], AF.Silu, pk, [("gate", jj)]))
            for ci in range(NCI):
                csl = slice(ci * 128, (ci + 1) * 128)
                for half in range(2):
                    hs = slice(half * 512, (half + 1) * 512)
                    P = self.ps[3]
                    p3 = self.psk(3)
                    self.mm(P[:, :], self.cfa("SUP"), G[:, ci, hs], True, True, Gk(ci, half) + ["cf"], p3)
                    e1 = self.tmpf[:, 0:512]
                    e2 = self.tmpf2[:, 0:512]
                    self.actv(e1, G[:, ci, hs], AF.Exp, Gk(ci, half), ["tmpf"])
                    self.actv(e2, P[:, :], AF.Exp, p3, ["tmpf2"])
                    self.ts("dve", e1, e1, -1.0, 1.0, ALU.mult, ALU.add, ["tmpf"], ["tmpf"])
                    self.tt("dve", self.kdt[:, hs], e1, e2, ALU.mult, ["tmpf", "tmpf2"], self.kdk(half * 512, (half + 1) * 512))
                def head(h, ci=ci, csl=csl):
                    hp = h % 2
                    hsl = slice(h * 128, (h + 1) * 128)
                    PE0, PE1 = self.ps[4 + 2 * hp], self.ps[5 + 2 * hp]
                    pk0, pk1 = self.psk(4 + 2 * hp), self.psk(5 + 2 * hp, 0, 384)
                    RH = self.cfa("RH", 896)
                    self.mm(PE0[:, :], G[:, ci, hsl], RH[:, 0:512], True, True, Gk(ci, h // 4) + ["cf"], pk0)
                    self.mm(PE1[:, 0:384], G[:, ci, hsl], RH[:, 512:896], True, True, Gk(ci, h // 4) + ["cf"], pk1)
                    E = self.mxf[:, hp, 0:896]
                    ek = self.mxfk(hp, 0, 896)
                    self.actv(E[:, 0:512], PE0[:, :], AF.Exp, pk0, self.mxfk(hp, 0, 512))
                    self.actv(E[:, 512:896], PE1[:, 0:384], AF.Exp, pk1, self.mxfk(hp, 512, 896))
                    kTf = self.mxf[:, hp, 896:1024]
                    kfk = self.mxfk(hp, 896, 1024)
                    self.ts("dve", kTf, E[:, 768:896], -1.0, 1.0, ALU.mult, ALU.add, ek, kfk)
                    KT4 = mb[:, hp, 0:4, :]
                    self.tt("dve", KT4, E[:, 0:512].rearrange("p (r n) -> p r n", r=4),
                            kTf.unsqueeze(1).to_broadcast([128, 4, 128]), ALU.mult, ek + kfk, self.mbk(hp, 0, 4))
                    QQ = mb[:, hp, 4:6, :]
                    self.tt("dve", QQ, E[:, 512:768].rearrange("p (r n) -> p r n", r=2),
                            qT[:, h, csl].unsqueeze(1).to_broadcast([128, 2, 128]), ALU.mult, ek + qk(h), self.mbk(hp, 4, 6))
                    PA = self.ps[1 + hp]
                    pa = self.psk(1 + hp)
                    for r4 in range(4):
                        self.mm(PA[:, r4 * 32:(r4 + 1) * 32], KT4[:, r4, :], QQ[:, 0, r4 * 32:(r4 + 1) * 32], True, True,
                                self.mbk(hp, 0, 6), pa)
                    AT = mb[:, hp, 6, :]
                    self.tt("dve", AT, PA[:, 0:128], self.cba("MASKT"), ALU.mult, pa + ["cb"], self.mbk(hp, 6))
                    PO = self.ps[0 if hp == 0 else 3]
                    po0 = po1 = self.psk(0 if hp == 0 else 3)
                    self.mm(PO[:, 0:128], V[:, ci, hsl], AT, True, False, Vk(ci, h // 4) + self.mbk(hp, 6), po0)
                    self.mm(PO[:, 0:128], self.Sbf[:, h, :], QQ[:, 1, :], False, True, [("Sbf", h)] + self.mbk(hp, 5), po0)
                    self.actv(self.OT[:, h, csl], PO[:, 0:128], AF.Copy, po0, [("OT", h)])
                    self.mm(PO[:, 128:256], self.kdt[:, hsl], V[:, ci, hsl], True, True, self.kdk(h * 128, (h + 1) * 128) + Vk(ci, h // 4), po1)
                    self.stt(self.Sst[:, h, :], self.Sst[:, h, :], E[:, 767:768], PO[:, 128:256], ALU.mult, ALU.add,
                             [("Sst", h)] + ek + po1, [("Sst", h)])
                    self.actv(self.Sbf[:, h, :], self.Sst[:, h, :], AF.Copy, [("Sst", h)], [("Sbf", h)])

                for h0 in range(0, H, 2):
                    self.S.interleave([lambda h0=h0: head(h0), lambda h0=h0: head(h0 + 1)])
            self.sumsq_rstd([self.OT[:, h, :] for h in range(H)], [("OT", h) for h in range(H)], TB, 1.0 / D, EPS)
            self.actv(self.rstd[:, 0:TB], self.lnt[:, 0:TB], AF.Exp, ["lnt"], ["rstd"], scale=-0.5)
            go = CV_OFF["gnorm"] + 8 * j
            for h in range(H):
                self.stt(self.OT[:, h, :], self.OT[:, h, :], self.cv[:, go + h:go + h + 1], self.rstd[:, 0:TB], ALU.mult, ALU.mult,
                         [("OT", h), "cv", "rstd"], [("OT", h)])
                self.tt("pool", self.OT[:, h, :], self.OT[:, h, :], self.gate[:, h, :], ALU.mult, [("OT", h), ("gate", h)], [("OT", h)])
            self.out_proj(self.hgrn_w_out[j], self.OT, lambda c: ("OT", c), tsl)

    def gdn_layer(self, l):
        TB, NCI = self.TB, self.NCI
        j = l // 2
        W = self.gdn_w_in[j]
        QKV = self.big[:, 0:24 * TB].rearrange("p (c t) -> p c t", c=24)
        ck = lambda c: self.bigk(c * TB * 2, (c + 1) * TB * 2)
        nega = self.smallf[:, 0, 0:8]
        self.actv(nega, self.rw[:, 8 * j:8 * j + 8], AF.Exp, ["rw"], ["nega"])
        self.ts("dve", nega, nega, -1.0, None, ALU.mult, ALU.bypass, ["nega"], ["nega"])
        self.op("dve", "memset", [], [("Sst", h) for h in range(H)], self.Sst[:], 0.0)
        self.op("pool", "memset", [], [("Sbf", h) for h in range(H)], self.Sbf[:], 0.0)
        self.op("pool", "memset", [], [("halo", c) for c in range(24)], self.halo[:], 0.0)
        cvo = CV_OFF["conv"] + j * 96
        for b in range(self.T // TB):
            tsl = slice(b * TB, (b + 1) * TB)
            self.rmsnorm_block(b, CV_OFF["norm_mix"] + 8 * l)

            def evac_qkv(jj, P, pk):
                pp = 0
                slot = jj % 4
                pv = self.pre[:, pp, slot, :]
                pkk = ("pre", pp, slot)
                self.actv(pv[:, 3:3 + TB], P[:, 0:TB], AF.Copy, pk, [pkk])
                self.op("pool", "tensor_copy", [("halo", jj)], [pkk], out=pv[:, 0:3], in_=self.halo[:, jj, :])
                acc = self.tmpf[:, 0:TB] if jj % 2 == 0 else self.tmpf2[:, 0:TB]
                ak = "tmpf" if jj % 2 == 0 else "tmpf2"
                wc = cvo + jj * 4
                self.ts("dve", acc, pv[:, 0:TB], self.cv[:, wc:wc + 1], None, ALU.mult, ALU.bypass, [pkk, "cv"], [ak])
                for tap in range(1, 4):
                    self.stt(acc, pv[:, tap:tap + TB], self.cv[:, wc + tap:wc + tap + 1], acc, ALU.mult, ALU.add, [pkk, "cv", ak], [ak])
                self.op("pool", "tensor_copy", [pkk], [("halo", jj)], out=self.halo[:, jj, :], in_=pv[:, TB:TB + 3])
                self.actv(QKV[:, jj, :], acc, AF.Silu, [ak], ck(jj))
            self.proj_fm(W, 0, 3072, evac_qkv)
            self.proj_fm(W, 3072, 1024, lambda jj, P, pk: self.actv(self.gate[:, jj, :], P[:, 0:TB], AF.Silu, pk, [("gate", jj)]))
            if self.dbg_stop == 1:
                continue

            def evac_ab(si, tt_, P, pk, w_):
                x1 = self.smallf[:, 1, 0:16]
                self.tt("dve", x1[:, 0:8], P[:, 0:8], self.rw[:, 16 + 8 * j:24 + 8 * j], ALU.add, pk + ["rw"], ["x1"])
                self.ts("dve", x1[:, 8:16], P[:, 8:16], -1.0, None, ALU.mult, ALU.bypass, pk, ["x1b"])
                self.actv(x1, x1, AF.Exp, ["x1", "x1b"], ["x1", "x1b"])
                self.actv(x1, x1, AF.Ln, ["x1", "x1b"], ["x1", "x1b"], bias=1.0)
                self.tt("dve", self.Gs[:, tt_, 0:8], x1[:, 0:8], nega, ALU.mult, ["x1", "nega"], [("Gs", tt_)])
                self.ts("dve", self.Gs[:, tt_, 8:16], x1[:, 8:16], -1.0, None, ALU.mult, ALU.bypass, ["x1b"], [("Gsb", tt_)])
            self.proj_tm(W, 4096, 16, evac_ab)
            if self.dbg_stop == 2:
                continue
            for c in range(16):
                self.sumsq_rstd([QKV[:, c, :]], [ck(c)[0]], TB, 1.0, EPS, pk=0)
                self.actv(self.rstd[:, 0:TB], self.lnt[:, 0:TB], AF.Exp, ["lnt"], ["rstd"], scale=-0.5)
                if c < 8:
                    self.stt(QKV[:, c, :], QKV[:, c, :], float(128 ** -0.5), self.rstd[:, 0:TB], ALU.mult, ALU.mult, ck(c) + ["rstd"], ck(c))
                else:
                    self.tt("dve", QKV[:, c, :], QKV[:, c, :], self.rstd[:, 0:TB], ALU.mult, ck(c) + ["rstd"], ck(c))
            if self.dbg_stop == 3:
                continue
            for ci in range(NCI):
                self.gdn_chunk(ci, QKV, ck)
            if self.dbg_stop >= 4:
                continue
            oo = CV_OFF["onorm"] + j
            for h in range(H):
                self.sumsq_rstd([self.OT[:, h, :]], [("OT", h)], TB, 1.0 / 128, EPS, pk=0)
                self.actv(self.rstd[:, 0:TB], self.lnt[:, 0:TB], AF.Exp, ["lnt"], ["rstd"], scale=-0.5)
                self.stt(self.OT[:, h, :], self.OT[:, h, :], self.cv[:, oo:oo + 1], self.rstd[:, 0:TB], ALU.mult, ALU.mult,
                         [("OT", h), "cv", "rstd"], [("OT", h)])
                self.tt("pool", self.OT[:, h, :], self.OT[:, h, :], self.gate[:, h, :], ALU.mult, [("OT", h), ("gate", h)], [("OT", h)])
            self.out_proj(self.gdn_w_out[j], self.OT, lambda c: ("OT", c), tsl)

    def gdn_chunk(self, ci, QKV, ck):
        mb = self.mxb
        csl = slice(ci * 128, (ci + 1) * 128)
        g = self.Gs[:, ci, 0:8]
        lnb = self.Gs[:, ci, 8:16]
        gk, lk = ("Gs", ci), ("Gsb", ci)
        PSm = self.ps[3]
        p3a = self.psk(3, 0, 128)
        self.mm(PSm[:, 0:8], self.cfa("LT"), g, True, False, [gk, "cf"], p3a)
        self.mm(PSm[:, 0:8], self.cfa("ID"), lnb, False, True, [lk, "cf"], p3a)
        self.mm(PSm[:, 8:16], self.cfa("SUP"), g, True, True, [gk, "cf"], p3a)
        self.mm(PSm[:, 16:24], self.cfa("ONES"), g, True, True, [gk, "cf"], p3a)
        self.mm(PSm[:, 24:32], self.cfa("ID"), lnb, True, True, [lk, "cf"], p3a)
        EG = self.smallf[:, 2, 0:32]
        self.actv(EG, PSm[:, 0:32], AF.Exp, p3a, ["EG"])
        negb = self.smallf[:, 3, 0:8]
        self.ts("dve", negb, EG[:, 24:32], -1.0, None, ALU.mult, ALU.bypass, ["EG"], ["negb"])
        if self.dbg_stop == 4:
            return
        nr, ys = self.nr, self.ysb
        def head(h):
            hp = h % 2
            qTh, kTh, vTh = QKV[:, h, csl], QKV[:, 8 + h, csl], QKV[:, 16 + h, csl]
            qk_, kk_, vk_ = ck(h), ck(8 + h), ck(16 + h)
            b3 = 0 if hp == 0 else 3
            PT = self.ps[b3][:, :].bitcast(BF16)
            p3b = p3c = p3d = self.psk(b3)
            Gb = self.mxf[:, hp, 0:128]
            Gb2 = self.mxf[:, hp, 128:256]
            gbk, gb2k = self.mxfk(hp, 0, 128), self.mxfk(hp, 128, 256)
            self.ts("dve", Gb, self.cfa("SUP"), g[:, h:h + 1], None, ALU.mult, ALU.bypass, ["cf", gk], gbk)
            self.ts("dve", Gb2, self.cfa("LT"), g[:, h:h + 1], None, ALU.mult, ALU.bypass, ["cf", gk], gb2k)
            PD = self.ps[4 + hp]
            pd0, pd1, pd2 = self.psk(4 + hp, 0, 128), self.psk(4 + hp, 128, 256), self.psk(4 + hp, 256, 384)
            self.mm(PD[:, 0:128], self.cfa("LT"), Gb, True, False, ["cf"] + gbk, pd0)
            self.mm(PD[:, 0:128], self.cba("ID"), self.cba("NEGS"), False, True, ["cb"], pd0)
            self.mm(PD[:, 128:256], self.cfa("SUP"), Gb2, True, False, ["cf"] + gb2k, pd1)
            self.mm(PD[:, 128:256], self.cba("ID"), self.cba("NEGCT"), False, True, ["cb"], pd1)
            self.mm(PD[:, 256:384], self.cfa("ONES"), Gb2, True, True, ["cf"] + gb2k, pd2)
            DEC = mb[:, hp, 0:3, :]
            self.actv(DEC[:, 1:3, :], PD[:, 128:384].rearrange("p (a n) -> p a n", a=2), AF.Exp, pd1 + pd2, self.mbk(hp, 1, 3))
            fz = self.mxf[:, hp, :]
            zk = self.mxfk(hp, 256, 896)
            decS, N, No, U64, T64 = fz[:, 256:384], fz[:, 384:512], fz[:, 512:640], fz[:, 640:768], fz[:, 768:896]
            Pm = fz[:, 896:1024]
            pmk = self.mxfk(hp, 896, 1024)
            self.actv(decS, PD[:, 0:128], AF.Exp, pd0, zk)
            PK = self.ps[6 + hp]
            pq0, pq1, pq2, pq3 = (self.psk(6 + hp, 0, 128), self.psk(6 + hp, 128, 256), self.psk(6 + hp, 256, 384),
                                  self.psk(6 + hp, 384, 512))
            self.mm(PK[:, 0:128], kTh, kTh, True, True, kk_, pq0)
            self.mm(PK[:, 128:256], kTh, qTh, True, True, kk_ + qk_, pq1)
            self.stt(N, PK[:, 0:128], negb[:, h:h + 1], decS, ALU.mult, ALU.mult, pq0 + ["negb"] + zk, zk)
            AT = mb[:, hp, 4, :]
            self.tt("dve", AT, PK[:, 128:256], DEC[:, 1, :], ALU.mult, pq1 + self.mbk(hp, 1), self.mbk(hp, 4))
            Nd = nr[:, hp, 0, 0:128]
            self.tt("pool", Nd, N, self.cba("BD"), ALU.mult, zk + ["cb"], [("nrN", hp, 0)])
            self.tt("pool", No, N, self.cba("OD"), ALU.mult, zk + ["cb"], zk)
            P3 = self.ps[b3]
            self.op("pe", "transpose", [("nrN", hp, 0), "cf"], p3b, P3[:, 128:256], Nd, self.cfa("ID"))
            Yd = ys[:, hp, 0, 0:128]
            self.cp(Yd, P3[:, 128:256], p3b, [("ysY", hp, 0)])
            self.mm(PK[:, 256:384], Nd, Yd, True, True, [("nrN", hp, 0), ("ysY", hp, 0)], pq2)
            self.mm(PK[:, 384:512], Yd, Nd, True, True, [("nrN", hp, 0), ("ysY", hp, 0)], pq3)
            self.cp(ys[:, hp, 1, 0:128], PK[:, 256:384], pq2, [("ysY", hp, 1)])
            self.tt("pool", ys[:, hp, 1, 128:256], Yd, self.cfa("ID"), ALU.add, [("ysY", hp, 0), "cf"], [("ysS", hp, 1)])
            self.cp(nr[:, hp, 1, 0:128], PK[:, 384:512], pq3, [("nrN", hp, 1)])
            self.tt("pool", nr[:, hp, 1, 128:256], Nd, self.cfa("ID"), ALU.add, [("nrN", hp, 0), "cf"], [("nrR", hp, 1)])
            cur = 1
            PA_, PB_ = self.ps[4 + hp], self.ps[6 + hp]
            pa01, pb01 = pd0 + pd1, pq0 + pq1
            for lev in range(1, 5):
                nx = 1 - cur
                rk = [("nrN", hp, cur), ("nrR", hp, cur), ("ysY", hp, cur), ("ysS", hp, cur)]
                self.mm(PA_[:, 0:256], nr[:, hp, cur, 0:128], ys[:, hp, cur, :], True, True, rk, pa01)
                self.mm(PB_[:, 0:256], ys[:, hp, cur, 0:128], nr[:, hp, cur, :], True, True, rk, pb01)
                self.cp(ys[:, hp, nx, 0:128], PA_[:, 0:128], pd0, [("ysY", hp, nx)])
                self.tt("dve", ys[:, hp, nx, 128:256], PA_[:, 128:256], ys[:, hp, cur, 128:256], ALU.add, pd1 + [("ysS", hp, cur)], [("ysS", hp, nx)])
                self.cp(nr[:, hp, nx, 0:128], PB_[:, 0:128], pq0, [("nrN", hp, nx)])
                self.tt("dve", nr[:, hp, nx, 128:256], PB_[:, 128:256], nr[:, hp, cur, 128:256], ALU.add, pq1 + [("nrR", hp, cur)], [("nrR", hp, nx)])
                cur = nx
            rk = [("nrN", hp, cur), ("nrR", hp, cur), ("ysY", hp, cur), ("ysS", hp, cur)]
            self.mm(PA_[:, 0:128], nr[:, hp, cur, 0:128], ys[:, hp, cur, 128:256], True, True, rk, pd0)
            self.mm(PB_[:, 0:128], ys[:, hp, cur, 0:128], nr[:, hp, cur, 128:256], True, True, rk, pq0)
            self.tt("dve", U64, PA_[:, 0:128], ys[:, hp, cur, 128:256], ALU.add, pd0 + [("ysS", hp, cur)], zk)
            self.tt("dve", T64, PB_[:, 0:128], nr[:, hp, cur, 128:256], ALU.add, pq0 + [("nrR", hp, cur)], zk)
            self.mm(PA_[:, 0:128], No, U64, True, True, zk, pd0)
            self.cp(Pm, PA_[:, 0:128], pd0, pmk)
            self.mm(PB_[:, 0:128], T64, Pm, True, True, zk + pmk, pq0)
            U = mb[:, hp, 9, :]
            self.tt("dve", U, PB_[:, 0:128], U64, ALU.add, pq0 + zk, self.mbk(hp, 9))
            if self.dbg_stop == 8:
                return
            self.tr(PT[:, 512:640], kTh, kk_, p3c)
            kbgn = mb[:, hp, 10, :]
            kdec = mb[:, hp, 11, :]
            self.ts("dve", kbgn, PT[:, 512:640], EG[:, h:h + 1], -1.0, ALU.mult, ALU.mult, p3c + ["EG"], self.mbk(hp, 10))
            self.ts("dve", kdec, PT[:, 512:640], EG[:, 8 + h:9 + h], None, ALU.mult, ALU.bypass, p3c + ["EG"], self.mbk(hp, 11))
            self.tr(PT[:, 768:896], vTh, vk_, p3d)
            c_vb, c_w, c_vn, c_qd = hp * 128, 256 + hp * 128, 512 + hp * 128, 768 + hp * 128
            vb = self.kdt[:, c_vb:c_vb + 128]
            self.ts("dve", vb, PT[:, 768:896], EG[:, 24 + h:25 + h], None, ALU.mult, ALU.bypass, p3d + ["EG"], self.kdk(c_vb, c_vb + 128))
            if self.dbg_stop == 9:
                return
            PO = self.ps[1 + hp]
            po = [self.psk(1 + hp) for q in range(4)]
            self.mm(PO[:, 0:128], kbgn, U, True, True, self.mbk(hp, 10) + self.mbk(hp, 9), po[0])
            wTn = self.kdt[:, c_w:c_w + 128]
            self.acp(wTn, PO[:, 0:128], po[0], self.kdk(c_w, c_w + 128))
            self.mm(PO[:, 128:256], U, vb, True, False, self.mbk(hp, 9) + self.kdk(c_vb, c_vb + 128), po[1])
            self.mm(PO[:, 128:256], wTn, self.Sbf[:, h, :], False, True, self.kdk(c_w, c_w + 128) + [("Sbf", h)], po[1])
            vnew = self.kdt[:, c_vn:c_vn + 128]
            vnk = self.kdk(c_vn, c_vn + 128)
            self.acp(vnew, PO[:, 128:256], po[1], vnk)
            qd = self.kdt[:, c_qd:c_qd + 128]
            qdk = self.kdk(c_qd, c_qd + 128)
            self.tt("pool", qd, qTh, DEC[:, 2, :], ALU.mult, qk_ + self.mbk(hp, 2), qdk)
            self.mm(PO[:, 256:384], self.Sbf[:, h, :], qd, True, False, [("Sbf", h)] + qdk, po[2])
            self.mm(PO[:, 256:384], vnew, AT, False, True, vnk + self.mbk(hp, 4), po[2])
            self.acp(self.OT[:, h, csl], PO[:, 256:384], po[2], [("OT", h)])
            self.mm(PO[:, 384:512], kdec, vnew, True, True, self.mbk(hp, 11) + vnk, po[3])
            self.stt(self.Sst[:, h, :], self.Sst[:, h, :], EG[:, 16 + h:17 + h], PO[:, 384:512], ALU.mult, ALU.add,
                     [("Sst", h), "EG"] + po[3], [("Sst", h)])
            self.acp(self.Sbf[:, h, :], self.Sst[:, h, :], [("Sst", h)], [("Sbf", h)])

        for h0 in range(0, H, 2):
            self.S.interleave([lambda h0=h0: head(h0), lambda h0=h0: head(h0 + 1)])


def make_in_map(inp, hT, prog):
    m = {"hT_in": np.ascontiguousarray(hT, dtype=np.float32), "cf": _consts(), "cvec": _make_cvec(inp), "rows": _make_rows(inp)}
    need_g, need_h = prog.need_g, prog.need_h
    if need_g:
        m["gdn_w_in"] = inp["gdn_w_in"]
        m["gdn_w_out"] = inp["gdn_w_out"]
    if need_h:
        m["hgrn_w_in"] = inp["hgrn_w_in"]
        m["hgrn_w_out"] = inp["hgrn_w_out"]
        m["lbrows"] = _make_lbrows(inp)
    if prog.layers:
        m["mlp_w_up"] = inp["mlp_w_up"]
        m["mlp_w_down"] = inp["mlp_w_down"]
    return m


N_CORES = 8
_CFG = {"T": 2048, "TB": 512, "nseq": 2, "groups": [[0, 1, 2, 3]], "skip_mixer": False}


def kernel(**inp):
    inp = {k: np.asarray(v) for k, v in inp.items()}
    x = inp["x"]
    B, T, Dm = x.shape
    nseq = B // N_CORES
    hT = [np.ascontiguousarray(x[c * nseq:(c + 1) * nseq].reshape(nseq * T, Dm).T) for c in range(N_CORES)]
    groups = _CFG["groups"]
    for gi, layers in enumerate(groups):
        p = Prog(nseq, T, _CFG["TB"], layers, final_norm=(gi == len(groups) - 1))
        p.skip_mixer = _CFG["skip_mixer"]
        nc = p.build()
        base = make_in_map(inp, hT[0], p)
        in_maps = []
        for c in range(N_CORES):
            m = dict(base)
            m["hT_in"] = hT[c]
            in_maps.append(m)
        res = run_bass_kernel_spmd(nc, in_maps, core_ids=list(range(N_CORES)))
        hT = [np.ascontiguousarray(r["hT_out"]) for r in res.results]
    out = np.stack([h.T.reshape(nseq, T, Dm) for h in hT], axis=0).reshape(B, T, Dm)
    return out.astype(np.float32)
```

```python
import numpy as np
from contextlib import ExitStack
import concourse.bass as bass
import concourse.mybir as mybir
from concourse.bass_utils import run_bass_kernel_spmd

F32 = mybir.dt.float32
BF16 = mybir.dt.bfloat16
AF = mybir.ActivationFunctionType
ALU = mybir.AluOpType

D = 1024
NCH = 8
H = 8
C = 128
EPS = 1e-6
GDN_IN = 4112
MAXV = 30000


class _Op:
    __slots__ = ("eng", "fn", "deps", "kind", "stream", "sig", "upg")

    def __init__(self, eng, fn, deps, kind, stream):
        self.eng, self.fn, self.deps, self.kind, self.stream, self.sig = eng, fn, deps, kind, stream, None


class Sched:
    ENGS = ("pe", "act", "dve", "pool", "sp")

    def __init__(self):
        self.ops = []
        self.lastw = {}
        self.readers = {}
        self.stream_last = {}
        self.recording = None

    def interleave(self, bodies):
        lists = []
        for b in bodies:
            self.recording = []
            b()
            lists.append(self.recording)
            self.recording = None
        n = max(len(l) for l in lists)
        for i in range(n):
            for l in lists:
                if i < len(l):
                    self.add(*l[i])

    def add(self, eng, fn, r=(), w=(), kind="c", stream=None):
        if self.recording is not None:
            self.recording.append((eng, fn, tuple(r), tuple(w), kind, stream))
            return -1
        i = len(self.ops)
        deps = {}

        def upd(d, t):
            o = deps.get(d)
            if o is None or (t == "raw") or (t == "waw" and o == "war"):
                deps[d] = t
        for k in r:
            if k in self.lastw:
                upd(self.lastw[k], "raw")
        for k in w:
            if k in self.lastw:
                upd(self.lastw[k], "waw")
            for d in self.readers.get(k, ()):
                if d != i:
                    upd(d, "war")
        op = _Op(eng, fn, deps, kind, stream)
        op.upg = dict(self.stream_last)
        self.ops.append(op)
        if kind == "d":
            self.stream_last[stream] = i
        for k in r:
            self.readers.setdefault(k, []).append(i)
        for k in w:
            self.lastw[k] = i
            self.readers[k] = []
        return i

    def dma(self, eng, fn, r, w, stream):
        return self.add(eng, fn, r, w, kind="d", stream=stream)

    def _needs_sync(self, op, D, t):
        if D.kind == "d":
            return True
        if D.eng != op.eng:
            return True
        if op.kind == "d":
            return True
        if op.eng == "pe":
            return False
        return True

    def emit(self, nc, es):
        ops = self.ops
        need = [False] * len(ops)
        for op in ops:
            for d, t in op.deps.items():
                if self._needs_sync(op, ops[d], t):
                    need[d] = True
        cnt = {}
        semkeys = []

        def nxt(key, inc):
            st = cnt.get(key)
            if st is None:
                st = cnt[key] = [0, 0]
                semkeys.append((key, 0))
            if st[1] + inc > MAXV:
                st[0] += 1
                st[1] = 0
                semkeys.append((key, st[0]))
            st[1] += inc
            return (key, st[0]), st[1]
        for i, op in enumerate(ops):
            if op.kind == "d":
                op.sig = nxt(("dma", op.stream), 16)
            elif need[i]:
                op.sig = nxt(("eng", op.eng), 1)
        sems = {}
        for sk in semkeys:
            nm = "s_%s_%s_%d" % (sk[0][0], str(sk[0][1]), sk[1])
            sems[sk] = es.enter_context(nc.semaphore(nm))
        self.nsems = len(sems)
        block = es.enter_context(nc.Block())
        per = {e: [] for e in self.ENGS}
        for i, op in enumerate(ops):
            per[op.eng].append(i)

        def run(eng_name, engine):
            waited = {}
            for i in per[eng_name]:
                op = ops[i]
                for d, t in sorted(op.deps.items()):
                    Dop = ops[d]
                    if not self._needs_sync(op, Dop, t):
                        continue
                    if Dop.kind == "d":
                        Dop = ops[op.upg[Dop.stream]]
                    sk, val = Dop.sig
                    if waited.get(sk, 0) >= val:
                        continue
                    engine.wait_ge(sems[sk], val)
                    waited[sk] = val
                ins = op.fn(engine)
                if op.sig is not None:
                    assert ins is not None
                    ins.then_inc(sems[op.sig[0]], 16 if op.kind == "d" else 1)

        @block.tensor
        def _(e):
            run("pe", e)

        @block.scalar
        def _(e):
            run("act", e)

        @block.vector
        def _(e):
            run("dve", e)

        @block.gpsimd
        def _(e):
            run("pool", e)

        @block.sync
        def _(e):
            run("sp", e)


def _consts():
    k = np.arange(128)[:, None]
    j = np.arange(128)[None, :]
    f = np.float32
    LT = (k <= j).astype(f)
    SUP = (k > j).astype(f)
    ID = (k == j).astype(f)
    ONES = np.ones((128, 128), f)
    NEG = -30000.0
    NEGS = np.where(k > j, 0.0, NEG).astype(f)
    NEGCT = np.where(j >= k, 0.0, NEG).astype(f)
    BD = ((k // 64) == (j // 64)).astype(f)
    OD = 1.0 - BD
    MASKT = (k <= j).astype(f)
    Ms = []
    for r in range(4):
        lo, hi = 32 * r, 32 * (r + 1)
        M = np.zeros((128, 128), f)
        M += ((j < lo) & (k > j) & (k < lo)) * 1.0
        M -= ((j >= lo) & (j < hi) & (k >= lo) & (k <= j)) * 1.0
        Ms.append(M.astype(f))
    MQ = ((k >= (j // 32) * 32) & (k <= j)).astype(f)
    RH = np.concatenate(Ms + [MQ, LT, ID], axis=1)
    cf = np.concatenate([LT, SUP, ID, ONES, RH, ID, ONES, NEGS, NEGCT, BD, OD, MASKT], axis=1)
    return np.ascontiguousarray(cf.astype(f))


CF_OFF = {"LT": 0, "SUP": 128, "ID": 256, "ONES": 384, "RH": 512}
CF_W = 512 + 896
CB_OFF = {"ID": 0, "ONES": 128, "NEGS": 256, "NEGCT": 384, "BD": 512, "OD": 640, "MASKT": 768}
CB_W = 896


def _cvec_layout():
    off = {}
    n = 0
    for nm, w in (("norm_mix", 32), ("norm_mlp", 32), ("norm_final", 8), ("conv", 2 * 24 * 4),
                  ("onorm", 2), ("gnorm", 16)):
        off[nm] = n
        n += w
    return off, n


CV_OFF, CV_W = _cvec_layout()


def _make_cvec(inp):
    cv = np.zeros((128, CV_W), np.float32)
    cv[:, CV_OFF["norm_mix"]:CV_OFF["norm_mix"] + 32] = inp["norm_mix"].reshape(4, 8, 128).transpose(2, 0, 1).reshape(128, 32)
    cv[:, CV_OFF["norm_mlp"]:CV_OFF["norm_mlp"] + 32] = inp["norm_mlp"].reshape(4, 8, 128).transpose(2, 0, 1).reshape(128, 32)
    cv[:, CV_OFF["norm_final"]:CV_OFF["norm_final"] + 8] = inp["norm_final"].reshape(8, 128).T
    cw = inp["gdn_conv"].reshape(2, 4, 24, 128).transpose(3, 0, 2, 1).reshape(128, 2 * 24 * 4)
    cv[:, CV_OFF["conv"]:CV_OFF["conv"] + 192] = cw
    cv[:, CV_OFF["onorm"]:CV_OFF["onorm"] + 2] = inp["gdn_onorm"].T
    cv[:, CV_OFF["gnorm"]:CV_OFF["gnorm"] + 16] = inp["hgrn_gnorm"].reshape(2, 8, 128).transpose(2, 0, 1).reshape(128, 16)
    return cv


RW_W = 32


def _make_rows(inp):
    r = np.zeros((RW_W,), np.float32)
    r[0:16] = inp["gdn_a_log"].reshape(-1)
    r[16:32] = inp["gdn_dt_bias"].reshape(-1)
    return np.ascontiguousarray(np.broadcast_to(r[None, :], (128, RW_W)))


def _make_lbrows(inp):
    r = inp["hgrn_lb_logits"].reshape(-1).astype(np.float32)
    return np.ascontiguousarray(np.broadcast_to(r[None, :], (128, 4096)))


class Prog:
    def __init__(self, nseq, T, TB, layers, final_norm):
        self.nseq, self.T, self.TB, self.layers, self.final_norm = nseq, T, TB, list(layers), final_norm
        self.NTOK = nseq * T
        self.NCI = TB // 128
        self.S = Sched()
        self.nc = bass.Bass("TRN2", target_bir_lowering=False)
        self.slab_i = 0
        self.pi = 0
        self.skip_mixer = False
        self.dbg_stop = 0
        self.dbg2 = 9
        self.dbg3 = 0

    def dram_in(self, name, shape, dt=F32):
        return self.nc.dram_tensor(name, list(shape), dt, kind="ExternalInput").ap()

    def sb(self, name, shape, dt):
        return self.es.enter_context(self.nc.sbuf_tensor(name, list(shape), dt))

    def op(self, eng, meth, r, w, *a, **kw):
        return self.S.add(eng, lambda e: getattr(e, meth)(*a, **kw), r, w)

    def mm(self, out, lhsT, rhs, start, stop, r, w):
        return self.op("pe", "matmul", r, w, out, lhsT=lhsT, rhs=rhs, start=start, stop=stop)

    def tr(self, out, in_, r, w):
        return self.op("pe", "transpose", r + ["cb"], w, out, in_, self.cba("ID"))

    def actv(self, out, in_, func, r, w, **kw):
        return self.op("act", "activation", r, w, out=out, in_=in_, func=func, **kw)

    def cp(self, out, in_, r, w):
        return self.op("dve", "tensor_copy", r, w, out=out, in_=in_)

    def tt(self, eng, out, in0, in1, op, r, w):
        return self.op(eng, "tensor_tensor", r, w, out=out, in0=in0, in1=in1, op=op)

    def ts(self, eng, out, in0, s1, s2, op0, op1, r, w):
        return self.op(eng, "tensor_scalar", r, w, out=out, in0=in0, scalar1=s1, scalar2=s2, op0=op0, op1=op1)

    def stt(self, out, in0, scalar, in1, op0, op1, r, w):
        return self.op("dve", "scalar_tensor_tensor", r, w, out=out, in0=in0, scalar=scalar, in1=in1, op0=op0, op1=op1)

    def dma(self, eng, out, in_, r, w, stream):
        return self.S.dma(eng, lambda e: e.dma_start(out=out, in_=in_), r, w, stream)

    def cfa(self, name, w=128):
        o = CF_OFF[name]
        return self.cf[:, o:o + w]

    def cba(self, name):
        o = CB_OFF[name]
        return self.cb[:, o:o + 128]

    def psk(self, bank, c0=0, c1=512):
        return [("ps", bank)]

    def bigk(self, lo, hi):
        return [("big", i) for i in range(lo // 512, (hi + 511) // 512)]

    def load_slab(self, src_ap, nc_, ncols):
        i = self.slab_i % 2
        self.slab_i += 1
        t = self.slabs[i]
        view = t[:, 0:nc_ * ncols].rearrange("p (c n) -> p c n", c=nc_)
        self.dma("pool", view, src_ap, [], [("slab", i)], "slab%d" % i)
        return ("slab", i), view

    def build(self):
        nc = self.nc
        NTOK, T, TB = self.NTOK, self.T, self.TB
        with ExitStack() as es:
            self.es = es
            self.hin = self.dram_in("hT_in", [D, NTOK])
            self.hout = nc.dram_tensor("hT_out", [D, NTOK], F32, kind="ExternalOutput").ap()
            self.cf_d = self.dram_in("cf", [128, CF_W + CB_W])
            self.cv_d = self.dram_in("cvec", [128, CV_W])
            self.rw_d = self.dram_in("rows", [128, RW_W])
            need_g = (not self.skip_mixer) and any(l % 2 == 0 for l in self.layers)
            need_h = (not self.skip_mixer) and any(l % 2 == 1 for l in self.layers)
            self.need_g, self.need_h = need_g, need_h
            if need_g:
                self.gdn_w_in = self.dram_in("gdn_w_in", [2, D, GDN_IN])
                self.gdn_w_out = self.dram_in("gdn_w_out", [2, D, D])
            if need_h:
                self.hgrn_w_in = self.dram_in("hgrn_w_in", [2, D, 4 * D])
                self.hgrn_w_out = self.dram_in("hgrn_w_out", [2, D, D])
                self.lb_d = self.dram_in("lbrows", [128, 4096])
            if self.layers:
                self.mlp_w_up = self.dram_in("mlp_w_up", [4, D, 4 * D])
                self.mlp_w_down = self.dram_in("mlp_w_down", [4, 4 * D, D])
            self.hT = self.sb("hT", [128, NCH, T], F32)
            self.yT = self.sb("yT", [128, NCH, TB], BF16)
            self.slabs = [self.sb("slab%d" % i, [128, 4096], BF16) for i in range(2)]
            self.cv = self.sb("cv", [128, CV_W], F32)
            self.rw = self.sb("rw", [128, RW_W], F32)
            self.cf = self.sb("cff", [128, CF_W], F32)
            self.cb = self.sb("cfb", [128, CB_W], BF16)
            self.rstd = self.sb("rstd", [128, 512], F32)
            self.lnt = self.sb("lnt", [128, 512], F32)
            self.rstd2 = self.sb("rstd2", [128, 512], F32)
            self.lnt2 = self.sb("lnt2", [128, 512], F32)
            self.sqs = self.sb("sqs", [128, 2, 512], BF16)
            self.big = self.sb("big", [128, 32 * TB], BF16)
            self.gate = self.sb("gate", [128, H, TB], BF16)
            self.OT = self.sb("OT", [128, H, TB], BF16)
            self.tmpf = self.sb("tmpf", [128, 1024], F32)
            self.tmpf2 = self.sb("tmpf2", [128, 1024], F32)
            self.Sst = self.sb("Sst", [128, H, 128], F32)
            self.Sbf = self.sb("Sbf", [128, H, 128], BF16)
            self.ps = [es.enter_context(nc.psum_tensor("ps%d" % i, [128, 512], F32)) for i in range(8)]
            if need_g or need_h:
                self.mxf = self.sb("mxf", [128, 2, 1024], F32)
                self.mxb = self.sb("mxb", [128, 2, 12, 128], BF16)
                self.lbt = self.sb("lbt", [128, 2, 1024], F32)
                self.kdt = self.sb("kdt", [128, 1024], BF16)
                self.smallf = self.sb("smallf", [128, 4, 64], F32)
            if need_g:
                self.pre = self.sb("pre", [128, 1, 4, 3 + TB], BF16)
                self.halo = self.sb("halo", [128, 24, 3], BF16)
                self.Gs = self.sb("Gs", [128, self.NCI, 16], F32)
                import os as _os
                if int(_os.environ.get("PADT", "0")):
                    self.padt = self.sb("padt", [128, int(_os.environ.get("PADT"))], BF16)
                self.nr = self.sb("nr", [128, 2, 2, 256], F32)
                self.ysb = self.sb("ysb", [128, 2, 2, 256], F32)
            self.emit_all()
            self.S.emit(nc, es)
        return nc

    def emit_all(self):
        T = self.T
        self.dma("sp", self.cf[:], self.cf_d[:, 0:CF_W], [], ["cf"], "c0")
        self.dma("pool", self.cb[:], self.cf_d[:, CF_W:CF_W + CB_W], [], ["cb"], "c3")
        self.dma("sp", self.cv[:], self.cv_d[:, :], [], ["cv"], "c1")
        self.dma("sp", self.rw[:], self.rw_d[:, :], [], ["rw"], "c2")
        for s in range(self.nseq):
            t0 = s * T
            for c in range(NCH):
                self.dma("sp", self.hT[:, c, :], self.hin[c * 128:(c + 1) * 128, t0:t0 + T], [], [("h", c)], "hin")
            for l in self.layers:
                if self.skip_mixer:
                    pass
                elif l % 2 == 0:
                    self.gdn_layer(l)
                else:
                    self.hgrn_layer(l)
                self.mlp_layer(l)
            if self.final_norm:
                for b in range(T // self.TB):
                    self.rmsnorm_block(b, CV_OFF["norm_final"], out_f32=True)
            for c in range(NCH):
                self.dma("sp", self.hout[c * 128:(c + 1) * 128, t0:t0 + T], self.hT[:, c, :], [("h", c)], [("hout", s, c)], "hout")
        self.S.add("sp", lambda e: None, r=[("hout", s, c) for s in range(self.nseq) for c in range(NCH)], w=["done"])

    def sumsq_rstd(self, srcs, keys, n, scale, bias, pk=0):
        P0 = self.ps[pk]
        pkey = self.psk(pk, 0, n)
        ns = len(srcs)
        for i, (sap, k) in enumerate(zip(srcs, keys)):
            sqv = self.sqs[:, i % 2, 0:n]
            self.actv(sqv, sap, AF.Square, [k], [("sqs", i % 2)])
            self.mm(P0[:, 0:n], self.cba("ONES"), sqv, i == 0, i == ns - 1, [("sqs", i % 2), "cb"], pkey)
        self.actv(self.lnt[:, 0:n], P0[:, 0:n], AF.Ln, pkey, ["lnt"], scale=scale, bias=bias)

    def norm1(self, st, src, skeys, n, scale, bias):
        bank = 0 if st == 0 else 3
        P0 = self.ps[bank]
        pkey = self.psk(bank)
        lnt = self.lnt if st == 0 else self.lnt2
        rstd = self.rstd if st == 0 else self.rstd2
        lk, rk = ("lnt" if st == 0 else "lnt2"), ("rstd" if st == 0 else "rstd2")
        sqv = self.sqs[:, st, 0:n]
        self.actv(sqv, src, AF.Square, skeys, [("sqs", st)])
        self.mm(P0[:, 0:n], self.cba("ONES"), sqv, True, True, [("sqs", st), "cb"], pkey)
        self.actv(lnt[:, 0:n], P0[:, 0:n], AF.Ln, pkey, [lk], scale=scale, bias=bias)
        self.actv(rstd[:, 0:n], lnt[:, 0:n], AF.Exp, [lk], [rk], scale=-0.5)
        return rstd[:, 0:n], rk

    def rmsnorm_block(self, b, cvoff, out_f32=False):
        TB = self.TB
        tsl = slice(b * TB, (b + 1) * TB)
        self.sumsq_rstd([self.hT[:, c, tsl] for c in range(NCH)], [("h", c) for c in range(NCH)], TB, 1.0 / D, EPS)
        self.actv(self.rstd[:, 0:TB], self.lnt[:, 0:TB], AF.Exp, ["lnt"], ["rstd"], scale=-0.5)
        for c in range(NCH):
            if out_f32:
                self.stt(self.hT[:, c, tsl], self.hT[:, c, tsl], self.cv[:, cvoff + c:cvoff + c + 1], self.rstd[:, 0:TB],
                         ALU.mult, ALU.mult, [("h", c), "rstd", "cv"], [("h", c)])
            else:
                self.stt(self.yT[:, c, :], self.hT[:, c, tsl], self.cv[:, cvoff + c:cvoff + c + 1], self.rstd[:, 0:TB],
                         ALU.mult, ALU.mult, [("h", c), "rstd", "cv"], [("y", c)])

    def proj_fm(self, W, col0, ncols, evac):
        TB = self.TB
        yk = [("y", c) for c in range(NCH)]
        Wv = W.rearrange("(c p) n -> p c n", p=128)
        for s0 in range(0, ncols, 512):
            w_ = min(512, ncols - s0)
            sk, wv = self.load_slab(Wv[:, :, col0 + s0:col0 + s0 + w_], 8, w_)
            for m in range(w_ // 128):
                pi = self.pi
                self.pi += 1
                P = self.ps[1 + (pi % 2)]
                pk = self.psk(1 + (pi % 2))
                for c in range(NCH):
                    self.mm(P[:, 0:TB], wv[:, c, m * 128:(m + 1) * 128], self.yT[:, c, :], c == 0, c == NCH - 1, [sk] + yk, pk)
                evac((s0 // 128) + m, P, pk)

    def proj_tm(self, W, col0, ncols, evac):
        TB = self.TB
        yk = [("y", c) for c in range(NCH)]
        Wv = W.rearrange("(c p) n -> p c n", p=128)
        for s0 in range(0, ncols, 512):
            w_ = min(512, ncols - s0)
            sk, wv = self.load_slab(Wv[:, :, col0 + s0:col0 + s0 + w_], 8, w_)
            for tt_ in range(TB // 128):
                pi = self.pi
                self.pi += 1
                P = self.ps[1 + (pi % 2)]
                pk = self.psk(1 + (pi % 2))
                for c in range(NCH):
                    self.mm(P[:, 0:w_], self.yT[:, c, tt_ * 128:(tt_ + 1) * 128], wv[:, c, :], c == 0, c == NCH - 1, [sk] + yk, pk)
                evac(s0 // 512, tt_, P, pk, w_)

    def out_proj(self, W, src, srck, tsl):
        TB = self.TB
        Wv = W.rearrange("(c p) n -> p c n", p=128)
        for s0 in range(0, D, 512):
            sk, wv = self.load_slab(Wv[:, :, s0:s0 + 512], 8, 512)
            for m in range(4):
                pi = self.pi
                self.pi += 1
                P = self.ps[1 + (pi % 2)]
                pk = self.psk(1 + (pi % 2))
                for c in range(NCH):
                    self.mm(P[:, 0:TB], wv[:, c, m * 128:(m + 1) * 128], src[:, c, :], c == 0, c == NCH - 1, [sk, srck(c)], pk)
                mo = s0 // 128 + m
                self.tt("dve", self.hT[:, mo, tsl], P[:, 0:TB], self.hT[:, mo, tsl], ALU.add, pk + [("h", mo)], [("h", mo)])

    def mlp_layer(self, l):
        TB = self.TB
        hid = self.big[:, 0:32 * TB].rearrange("p (c t) -> p c t", c=32)
        hk = lambda c: self.bigk(c * TB * 2, (c + 1) * TB * 2)
        for b in range(self.T // TB):
            tsl = slice(b * TB, (b + 1) * TB)
            self.rmsnorm_block(b, CV_OFF["norm_mlp"] + 8 * l)

            def evac(j, P, pk):
                tf = self.tmpf if (j % 2 == 0) else self.tmpf2
                tk = "tmpf" if (j % 2 == 0) else "tmpf2"
                self.actv(tf[:, 0:TB], P[:, 0:TB], AF.Relu, pk, [tk])
                self.tt("pool", hid[:, j, :], tf[:, 0:TB], tf[:, 0:TB], ALU.mult, [tk], hk(j))
            self.proj_fm(self.mlp_w_up[l], 0, 4 * D, evac)
            Wd = self.mlp_w_down[l].rearrange("(c p) n -> p c n", p=128)
            for m in range(NCH):
                sk, wv = self.load_slab(Wd[:, :, m * 128:(m + 1) * 128], 32, 128)
                pi = self.pi
                self.pi += 1
                P = self.ps[1 + (pi % 2)]
                pk = self.psk(1 + (pi % 2))
                for kc in range(32):
                    self.mm(P[:, 0:TB], wv[:, kc, :], hid[:, kc, :], kc == 0, kc == 31, [sk] + hk(kc), pk)
                self.tt("dve", self.hT[:, m, tsl], P[:, 0:TB], self.hT[:, m, tsl], ALU.add, pk + [("h", m)], [("h", m)])

    def mxfk(self, hp, c0, c1):
        bounds = [0, 128, 256, 896, 1024]
        return [("mxf", hp, q) for q in range(4) if c0 < bounds[q + 1] and c1 > bounds[q]]

    def mbk(self, hp, s0, s1=None):
        s1 = s0 + 1 if s1 is None else s1
        return [("mxb", hp, s) for s in range(s0, s1)]

    def kdk(self, c0, c1):
        return [("kdt", q) for q in range(c0 // 128, (c1 + 127) // 128)]

    def hgrn_layer(self, l):
        TB, NCI = self.TB, self.NCI
        j = l // 2
        W = self.hgrn_w_in[j]
        mb = self.mxb
        lg = self.big[:, 0:8192].bitcast(F32)
        lgk = self.bigk(0, 16384)
        self.dma("sp", lg, self.lb_d[:, :], [], lgk, "lb")
        self.actv(lg, lg, AF.Exp, lgk, lgk)
        t1 = self.mxf[:, 0, :]
        t2 = self.mxf[:, 1, :]
        k1 = self.mxfk(0, 0, 1024)
        k2 = self.mxfk(1, 0, 1024)
        self.tt("dve", t1, lg[:, 0:1024], lg[:, 1024:2048], ALU.add, lgk, k1)
        self.tt("dve", t1, t1, lg[:, 2048:3072], ALU.add, lgk + k1, k1)
        self.tt("dve", t1, t1, lg[:, 3072:4096], ALU.add, lgk + k1, k1)
        self.op("dve", "reciprocal", k1, k1, out=t1, in_=t1)
        if l == 1:
            self.op("dve", "tensor_copy", lgk, k2, out=t2, in_=lg[:, 1024:2048])
        else:
            self.tt("dve", t2, lg[:, 1024:2048], lg[:, 2048:3072], ALU.add, lgk, k2)
            self.tt("dve", t2, t2, lg[:, 3072:4096], ALU.add, lgk + k2, k2)
        LB = self.lbt[:, 0, :]
        OML = self.lbt[:, 1, :]
        self.tt("dve", LB, t2, t1, ALU.mult, k1 + k2, ["lbt"])
        self.ts("dve", OML, LB, -1.0, 1.0, ALU.mult, ALU.add, ["lbt"], ["oml"])
        self.op("dve", "memset", [], [("Sst", h) for h in range(H)], self.Sst[:], 0.0)
        self.op("pool", "memset", [], [("Sbf", h) for h in range(H)], self.Sbf[:], 0.0)
        gbytes = NCI * 4096
        G = self.big[:, 0:gbytes // 2].bitcast(F32).rearrange("p (c n) -> p c n", c=NCI)
        V = self.big[:, gbytes // 2:gbytes // 2 + NCI * 1024].rearrange("p (c n) -> p c n", c=NCI)
        q0 = gbytes + NCI * 2048
        qT = self.big[:, q0 // 2:q0 // 2 + H * TB].rearrange("p (h t) -> p h t", h=H)
        Gk = lambda ci, half: self.bigk(ci * 4096 + half * 2048, ci * 4096 + (half + 1) * 2048)
        Vk = lambda ci, half: self.bigk(gbytes + ci * 2048 + half * 1024, gbytes + ci * 2048 + (half + 1) * 1024)
        qk = lambda h: self.bigk(q0 + h * TB * 2, q0 + (h + 1) * TB * 2)
        qsc = float(128 ** -0.5)
        for b in range(self.T // TB):
            tsl = slice(b * TB, (b + 1) * TB)
            self.rmsnorm_block(b, CV_OFF["norm_mix"] + 8 * l)

            def evac_q(jj, P, pk):
                self.actv(qT[:, jj, :], P[:, 0:TB], AF.Silu, pk, qk(jj))
                self.ts("pool", qT[:, jj, :], qT[:, jj, :], qsc, None, ALU.mult, ALU.bypass, qk(jj), qk(jj))
            self.proj_fm(W, 0, 1024, evac_q)

            def evac_f(si, tt_, P, pk, w_):
                cs = slice(si * 512, (si + 1) * 512)
                a = self.tmpf[:, 0:512]
                self.actv(a, P[:, 0:512], AF.Exp, pk, ["tmpf"], scale=-1.0)
                self.ts("dve", a, a, 1.0, None, ALU.add, ALU.bypass, ["tmpf"], ["tmpf"])
                self.op("dve", "reciprocal", ["tmpf"], ["tmpf"], out=a, in_=a)
                self.tt("dve", a, a, OML[:, cs], ALU.mult, ["tmpf", "oml"], ["tmpf"])
                self.tt("dve", a, a, LB[:, cs], ALU.add, ["tmpf", "lbt"], ["tmpf"])
                self.actv(G[:, tt_, cs], a, AF.Ln, ["tmpf"], Gk(tt_, si))
            self.proj_tm(W, 1024, 1024, evac_f)
            self.proj_tm(W, 2048, 1024, lambda si, tt_, P, pk, w_: self.actv(V[:, tt_, si * 512:(si + 1) * 512], P[:, 0:512], AF.Copy, pk, Vk(tt_, si)))
            self.proj_fm(W, 3072, 1024, lambda jj, P, pk: self.actv(self.gate[:, jj, :], P[:, 0:TB], AF.Silu, pk, [("gate", jj)]))
            for ci in range(NCI):
                csl = slice(ci * 128, (ci + 1) * 128)
                for half in range(2):
                    hs = slice(half * 512, (half + 1) * 512)
                    P = self.ps[3]
                    p3 = self.psk(3)
                    self.mm(P[:, :], self.cfa("SUP"), G[:, ci, hs], True, True, Gk(ci, half) + ["cf"], p3)
                    e1 = self.tmpf[:, 0:512]
                    e2 = self.tmpf2[:, 0:512]
                    self.actv(e1, G[:, ci, hs], AF.Exp, Gk(ci, half), ["tmpf"])
                    self.actv(e2, P[:, :], AF.Exp, p3, ["tmpf2"])
                    self.ts("dve", e1, e1, -1.0, 1.0, ALU.mult, ALU.add, ["tmpf"], ["tmpf"])
                    self.tt("dve", self.kdt[:, hs], e1, e2, ALU.mult, ["tmpf", "tmpf2"], self.kdk(half * 512, (half + 1) * 512))
                def head(h, ci=ci, csl=csl):
                    hp = h % 2
                    hsl = slice(h * 128, (h + 1) * 128)
                    PE0, PE1 = self.ps[4 + 2 * hp], self.ps[5 + 2 * hp]
                    pk0, pk1 = self.psk(4 + 2 * hp), self.psk(5 + 2 * hp, 0, 384)
                    RH = self.cfa("RH", 896)
                    self.mm(PE0[:, :], G[:, ci, hsl], RH[:, 0:512], True, True, Gk(ci, h // 4) + ["cf"], pk0)
                    self.mm(PE1[:, 0:384], G[:, ci, hsl], RH[:, 512:896], True, True, Gk(ci, h // 4) + ["cf"], pk1)
                    E = self.mxf[:, hp, 0:896]
                    ek = self.mxfk(hp, 0, 896)
                    self.actv(E[:, 0:512], PE0[:, :], AF.Exp, pk0, self.mxfk(hp, 0, 512))
                    self.actv(E[:, 512:896], PE1[:, 0:384], AF.Exp, pk1, self.mxfk(hp, 512, 896))
                    kTf = self.mxf[:, hp, 896:1024]
                    kfk = self.mxfk(hp, 896, 1024)
                    self.ts("dve", kTf, E[:, 768:896], -1.0, 1.0, ALU.mult, ALU.add, ek, kfk)
                    KT4 = mb[:, hp, 0:4, :]
                    self.tt("dve", KT4, E[:, 0:512].rearrange("p (r n) -> p r n", r=4),
                            kTf.unsqueeze(1).to_broadcast([128, 4, 128]), ALU.mult, ek + kfk, self.mbk(hp, 0, 4))
                    QQ = mb[:, hp, 4:6, :]
                    self.tt("dve", QQ, E[:, 512:768].rearrange("p (r n) -> p r n", r=2),
                            qT[:, h, csl].unsqueeze(1).to_broadcast([128, 2, 128]), ALU.mult, ek + qk(h), self.mbk(hp, 4, 6))
                    PA = self.ps[1 + hp]
                    pa = self.psk(1 + hp)
                    for r4 in range(4):
                        self.mm(PA[:, r4 * 32:(r4 + 1) * 32], KT4[:, r4, :], QQ[:, 0, r4 * 32:(r4 + 1) * 32], True, True,
                                self.mbk(hp, 0, 6), pa)
                    AT = mb[:, hp, 6, :]
                    self.tt("dve", AT, PA[:, 0:128], self.cba("MASKT"), ALU.mult, pa + ["cb"], self.mbk(hp, 6))
                    PO = self.ps[0 if hp == 0 else 3]
                    po0 = po1 = self.psk(0 if hp == 0 else 3)
                    self.mm(PO[:, 0:128], V[:, ci, hsl], AT, True, False, Vk(ci, h // 4) + self.mbk(hp, 6), po0)
                    self.mm(PO[:, 0:128], self.Sbf[:, h, :], QQ[:, 1, :], False, True, [("Sbf", h)] + self.mbk(hp, 5), po0)
                    self.actv(self.OT[:, h, csl], PO[:, 0:128], AF.Copy, po0, [("OT", h)])
                    self.mm(PO[:, 128:256], self.kdt[:, hsl], V[:, ci, hsl], True, True, self.kdk(h * 128, (h + 1) * 128) + Vk(ci, h // 4), po1)
                    self.stt(self.Sst[:, h, :], self.Sst[:, h, :], E[:, 767:768], PO[:, 128:256], ALU.mult, ALU.add,
                             [("Sst", h)] + ek + po1, [("Sst", h)])
                    self.actv(self.Sbf[:, h, :], self.Sst[:, h, :], AF.Copy, [("Sst", h)], [("Sbf", h)])

                for h0 in range(0, H, 2):
                    self.S.interleave([lambda h0=h0: head(h0), lambda h0=h0: head(h0 + 1)])
            self.sumsq_rstd([self.OT[:, h, :] for h in range(H)], [("OT", h) for h in range(H)], TB, 1.0 / D, EPS)
            self.actv(self.rstd[:, 0:TB], self.lnt[:, 0:TB], AF.Exp, ["lnt"], ["rstd"], scale=-0.5)
            go = CV_OFF["gnorm"] + 8 * j
            for h in range(H):
                self.stt(self.OT[:, h, :], self.OT[:, h, :], self.cv[:, go + h:go + h + 1], self.rstd[:, 0:TB], ALU.mult, ALU.mult,
                         [("OT", h), "cv", "rstd"], [("OT", h)])
                self.tt("pool", self.OT[:, h, :], self.OT[:, h, :], self.gate[:, h, :], ALU.mult, [("OT", h), ("gate", h)], [("OT", h)])
            self.out_proj(self.hgrn_w_out[j], self.OT, lambda c: ("OT", c), tsl)

    def gdn_layer(self, l):
        TB, NCI = self.TB, self.NCI
        j = l // 2
        W = self.gdn_w_in[j]
        QKV = self.big[:, 0:24 * TB].rearrange("p (c t) -> p c t", c=24)
        ck = lambda c: self.bigk(c * TB * 2, (c + 1) * TB * 2)
        nega = self.smallf[:, 0, 0:8]
        self.actv(nega, self.rw[:, 8 * j:8 * j + 8], AF.Exp, ["rw"], ["nega"])
        self.ts("dve", nega, nega, -1.0, None, ALU.mult, ALU.bypass, ["nega"], ["nega"])
        self.op("dve", "memset", [], [("Sst", h) for h in range(H)], self.Sst[:], 0.0)
        self.op("pool", "memset", [], [("Sbf", h) for h in range(H)], self.Sbf[:], 0.0)
        self.op("pool", "memset", [], [("halo", c) for c in range(24)], self.halo[:], 0.0)
        cvo = CV_OFF["conv"] + j * 96
        for b in range(self.T // TB):
            tsl = slice(b * TB, (b + 1) * TB)
            self.rmsnorm_block(b, CV_OFF["norm_mix"] + 8 * l)

            def evac_qkv(jj, P, pk):
                pp = 0
                slot = jj % 4
                pv = self.pre[:, pp, slot, :]
                pkk = ("pre", pp, slot)
                self.actv(pv[:, 3:3 + TB], P[:, 0:TB], AF.Copy, pk, [pkk])
                self.op("pool", "tensor_copy", [("halo", jj)], [pkk], out=pv[:, 0:3], in_=self.halo[:, jj, :])
                acc = self.tmpf[:, 0:TB] if jj % 2 == 0 else self.tmpf2[:, 0:TB]
                ak = "tmpf" if jj % 2 == 0 else "tmpf2"
                wc = cvo + jj * 4
                self.ts("dve", acc, pv[:, 0:TB], self.cv[:, wc:wc + 1], None, ALU.mult, ALU.bypass, [pkk, "cv"], [ak])
                for tap in range(1, 4):
                    self.stt(acc, pv[:, tap:tap + TB], self.cv[:, wc + tap:wc + tap + 1], acc, ALU.mult, ALU.add, [pkk, "cv", ak], [ak])
                self.op("pool", "tensor_copy", [pkk], [("halo", jj)], out=self.halo[:, jj, :], in_=pv[:, TB:TB + 3])
                self.actv(QKV[:, jj, :], acc, AF.Silu, [ak], ck(jj))
            self.proj_fm(W, 0, 3072, evac_qkv)
            self.proj_fm(W, 3072, 1024, lambda jj, P, pk: self.actv(self.gate[:, jj, :], P[:, 0:TB], AF.Silu, pk, [("gate", jj)]))
            if self.dbg_stop == 1:
                continue

            def evac_ab(si, tt_, P, pk, w_):
                x1 = self.smallf[:, 1, 0:16]
                self.tt("dve", x1[:, 0:8], P[:, 0:8], self.rw[:, 16 + 8 * j:24 + 8 * j], ALU.add, pk + ["rw"], ["x1"])
                self.ts("dve", x1[:, 8:16], P[:, 8:16], -1.0, None, ALU.mult, ALU.bypass, pk, ["x1b"])
                self.actv(x1, x1, AF.Exp, ["x1", "x1b"], ["x1", "x1b"])
                self.actv(x1, x1, AF.Ln, ["x1", "x1b"], ["x1", "x1b"], bias=1.0)
                self.tt("dve", self.Gs[:, tt_, 0:8], x1[:, 0:8], nega, ALU.mult, ["x1", "nega"], [("Gs", tt_)])
                self.ts("dve", self.Gs[:, tt_, 8:16], x1[:, 8:16], -1.0, None, ALU.mult, ALU.bypass, ["x1b"], [("Gsb", tt_)])
            self.proj_tm(W, 4096, 16, evac_ab)
            if self.dbg_stop == 2:
                continue
            def l2n(c, st):
                rs, rk = self.norm1(st, QKV[:, c, :], ck(c), TB, 1.0, EPS)
                if c < 8:
                    self.stt(QKV[:, c, :], QKV[:, c, :], float(128 ** -0.5), rs, ALU.mult, ALU.mult, ck(c) + [rk], ck(c))
                else:
                    self.tt("dve", QKV[:, c, :], QKV[:, c, :], rs, ALU.mult, ck(c) + [rk], ck(c))
            for c0 in range(0, 16, 2):
                self.S.interleave([lambda c0=c0: l2n(c0, 0), lambda c0=c0: l2n(c0 + 1, 1)])
            if self.dbg_stop == 3:
                continue
            for ci in range(NCI):
                self.gdn_chunk(ci, QKV, ck)
            if self.dbg_stop >= 4:
                continue
            oo = CV_OFF["onorm"] + j
            def hnorm(h, st):
                rs, rk = self.norm1(st, self.OT[:, h, :], [("OT", h)], TB, 1.0 / 128, EPS)
                self.stt(self.OT[:, h, :], self.OT[:, h, :], self.cv[:, oo:oo + 1], rs, ALU.mult, ALU.mult,
                         [("OT", h), "cv", rk], [("OT", h)])
                self.tt("pool", self.OT[:, h, :], self.OT[:, h, :], self.gate[:, h, :], ALU.mult, [("OT", h), ("gate", h)], [("OT", h)])
            for h0 in range(0, H, 2):
                self.S.interleave([lambda h0=h0: hnorm(h0, 0), lambda h0=h0: hnorm(h0 + 1, 1)])
            self.out_proj(self.gdn_w_out[j], self.OT, lambda c: ("OT", c), tsl)

    def gdn_chunk(self, ci, QKV, ck):
        mb = self.mxb
        csl = slice(ci * 128, (ci + 1) * 128)
        g = self.Gs[:, ci, 0:8]
        lnb = self.Gs[:, ci, 8:16]
        gk, lk = ("Gs", ci), ("Gsb", ci)
        PSm = self.ps[3]
        p3a = self.psk(3, 0, 128)
        self.mm(PSm[:, 0:8], self.cfa("LT"), g, True, False, [gk, "cf"], p3a)
        self.mm(PSm[:, 0:8], self.cfa("ID"), lnb, False, True, [lk, "cf"], p3a)
        self.mm(PSm[:, 8:16], self.cfa("SUP"), g, True, True, [gk, "cf"], p3a)
        self.mm(PSm[:, 16:24], self.cfa("ONES"), g, True, True, [gk, "cf"], p3a)
        self.mm(PSm[:, 24:32], self.cfa("ID"), lnb, True, True, [lk, "cf"], p3a)
        EG = self.smallf[:, 2, 0:32]
        self.actv(EG, PSm[:, 0:32], AF.Exp, p3a, ["EG"])
        negb = self.smallf[:, 3, 0:8]
        self.ts("dve", negb, EG[:, 24:32], -1.0, None, ALU.mult, ALU.bypass, ["EG"], ["negb"])
        if self.dbg_stop == 4:
            return
        nr, ys = self.nr, self.ysb
        def head(h):
            hp = h % 2
            qTh, kTh, vTh = QKV[:, h, csl], QKV[:, 8 + h, csl], QKV[:, 16 + h, csl]
            qk_, kk_, vk_ = ck(h), ck(8 + h), ck(16 + h)
            b3 = 0 if hp == 0 else 3
            PT = self.ps[b3][:, :].bitcast(BF16)
            p3b = p3c = p3d = self.psk(b3)
            Gb = self.mxf[:, hp, 0:128]
            Gb2 = self.mxf[:, hp, 128:256]
            gbk, gb2k = self.mxfk(hp, 0, 128), self.mxfk(hp, 128, 256)
            self.ts("dve", Gb, self.cfa("SUP"), g[:, h:h + 1], None, ALU.mult, ALU.bypass, ["cf", gk], gbk)
            self.ts("dve", Gb2, self.cfa("LT"), g[:, h:h + 1], None, ALU.mult, ALU.bypass, ["cf", gk], gb2k)
            PD = self.ps[4 + hp]
            pd0, pd1, pd2 = self.psk(4 + hp, 0, 128), self.psk(4 + hp, 128, 256), self.psk(4 + hp, 256, 384)
            self.mm(PD[:, 0:128], self.cfa("LT"), Gb, True, False, ["cf"] + gbk, pd0)
            self.mm(PD[:, 0:128], self.cba("ID"), self.cba("NEGS"), False, True, ["cb"], pd0)
            self.mm(PD[:, 128:256], self.cfa("SUP"), Gb2, True, False, ["cf"] + gb2k, pd1)
            self.mm(PD[:, 128:256], self.cba("ID"), self.cba("NEGCT"), False, True, ["cb"], pd1)
            self.mm(PD[:, 256:384], self.cfa("ONES"), Gb2, True, True, ["cf"] + gb2k, pd2)
            DEC = mb[:, hp, 0:3, :]
            self.actv(DEC[:, 1:3, :], PD[:, 128:384].rearrange("p (a n) -> p a n", a=2), AF.Exp, pd1 + pd2, self.mbk(hp, 1, 3))
            fz = self.mxf[:, hp, :]
            zk = self.mxfk(hp, 256, 896)
            decS, N, No, U64, T64 = fz[:, 256:384], fz[:, 384:512], fz[:, 512:640], fz[:, 640:768], fz[:, 768:896]
            Pm = fz[:, 896:1024]
            pmk = self.mxfk(hp, 896, 1024)
            self.actv(decS, PD[:, 0:128], AF.Exp, pd0, zk)
            PK = self.ps[6 + hp]
            pq0, pq1, pq2, pq3 = (self.psk(6 + hp, 0, 128), self.psk(6 + hp, 128, 256), self.psk(6 + hp, 256, 384),
                                  self.psk(6 + hp, 384, 512))
            self.mm(PK[:, 0:128], kTh, kTh, True, True, kk_, pq0)
            self.mm(PK[:, 128:256], kTh, qTh, True, True, kk_ + qk_, pq1)
            self.stt(N, PK[:, 0:128], negb[:, h:h + 1], decS, ALU.mult, ALU.mult, pq0 + ["negb"] + zk, zk)
            AT = mb[:, hp, 4, :]
            self.tt("dve", AT, PK[:, 128:256], DEC[:, 1, :], ALU.mult, pq1 + self.mbk(hp, 1), self.mbk(hp, 4))
            Nd = nr[:, hp, 0, 0:128]
            self.tt("pool", Nd, N, self.cba("BD"), ALU.mult, zk + ["cb"], [("nrN", hp, 0)])
            self.tt("pool", No, N, self.cba("OD"), ALU.mult, zk + ["cb"], zk)
            P3 = self.ps[b3]
            self.op("pe", "transpose", [("nrN", hp, 0), "cf"], p3b, P3[:, 128:256], Nd, self.cfa("ID"))
            Yd = ys[:, hp, 0, 0:128]
            self.cp(Yd, P3[:, 128:256], p3b, [("ysY", hp, 0)])
            self.mm(PK[:, 256:384], Nd, Yd, True, True, [("nrN", hp, 0), ("ysY", hp, 0)], pq2)
            self.mm(PK[:, 384:512], Yd, Nd, True, True, [("nrN", hp, 0), ("ysY", hp, 0)], pq3)
            self.cp(ys[:, hp, 1, 0:128], PK[:, 256:384], pq2, [("ysY", hp, 1)])
            self.tt("pool", ys[:, hp, 1, 128:256], Yd, self.cfa("ID"), ALU.add, [("ysY", hp, 0), "cf"], [("ysS", hp, 1)])
            self.cp(nr[:, hp, 1, 0:128], PK[:, 384:512], pq3, [("nrN", hp, 1)])
            self.tt("pool", nr[:, hp, 1, 128:256], Nd, self.cfa("ID"), ALU.add, [("nrN", hp, 0), "cf"], [("nrR", hp, 1)])
            cur = 1
            PA_, PB_ = self.ps[4 + hp], self.ps[6 + hp]
            pa01, pb01 = pd0 + pd1, pq0 + pq1
            for lev in range(1, 5):
                nx = 1 - cur
                rk = [("nrN", hp, cur), ("nrR", hp, cur), ("ysY", hp, cur), ("ysS", hp, cur)]
                self.mm(PA_[:, 0:256], nr[:, hp, cur, 0:128], ys[:, hp, cur, :], True, True, rk, pa01)
                self.mm(PB_[:, 0:256], ys[:, hp, cur, 0:128], nr[:, hp, cur, :], True, True, rk, pb01)
                self.cp(ys[:, hp, nx, 0:128], PA_[:, 0:128], pd0, [("ysY", hp, nx)])
                self.tt("dve", ys[:, hp, nx, 128:256], PA_[:, 128:256], ys[:, hp, cur, 128:256], ALU.add, pd1 + [("ysS", hp, cur)], [("ysS", hp, nx)])
                self.cp(nr[:, hp, nx, 0:128], PB_[:, 0:128], pq0, [("nrN", hp, nx)])
                self.tt("dve", nr[:, hp, nx, 128:256], PB_[:, 128:256], nr[:, hp, cur, 128:256], ALU.add, pq1 + [("nrR", hp, cur)], [("nrR", hp, nx)])
                cur = nx
            rk = [("nrN", hp, cur), ("nrR", hp, cur), ("ysY", hp, cur), ("ysS", hp, cur)]
            self.mm(PA_[:, 0:128], nr[:, hp, cur, 0:128], ys[:, hp, cur, 128:256], True, True, rk, pd0)
            self.mm(PB_[:, 0:128], ys[:, hp, cur, 0:128], nr[:, hp, cur, 128:256], True, True, rk, pq0)
            self.tt("dve", U64, PA_[:, 0:128], ys[:, hp, cur, 128:256], ALU.add, pd0 + [("ysS", hp, cur)], zk)
            self.tt("dve", T64, PB_[:, 0:128], nr[:, hp, cur, 128:256], ALU.add, pq0 + [("nrR", hp, cur)], zk)
            self.mm(PA_[:, 0:128], No, U64, True, True, zk, pd0)
            self.cp(Pm, PA_[:, 0:128], pd0, pmk)
            self.mm(PB_[:, 0:128], T64, Pm, True, True, zk + pmk, pq0)
            U = mb[:, hp, 9, :]
            self.tt("dve", U, PB_[:, 0:128], U64, ALU.add, pq0 + zk, self.mbk(hp, 9))
            if self.dbg_stop == 8:
                return
            self.tr(PT[:, 512:640], kTh, kk_, p3c)
            kbgn = mb[:, hp, 10, :]
            kdec = mb[:, hp, 11, :]
            self.ts("dve", kbgn, PT[:, 512:640], EG[:, h:h + 1], -1.0, ALU.mult, ALU.mult, p3c + ["EG"], self.mbk(hp, 10))
            self.ts("dve", kdec, PT[:, 512:640], EG[:, 8 + h:9 + h], None, ALU.mult, ALU.bypass, p3c + ["EG"], self.mbk(hp, 11))
            self.tr(PT[:, 768:896], vTh, vk_, p3d)
            c_vb, c_w, c_vn, c_qd = hp * 128, 256 + hp * 128, 512 + hp * 128, 768 + hp * 128
            vb = self.kdt[:, c_vb:c_vb + 128]
            self.ts("dve", vb, PT[:, 768:896], EG[:, 24 + h:25 + h], None, ALU.mult, ALU.bypass, p3d + ["EG"], self.kdk(c_vb, c_vb + 128))
            if self.dbg_stop == 9:
                return
            PO = self.ps[1 + hp]
            po = [self.psk(1 + hp) for q in range(4)]
            self.mm(PO[:, 0:128], kbgn, U, True, True, self.mbk(hp, 10) + self.mbk(hp, 9), po[0])
            wTn = self.kdt[:, c_w:c_w + 128]
            self.cp(wTn, PO[:, 0:128], po[0], self.kdk(c_w, c_w + 128))
            self.mm(PO[:, 128:256], U, vb, True, False, self.mbk(hp, 9) + self.kdk(c_vb, c_vb + 128), po[1])
            self.mm(PO[:, 128:256], wTn, self.Sbf[:, h, :], False, True, self.kdk(c_w, c_w + 128) + [("Sbf", h)], po[1])
            vnew = self.kdt[:, c_vn:c_vn + 128]
            vnk = self.kdk(c_vn, c_vn + 128)
            self.cp(vnew, PO[:, 128:256], po[1], vnk)
            qd = self.kdt[:, c_qd:c_qd + 128]
            qdk = self.kdk(c_qd, c_qd + 128)
            self.tt("pool", qd, qTh, DEC[:, 2, :], ALU.mult, qk_ + self.mbk(hp, 2), qdk)
            self.mm(PO[:, 256:384], self.Sbf[:, h, :], qd, True, False, [("Sbf", h)] + qdk, po[2])
            self.mm(PO[:, 256:384], vnew, AT, False, True, vnk + self.mbk(hp, 4), po[2])
            self.cp(self.OT[:, h, csl], PO[:, 256:384], po[2], [("OT", h)])
            self.mm(PO[:, 384:512], kdec, vnew, True, True, self.mbk(hp, 11) + vnk, po[3])
            self.stt(self.Sst[:, h, :], self.Sst[:, h, :], EG[:, 16 + h:17 + h], PO[:, 384:512], ALU.mult, ALU.add,
                     [("Sst", h), "EG"] + po[3], [("Sst", h)])
            self.cp(self.Sbf[:, h, :], self.Sst[:, h, :], [("Sst", h)], [("Sbf", h)])

        for h0 in range(0, H, 2):
            self.S.interleave([lambda h0=h0: head(h0), lambda h0=h0: head(h0 + 1)])


def make_in_map(inp, hT, prog):
    m = {"hT_in": np.ascontiguousarray(hT, dtype=np.float32), "cf": _consts(), "cvec": _make_cvec(inp), "rows": _make_rows(inp)}
    need_g, need_h = prog.need_g, prog.need_h
    if need_g:
        m["gdn_w_in"] = inp["gdn_w_in"]
        m["gdn_w_out"] = inp["gdn_w_out"]
    if need_h:
        m["hgrn_w_in"] = inp["hgrn_w_in"]
        m["hgrn_w_out"] = inp["hgrn_w_out"]
        m["lbrows"] = _make_lbrows(inp)
    if prog.layers:
        m["mlp_w_up"] = inp["mlp_w_up"]
        m["mlp_w_down"] = inp["mlp_w_down"]
    return m


N_CORES = 8
_CFG = {"T": 2048, "TB": 512, "nseq": 2, "groups": [[0, 1, 2, 3]], "skip_mixer": False}


def kernel(**inp):
    inp = {k: np.asarray(v) for k, v in inp.items()}
    x = inp["x"]
    B, T, Dm = x.shape
    nseq = B // N_CORES
    hT = [np.ascontiguousarray(x[c * nseq:(c + 1) * nseq].reshape(nseq * T, Dm).T) for c in range(N_CORES)]
    groups = _CFG["groups"]
    for gi, layers in enumerate(groups):
        p = Prog(nseq, T, _CFG["TB"], layers, final_norm=(gi == len(groups) - 1))
        p.skip_mixer = _CFG["skip_mixer"]
        nc = p.build()
        base = make_in_map(inp, hT[0], p)
        in_maps = []
        for c in range(N_CORES):
            m = dict(base)
            m["hT_in"] = hT[c]
            in_maps.append(m)
        res = run_bass_kernel_spmd(nc, in_maps, core_ids=list(range(N_CORES)))
        hT = [np.ascontiguousarray(r["hT_out"]) for r in res.results]
    out = np.stack([h.T.reshape(nseq, T, Dm) for h in hT], axis=0).reshape(B, T, Dm)
    return out.astype(np.float32)
```

```python
import numpy as np
from contextlib import ExitStack
import concourse.bass as bass
import concourse.mybir as mybir
from concourse.bass_utils import run_bass_kernel_spmd

F32 = mybir.dt.float32
BF16 = mybir.dt.bfloat16
AF = mybir.ActivationFunctionType
ALU = mybir.AluOpType

D = 1024
NCH = 8
H = 8
C = 128
EPS = 1e-6
GDN_IN = 4112
MAXV = 30000


class _Op:
    __slots__ = ("eng", "fn", "deps", "kind", "stream", "sig", "upg")

    def __init__(self, eng, fn, deps, kind, stream):
        self.eng, self.fn, self.deps, self.kind, self.stream, self.sig = eng, fn, deps, kind, stream, None


class Sched:
    ENGS = ("pe", "act", "dve", "pool", "sp")

    def __init__(self):
        self.ops = []
        self.lastw = {}
        self.readers = {}
        self.stream_last = {}
        self.recording = None

    def merge_lists(self, lists):
        n = max(len(l) for l in lists)
        for i in range(n):
            for l in lists:
                if i < len(l):
                    self.add(*l[i])

    def interleave(self, bodies):
        lists = []
        for b in bodies:
            self.recording = []
            b()
            lists.append(self.recording)
            self.recording = None
        n = max(len(l) for l in lists)
        for i in range(n):
            for l in lists:
                if i < len(l):
                    self.add(*l[i])

    def add(self, eng, fn, r=(), w=(), kind="c", stream=None):
        if self.recording is not None:
            self.recording.append((eng, fn, tuple(r), tuple(w), kind, stream))
            return -1
        i = len(self.ops)
        deps = {}

        def upd(d, t):
            o = deps.get(d)
            if o is None or (t == "raw") or (t == "waw" and o == "war"):
                deps[d] = t
        for k in r:
            if k in self.lastw:
                upd(self.lastw[k], "raw")
        for k in w:
            if k in self.lastw:
                upd(self.lastw[k], "waw")
            for d in self.readers.get(k, ()):
                if d != i:
                    upd(d, "war")
        op = _Op(eng, fn, deps, kind, stream)
        op.upg = dict(self.stream_last)
        self.ops.append(op)
        if kind == "d":
            self.stream_last[stream] = i
        for k in r:
            self.readers.setdefault(k, []).append(i)
        for k in w:
            self.lastw[k] = i
            self.readers[k] = []
        return i

    def dma(self, eng, fn, r, w, stream):
        return self.add(eng, fn, r, w, kind="d", stream=stream)

    def _needs_sync(self, op, D, t):
        if D.kind == "d":
            return True
        if D.eng != op.eng:
            return True
        if op.kind == "d":
            return True
        if op.eng == "pe":
            return False
        return True

    def emit(self, nc, es):
        ops = self.ops
        need = [False] * len(ops)
        for op in ops:
            for d, t in op.deps.items():
                if self._needs_sync(op, ops[d], t):
                    need[d] = True
        cnt = {}
        semkeys = []

        def nxt(key, inc):
            st = cnt.get(key)
            if st is None:
                st = cnt[key] = [0, 0]
                semkeys.append((key, 0))
            if st[1] + inc > MAXV:
                st[0] += 1
                st[1] = 0
                semkeys.append((key, st[0]))
            st[1] += inc
            return (key, st[0]), st[1]
        for i, op in enumerate(ops):
            if op.kind == "d":
                op.sig = nxt(("dma", op.stream), 16)
            elif need[i]:
                op.sig = nxt(("eng", op.eng), 1)
        sems = {}
        for sk in semkeys:
            nm = "s_%s_%s_%d" % (sk[0][0], str(sk[0][1]), sk[1])
            sems[sk] = es.enter_context(nc.semaphore(nm))
        self.nsems = len(sems)
        block = es.enter_context(nc.Block())
        per = {e: [] for e in self.ENGS}
        for i, op in enumerate(ops):
            per[op.eng].append(i)

        def run(eng_name, engine):
            waited = {}
            for i in per[eng_name]:
                op = ops[i]
                for d, t in sorted(op.deps.items()):
                    Dop = ops[d]
                    if not self._needs_sync(op, Dop, t):
                        continue
                    if Dop.kind == "d":
                        Dop = ops[op.upg[Dop.stream]]
                    sk, val = Dop.sig
                    if waited.get(sk, 0) >= val:
                        continue
                    engine.wait_ge(sems[sk], val)
                    waited[sk] = val
                ins = op.fn(engine)
                if op.sig is not None:
                    assert ins is not None
                    ins.then_inc(sems[op.sig[0]], 16 if op.kind == "d" else 1)

        @block.tensor
        def _(e):
            run("pe", e)

        @block.scalar
        def _(e):
            run("act", e)

        @block.vector
        def _(e):
            run("dve", e)

        @block.gpsimd
        def _(e):
            run("pool", e)

        @block.sync
        def _(e):
            run("sp", e)


def _consts():
    k = np.arange(128)[:, None]
    j = np.arange(128)[None, :]
    f = np.float32
    LT = (k <= j).astype(f)
    SUP = (k > j).astype(f)
    ID = (k == j).astype(f)
    ONES = np.ones((128, 128), f)
    NEG = -30000.0
    NEGS = np.where(k > j, 0.0, NEG).astype(f)
    NEGCT = np.where(j >= k, 0.0, NEG).astype(f)
    BD = ((k // 64) == (j // 64)).astype(f)
    OD = 1.0 - BD
    MASKT = (k <= j).astype(f)
    Ms = []
    for r in range(4):
        lo, hi = 32 * r, 32 * (r + 1)
        M = np.zeros((128, 128), f)
        M += ((j < lo) & (k > j) & (k < lo)) * 1.0
        M -= ((j >= lo) & (j < hi) & (k >= lo) & (k <= j)) * 1.0
        Ms.append(M.astype(f))
    MQ = ((k >= (j // 32) * 32) & (k <= j)).astype(f)
    RH = np.concatenate(Ms + [MQ, LT, ID], axis=1)
    cf = np.concatenate([LT, SUP, ID, ONES, RH, ID, ONES, NEGS, NEGCT, BD, OD, MASKT], axis=1)
    return np.ascontiguousarray(cf.astype(f))


CF_OFF = {"LT": 0, "SUP": 128, "ID": 256, "ONES": 384, "RH": 512}
CF_W = 512 + 896
CB_OFF = {"ID": 0, "ONES": 128, "NEGS": 256, "NEGCT": 384, "BD": 512, "OD": 640, "MASKT": 768}
CB_W = 896


def _cvec_layout():
    off = {}
    n = 0
    for nm, w in (("norm_mix", 32), ("norm_mlp", 32), ("norm_final", 8), ("conv", 2 * 24 * 4),
                  ("onorm", 2), ("gnorm", 16)):
        off[nm] = n
        n += w
    return off, n


CV_OFF, CV_W = _cvec_layout()


def _make_cvec(inp):
    cv = np.zeros((128, CV_W), np.float32)
    cv[:, CV_OFF["norm_mix"]:CV_OFF["norm_mix"] + 32] = inp["norm_mix"].reshape(4, 8, 128).transpose(2, 0, 1).reshape(128, 32)
    cv[:, CV_OFF["norm_mlp"]:CV_OFF["norm_mlp"] + 32] = inp["norm_mlp"].reshape(4, 8, 128).transpose(2, 0, 1).reshape(128, 32)
    cv[:, CV_OFF["norm_final"]:CV_OFF["norm_final"] + 8] = inp["norm_final"].reshape(8, 128).T
    cw = inp["gdn_conv"].reshape(2, 4, 24, 128).transpose(3, 0, 2, 1).reshape(128, 2 * 24 * 4)
    cv[:, CV_OFF["conv"]:CV_OFF["conv"] + 192] = cw
    cv[:, CV_OFF["onorm"]:CV_OFF["onorm"] + 2] = inp["gdn_onorm"].T
    cv[:, CV_OFF["gnorm"]:CV_OFF["gnorm"] + 16] = inp["hgrn_gnorm"].reshape(2, 8, 128).transpose(2, 0, 1).reshape(128, 16)
    return cv


RW_W = 32


def _make_rows(inp):
    r = np.zeros((RW_W,), np.float32)
    r[0:16] = inp["gdn_a_log"].reshape(-1)
    r[16:32] = inp["gdn_dt_bias"].reshape(-1)
    return np.ascontiguousarray(np.broadcast_to(r[None, :], (128, RW_W)))


def _make_lbrows(inp):
    r = inp["hgrn_lb_logits"].reshape(-1).astype(np.float32)
    return np.ascontiguousarray(np.broadcast_to(r[None, :], (128, 4096)))


class Prog:
    def __init__(self, nseq, T, TB, layers, final_norm):
        self.nseq, self.T, self.TB, self.layers, self.final_norm = nseq, T, TB, list(layers), final_norm
        self.NTOK = nseq * T
        self.NCI = TB // 128
        self.S = Sched()
        self.nc = bass.Bass("TRN2", target_bir_lowering=False)
        self.slab_i = 0
        self.pi = 0
        self.skip_mixer = False
        self.dbg_stop = 0
        self.dbg2 = 9
        self.dbg3 = 0

    def dram_in(self, name, shape, dt=F32):
        return self.nc.dram_tensor(name, list(shape), dt, kind="ExternalInput").ap()

    def sb(self, name, shape, dt):
        return self.es.enter_context(self.nc.sbuf_tensor(name, list(shape), dt))

    def op(self, eng, meth, r, w, *a, **kw):
        return self.S.add(eng, lambda e: getattr(e, meth)(*a, **kw), r, w)

    def mm(self, out, lhsT, rhs, start, stop, r, w):
        return self.op("pe", "matmul", r, w, out, lhsT=lhsT, rhs=rhs, start=start, stop=stop)

    def tr(self, out, in_, r, w):
        return self.op("pe", "transpose", r + ["cb"], w, out, in_, self.cba("ID"))

    def actv(self, out, in_, func, r, w, **kw):
        return self.op("act", "activation", r, w, out=out, in_=in_, func=func, **kw)

    def cp(self, out, in_, r, w):
        return self.op("dve", "tensor_copy", r, w, out=out, in_=in_)

    def tt(self, eng, out, in0, in1, op, r, w):
        return self.op(eng, "tensor_tensor", r, w, out=out, in0=in0, in1=in1, op=op)

    def ts(self, eng, out, in0, s1, s2, op0, op1, r, w):
        return self.op(eng, "tensor_scalar", r, w, out=out, in0=in0, scalar1=s1, scalar2=s2, op0=op0, op1=op1)

    def stt(self, out, in0, scalar, in1, op0, op1, r, w):
        return self.op("dve", "scalar_tensor_tensor", r, w, out=out, in0=in0, scalar=scalar, in1=in1, op0=op0, op1=op1)

    def dma(self, eng, out, in_, r, w, stream):
        return self.S.dma(eng, lambda e: e.dma_start(out=out, in_=in_), r, w, stream)

    def cfa(self, name, w=128):
        o = CF_OFF[name]
        return self.cf[:, o:o + w]

    def cba(self, name):
        o = CB_OFF[name]
        return self.cb[:, o:o + 128]

    def psk(self, bank, c0=0, c1=512):
        return [("ps", bank)]

    def bigk(self, lo, hi):
        return [("big", i) for i in range(lo // 512, (hi + 511) // 512)]

    def load_slab(self, src_ap, nc_, ncols):
        i = self.slab_i % 2
        self.slab_i += 1
        t = self.slabs[i]
        view = t[:, 0:nc_ * ncols].rearrange("p (c n) -> p c n", c=nc_)
        self.dma("pool", view, src_ap, [], [("slab", i)], "slab%d" % i)
        return ("slab", i), view

    def build(self):
        nc = self.nc
        NTOK, T, TB = self.NTOK, self.T, self.TB
        with ExitStack() as es:
            self.es = es
            self.hin = self.dram_in("hT_in", [D, NTOK])
            self.hout = nc.dram_tensor("hT_out", [D, NTOK], F32, kind="ExternalOutput").ap()
            self.cf_d = self.dram_in("cf", [128, CF_W + CB_W])
            self.cv_d = self.dram_in("cvec", [128, CV_W])
            self.rw_d = self.dram_in("rows", [128, RW_W])
            need_g = (not self.skip_mixer) and any(l % 2 == 0 for l in self.layers)
            need_h = (not self.skip_mixer) and any(l % 2 == 1 for l in self.layers)
            self.need_g, self.need_h = need_g, need_h
            if need_g:
                self.gdn_w_in = self.dram_in("gdn_w_in", [2, D, GDN_IN])
                self.gdn_w_out = self.dram_in("gdn_w_out", [2, D, D])
            if need_h:
                self.hgrn_w_in = self.dram_in("hgrn_w_in", [2, D, 4 * D])
                self.hgrn_w_out = self.dram_in("hgrn_w_out", [2, D, D])
                self.lb_d = self.dram_in("lbrows", [128, 4096])
            if self.layers:
                self.mlp_w_up = self.dram_in("mlp_w_up", [4, D, 4 * D])
                self.mlp_w_down = self.dram_in("mlp_w_down", [4, 4 * D, D])
            self.hT = self.sb("hT", [128, NCH, T], F32)
            self.yT = self.sb("yT", [128, NCH, TB], BF16)
            self.slabs = [self.sb("slab%d" % i, [128, 4096], BF16) for i in range(2)]
            self.cv = self.sb("cv", [128, CV_W], F32)
            self.rw = self.sb("rw", [128, RW_W], F32)
            self.cf = self.sb("cff", [128, CF_W], F32)
            self.cb = self.sb("cfb", [128, CB_W], BF16)
            self.rstd = self.sb("rstd", [128, 512], F32)
            self.lnt = self.sb("lnt", [128, 512], F32)
            self.rstd2 = self.sb("rstd2", [128, 512], F32)
            self.lnt2 = self.sb("lnt2", [128, 512], F32)
            self.sqs = self.sb("sqs", [128, 2, 512], BF16)
            self.big = self.sb("big", [128, 32 * TB], BF16)
            self.gate = self.sb("gate", [128, H, TB], BF16)
            self.OT = self.sb("OT", [128, H, TB], BF16)
            self.tmpf = self.sb("tmpf", [128, 1024], F32)
            self.tmpf2 = self.sb("tmpf2", [128, 1024], F32)
            self.Sst = self.sb("Sst", [128, H, 128], F32)
            self.Sbf = self.sb("Sbf", [128, H, 128], BF16)
            self.ps = [es.enter_context(nc.psum_tensor("ps%d" % i, [128, 512], F32)) for i in range(8)]
            if need_g or need_h:
                self.mxf = self.sb("mxf", [128, 2, 1024], F32)
                self.mxb = self.sb("mxb", [128, 2, 12, 128], BF16)
                self.lbt = self.sb("lbt", [128, 2, 1024], F32)
                self.kdt = self.sb("kdt", [128, 1024], BF16)
                self.smallf = self.sb("smallf", [128, 4, 64], F32)
            if need_g:
                self.pre = self.sb("pre", [128, 1, 4, 3 + TB], BF16)
                self.halo = self.sb("halo", [128, 24, 3], BF16)
                self.Gs = self.sb("Gs", [128, self.NCI, 16], F32)
                import os as _os
                if int(_os.environ.get("PADT", "0")):
                    self.padt = self.sb("padt", [128, int(_os.environ.get("PADT"))], BF16)
                self.nr = self.sb("nr", [128, 2, 2, 256], F32)
                self.ysb = self.sb("ysb", [128, 2, 2, 256], F32)
            self.emit_all()
            self.S.emit(nc, es)
        return nc

    def emit_all(self):
        T = self.T
        self.dma("sp", self.cf[:], self.cf_d[:, 0:CF_W], [], ["cf"], "c0")
        self.dma("pool", self.cb[:], self.cf_d[:, CF_W:CF_W + CB_W], [], ["cb"], "c3")
        self.dma("sp", self.cv[:], self.cv_d[:, :], [], ["cv"], "c1")
        self.dma("sp", self.rw[:], self.rw_d[:, :], [], ["rw"], "c2")
        for s in range(self.nseq):
            t0 = s * T
            for c in range(NCH):
                self.dma("sp", self.hT[:, c, :], self.hin[c * 128:(c + 1) * 128, t0:t0 + T], [], [("h", c)], "hin")
            for l in self.layers:
                if self.skip_mixer:
                    pass
                elif l % 2 == 0:
                    self.gdn_layer(l)
                else:
                    self.hgrn_layer(l)
                self.mlp_layer(l)
            if self.final_norm:
                for b in range(T // self.TB):
                    self.rmsnorm_block(b, CV_OFF["norm_final"], out_f32=True)
            for c in range(NCH):
                self.dma("sp", self.hout[c * 128:(c + 1) * 128, t0:t0 + T], self.hT[:, c, :], [("h", c)], [("hout", s, c)], "hout")
        self.S.add("sp", lambda e: None, r=[("hout", s, c) for s in range(self.nseq) for c in range(NCH)], w=["done"])

    def sumsq_rstd(self, srcs, keys, n, scale, bias, pk=0):
        P0 = self.ps[pk]
        pkey = self.psk(pk, 0, n)
        ns = len(srcs)
        for i, (sap, k) in enumerate(zip(srcs, keys)):
            sqv = self.sqs[:, i % 2, 0:n]
            self.actv(sqv, sap, AF.Square, [k], [("sqs", i % 2)])
            self.mm(P0[:, 0:n], self.cba("ONES"), sqv, i == 0, i == ns - 1, [("sqs", i % 2), "cb"], pkey)
        self.actv(self.lnt[:, 0:n], P0[:, 0:n], AF.Ln, pkey, ["lnt"], scale=scale, bias=bias)

    def norm1(self, st, src, skeys, n, scale, bias):
        bank = 0 if st == 0 else 3
        P0 = self.ps[bank]
        pkey = self.psk(bank)
        lnt = self.lnt if st == 0 else self.lnt2
        rstd = self.rstd if st == 0 else self.rstd2
        lk, rk = ("lnt" if st == 0 else "lnt2"), ("rstd" if st == 0 else "rstd2")
        sqv = self.sqs[:, st, 0:n]
        self.actv(sqv, src, AF.Square, skeys, [("sqs", st)])
        self.mm(P0[:, 0:n], self.cba("ONES"), sqv, True, True, [("sqs", st), "cb"], pkey)
        self.actv(lnt[:, 0:n], P0[:, 0:n], AF.Ln, pkey, [lk], scale=scale, bias=bias)
        self.actv(rstd[:, 0:n], lnt[:, 0:n], AF.Exp, [lk], [rk], scale=-0.5)
        return rstd[:, 0:n], rk

    def rmsnorm_block(self, b, cvoff, out_f32=False):
        TB = self.TB
        tsl = slice(b * TB, (b + 1) * TB)
        self.sumsq_rstd([self.hT[:, c, tsl] for c in range(NCH)], [("h", c) for c in range(NCH)], TB, 1.0 / D, EPS)
        self.actv(self.rstd[:, 0:TB], self.lnt[:, 0:TB], AF.Exp, ["lnt"], ["rstd"], scale=-0.5)
        for c in range(NCH):
            if out_f32:
                self.stt(self.hT[:, c, tsl], self.hT[:, c, tsl], self.cv[:, cvoff + c:cvoff + c + 1], self.rstd[:, 0:TB],
                         ALU.mult, ALU.mult, [("h", c), "rstd", "cv"], [("h", c)])
            else:
                self.stt(self.yT[:, c, :], self.hT[:, c, tsl], self.cv[:, cvoff + c:cvoff + c + 1], self.rstd[:, 0:TB],
                         ALU.mult, ALU.mult, [("h", c), "rstd", "cv"], [("y", c)])

    def proj_fm(self, W, col0, ncols, evac):
        TB = self.TB
        yk = [("y", c) for c in range(NCH)]
        Wv = W.rearrange("(c p) n -> p c n", p=128)
        for s0 in range(0, ncols, 512):
            w_ = min(512, ncols - s0)
            sk, wv = self.load_slab(Wv[:, :, col0 + s0:col0 + s0 + w_], 8, w_)
            for m in range(w_ // 128):
                pi = self.pi
                self.pi += 1
                P = self.ps[1 + (pi % 2)]
                pk = self.psk(1 + (pi % 2))
                for c in range(NCH):
                    self.mm(P[:, 0:TB], wv[:, c, m * 128:(m + 1) * 128], self.yT[:, c, :], c == 0, c == NCH - 1, [sk] + yk, pk)
                evac((s0 // 128) + m, P, pk)

    def proj_tm(self, W, col0, ncols, evac):
        TB = self.TB
        yk = [("y", c) for c in range(NCH)]
        Wv = W.rearrange("(c p) n -> p c n", p=128)
        for s0 in range(0, ncols, 512):
            w_ = min(512, ncols - s0)
            sk, wv = self.load_slab(Wv[:, :, col0 + s0:col0 + s0 + w_], 8, w_)
            for tt_ in range(TB // 128):
                pi = self.pi
                self.pi += 1
                P = self.ps[1 + (pi % 2)]
                pk = self.psk(1 + (pi % 2))
                for c in range(NCH):
                    self.mm(P[:, 0:w_], self.yT[:, c, tt_ * 128:(tt_ + 1) * 128], wv[:, c, :], c == 0, c == NCH - 1, [sk] + yk, pk)
                evac(s0 // 512, tt_, P, pk, w_)

    def out_proj(self, W, src, srck, tsl):
        TB = self.TB
        Wv = W.rearrange("(c p) n -> p c n", p=128)
        for s0 in range(0, D, 512):
            sk, wv = self.load_slab(Wv[:, :, s0:s0 + 512], 8, 512)
            for m in range(4):
                pi = self.pi
                self.pi += 1
                P = self.ps[1 + (pi % 2)]
                pk = self.psk(1 + (pi % 2))
                for c in range(NCH):
                    self.mm(P[:, 0:TB], wv[:, c, m * 128:(m + 1) * 128], src[:, c, :], c == 0, c == NCH - 1, [sk, srck(c)], pk)
                mo = s0 // 128 + m
                self.tt("dve", self.hT[:, mo, tsl], P[:, 0:TB], self.hT[:, mo, tsl], ALU.add, pk + [("h", mo)], [("h", mo)])

    def mlp_layer(self, l):
        TB = self.TB
        hid = self.big[:, 0:32 * TB].rearrange("p (c t) -> p c t", c=32)
        hk = lambda c: self.bigk(c * TB * 2, (c + 1) * TB * 2)
        for b in range(self.T // TB):
            tsl = slice(b * TB, (b + 1) * TB)
            self.rmsnorm_block(b, CV_OFF["norm_mlp"] + 8 * l)

            def evac(j, P, pk):
                tf = self.tmpf if (j % 2 == 0) else self.tmpf2
                tk = "tmpf" if (j % 2 == 0) else "tmpf2"
                self.actv(tf[:, 0:TB], P[:, 0:TB], AF.Relu, pk, [tk])
                self.tt("pool", hid[:, j, :], tf[:, 0:TB], tf[:, 0:TB], ALU.mult, [tk], hk(j))
            self.proj_fm(self.mlp_w_up[l], 0, 4 * D, evac)
            Wd = self.mlp_w_down[l].rearrange("(c p) n -> p c n", p=128)
            for m in range(NCH):
                sk, wv = self.load_slab(Wd[:, :, m * 128:(m + 1) * 128], 32, 128)
                pi = self.pi
                self.pi += 1
                P = self.ps[1 + (pi % 2)]
                pk = self.psk(1 + (pi % 2))
                for kc in range(32):
                    self.mm(P[:, 0:TB], wv[:, kc, :], hid[:, kc, :], kc == 0, kc == 31, [sk] + hk(kc), pk)
                self.tt("dve", self.hT[:, m, tsl], P[:, 0:TB], self.hT[:, m, tsl], ALU.add, pk + [("h", m)], [("h", m)])

    def mxfk(self, hp, c0, c1):
        bounds = [0, 128, 256, 896, 1024]
        return [("mxf", hp, q) for q in range(4) if c0 < bounds[q + 1] and c1 > bounds[q]]

    def mbk(self, hp, s0, s1=None):
        s1 = s0 + 1 if s1 is None else s1
        return [("mxb", hp, s) for s in range(s0, s1)]

    def kdk(self, c0, c1):
        return [("kdt", q) for q in range(c0 // 128, (c1 + 127) // 128)]

    def hgrn_layer(self, l):
        TB, NCI = self.TB, self.NCI
        j = l // 2
        W = self.hgrn_w_in[j]
        mb = self.mxb
        lg = self.big[:, 0:8192].bitcast(F32)
        lgk = self.bigk(0, 16384)
        self.dma("sp", lg, self.lb_d[:, :], [], lgk, "lb")
        self.actv(lg, lg, AF.Exp, lgk, lgk)
        t1 = self.mxf[:, 0, :]
        t2 = self.mxf[:, 1, :]
        k1 = self.mxfk(0, 0, 1024)
        k2 = self.mxfk(1, 0, 1024)
        self.tt("dve", t1, lg[:, 0:1024], lg[:, 1024:2048], ALU.add, lgk, k1)
        self.tt("dve", t1, t1, lg[:, 2048:3072], ALU.add, lgk + k1, k1)
        self.tt("dve", t1, t1, lg[:, 3072:4096], ALU.add, lgk + k1, k1)
        self.op("dve", "reciprocal", k1, k1, out=t1, in_=t1)
        if l == 1:
            self.op("dve", "tensor_copy", lgk, k2, out=t2, in_=lg[:, 1024:2048])
        else:
            self.tt("dve", t2, lg[:, 1024:2048], lg[:, 2048:3072], ALU.add, lgk, k2)
            self.tt("dve", t2, t2, lg[:, 3072:4096], ALU.add, lgk + k2, k2)
        LB = self.lbt[:, 0, :]
        OML = self.lbt[:, 1, :]
        self.tt("dve", LB, t2, t1, ALU.mult, k1 + k2, ["lbt"])
        self.ts("dve", OML, LB, -1.0, 1.0, ALU.mult, ALU.add, ["lbt"], ["oml"])
        self.op("dve", "memset", [], [("Sst", h) for h in range(H)], self.Sst[:], 0.0)
        self.op("pool", "memset", [], [("Sbf", h) for h in range(H)], self.Sbf[:], 0.0)
        gbytes = NCI * 4096
        G = self.big[:, 0:gbytes // 2].bitcast(F32).rearrange("p (c n) -> p c n", c=NCI)
        V = self.big[:, gbytes // 2:gbytes // 2 + NCI * 1024].rearrange("p (c n) -> p c n", c=NCI)
        q0 = gbytes + NCI * 2048
        qT = self.big[:, q0 // 2:q0 // 2 + H * TB].rearrange("p (h t) -> p h t", h=H)
        Gk = lambda ci, half: self.bigk(ci * 4096 + half * 2048, ci * 4096 + (half + 1) * 2048)
        Vk = lambda ci, half: self.bigk(gbytes + ci * 2048 + half * 1024, gbytes + ci * 2048 + (half + 1) * 1024)
        qk = lambda h: self.bigk(q0 + h * TB * 2, q0 + (h + 1) * TB * 2)
        qsc = float(128 ** -0.5)
        for b in range(self.T // TB):
            tsl = slice(b * TB, (b + 1) * TB)
            self.rmsnorm_block(b, CV_OFF["norm_mix"] + 8 * l)

            def evac_q(jj, P, pk):
                self.actv(qT[:, jj, :], P[:, 0:TB], AF.Silu, pk, qk(jj))
                self.ts("pool", qT[:, jj, :], qT[:, jj, :], qsc, None, ALU.mult, ALU.bypass, qk(jj), qk(jj))
            self.proj_fm(W, 0, 1024, evac_q)

            def evac_f(si, tt_, P, pk, w_):
                cs = slice(si * 512, (si + 1) * 512)
                a = self.tmpf[:, 0:512]
                self.actv(a, P[:, 0:512], AF.Exp, pk, ["tmpf"], scale=-1.0)
                self.ts("dve", a, a, 1.0, None, ALU.add, ALU.bypass, ["tmpf"], ["tmpf"])
                self.op("dve", "reciprocal", ["tmpf"], ["tmpf"], out=a, in_=a)
                self.tt("dve", a, a, OML[:, cs], ALU.mult, ["tmpf", "oml"], ["tmpf"])
                self.tt("dve", a, a, LB[:, cs], ALU.add, ["tmpf", "lbt"], ["tmpf"])
                self.actv(G[:, tt_, cs], a, AF.Ln, ["tmpf"], Gk(tt_, si))
            self.proj_tm(W, 1024, 1024, evac_f)
            self.proj_tm(W, 2048, 1024, lambda si, tt_, P, pk, w_: self.actv(V[:, tt_, si * 512:(si + 1) * 512], P[:, 0:512], AF.Copy, pk, Vk(tt_, si)))
            self.proj_fm(W, 3072, 1024, lambda jj, P, pk: self.actv(self.gate[:, jj, :], P[:, 0:TB], AF.Silu, pk, [("gate", jj)]))
            for ci in range(NCI):
                csl = slice(ci * 128, (ci + 1) * 128)
                for half in range(2):
                    hs = slice(half * 512, (half + 1) * 512)
                    P = self.ps[3]
                    p3 = self.psk(3)
                    self.mm(P[:, :], self.cfa("SUP"), G[:, ci, hs], True, True, Gk(ci, half) + ["cf"], p3)
                    e1 = self.tmpf[:, 0:512]
                    e2 = self.tmpf2[:, 0:512]
                    self.actv(e1, G[:, ci, hs], AF.Exp, Gk(ci, half), ["tmpf"])
                    self.actv(e2, P[:, :], AF.Exp, p3, ["tmpf2"])
                    self.ts("dve", e1, e1, -1.0, 1.0, ALU.mult, ALU.add, ["tmpf"], ["tmpf"])
                    self.tt("dve", self.kdt[:, hs], e1, e2, ALU.mult, ["tmpf", "tmpf2"], self.kdk(half * 512, (half + 1) * 512))
                def head(h, ci=ci, csl=csl):
                    hp = h % 2
                    hsl = slice(h * 128, (h + 1) * 128)
                    PE0, PE1 = self.ps[4 + 2 * hp], self.ps[5 + 2 * hp]
                    pk0, pk1 = self.psk(4 + 2 * hp), self.psk(5 + 2 * hp, 0, 384)
                    RH = self.cfa("RH", 896)
                    self.mm(PE0[:, :], G[:, ci, hsl], RH[:, 0:512], True, True, Gk(ci, h // 4) + ["cf"], pk0)
                    self.mm(PE1[:, 0:384], G[:, ci, hsl], RH[:, 512:896], True, True, Gk(ci, h // 4) + ["cf"], pk1)
                    E = self.mxf[:, hp, 0:896]
                    ek = self.mxfk(hp, 0, 896)
                    self.actv(E[:, 0:512], PE0[:, :], AF.Exp, pk0, self.mxfk(hp, 0, 512))
                    self.actv(E[:, 512:896], PE1[:, 0:384], AF.Exp, pk1, self.mxfk(hp, 512, 896))
                    kTf = self.mxf[:, hp, 896:1024]
                    kfk = self.mxfk(hp, 896, 1024)
                    self.ts("dve", kTf, E[:, 768:896], -1.0, 1.0, ALU.mult, ALU.add, ek, kfk)
                    KT4 = mb[:, hp, 0:4, :]
                    self.tt("dve", KT4, E[:, 0:512].rearrange("p (r n) -> p r n", r=4),
                            kTf.unsqueeze(1).to_broadcast([128, 4, 128]), ALU.mult, ek + kfk, self.mbk(hp, 0, 4))
                    QQ = mb[:, hp, 4:6, :]
                    self.tt("dve", QQ, E[:, 512:768].rearrange("p (r n) -> p r n", r=2),
                            qT[:, h, csl].unsqueeze(1).to_broadcast([128, 2, 128]), ALU.mult, ek + qk(h), self.mbk(hp, 4, 6))
                    PA = self.ps[1 + hp]
                    pa = self.psk(1 + hp)
                    for r4 in range(4):
                        self.mm(PA[:, r4 * 32:(r4 + 1) * 32], KT4[:, r4, :], QQ[:, 0, r4 * 32:(r4 + 1) * 32], True, True,
                                self.mbk(hp, 0, 6), pa)
                    AT = mb[:, hp, 6, :]
                    self.tt("dve", AT, PA[:, 0:128], self.cba("MASKT"), ALU.mult, pa + ["cb"], self.mbk(hp, 6))
                    PO = self.ps[0 if hp == 0 else 3]
                    po0 = po1 = self.psk(0 if hp == 0 else 3)
                    self.mm(PO[:, 0:128], V[:, ci, hsl], AT, True, False, Vk(ci, h // 4) + self.mbk(hp, 6), po0)
                    self.mm(PO[:, 0:128], self.Sbf[:, h, :], QQ[:, 1, :], False, True, [("Sbf", h)] + self.mbk(hp, 5), po0)
                    self.actv(self.OT[:, h, csl], PO[:, 0:128], AF.Copy, po0, [("OT", h)])
                    self.mm(PO[:, 128:256], self.kdt[:, hsl], V[:, ci, hsl], True, True, self.kdk(h * 128, (h + 1) * 128) + Vk(ci, h // 4), po1)
                    self.stt(self.Sst[:, h, :], self.Sst[:, h, :], E[:, 767:768], PO[:, 128:256], ALU.mult, ALU.add,
                             [("Sst", h)] + ek + po1, [("Sst", h)])
                    self.actv(self.Sbf[:, h, :], self.Sst[:, h, :], AF.Copy, [("Sst", h)], [("Sbf", h)])

                for h0 in range(0, H, 2):
                    self.S.interleave([lambda h0=h0: head(h0), lambda h0=h0: head(h0 + 1)])
            self.sumsq_rstd([self.OT[:, h, :] for h in range(H)], [("OT", h) for h in range(H)], TB, 1.0 / D, EPS)
            self.actv(self.rstd[:, 0:TB], self.lnt[:, 0:TB], AF.Exp, ["lnt"], ["rstd"], scale=-0.5)
            go = CV_OFF["gnorm"] + 8 * j
            for h in range(H):
                self.stt(self.OT[:, h, :], self.OT[:, h, :], self.cv[:, go + h:go + h + 1], self.rstd[:, 0:TB], ALU.mult, ALU.mult,
                         [("OT", h), "cv", "rstd"], [("OT", h)])
                self.tt("pool", self.OT[:, h, :], self.OT[:, h, :], self.gate[:, h, :], ALU.mult, [("OT", h), ("gate", h)], [("OT", h)])
            self.out_proj(self.hgrn_w_out[j], self.OT, lambda c: ("OT", c), tsl)

    def gdn_layer(self, l):
        TB, NCI = self.TB, self.NCI
        j = l // 2
        W = self.gdn_w_in[j]
        QKV = self.big[:, 0:24 * TB].rearrange("p (c t) -> p c t", c=24)
        ck = lambda c: self.bigk(c * TB * 2, (c + 1) * TB * 2)
        nega = self.smallf[:, 0, 0:8]
        self.actv(nega, self.rw[:, 8 * j:8 * j + 8], AF.Exp, ["rw"], ["nega"])
        self.ts("dve", nega, nega, -1.0, None, ALU.mult, ALU.bypass, ["nega"], ["nega"])
        self.op("dve", "memset", [], [("Sst", h) for h in range(H)], self.Sst[:], 0.0)
        self.op("pool", "memset", [], [("Sbf", h) for h in range(H)], self.Sbf[:], 0.0)
        self.op("pool", "memset", [], [("halo", c) for c in range(24)], self.halo[:], 0.0)
        cvo = CV_OFF["conv"] + j * 96
        for b in range(self.T // TB):
            tsl = slice(b * TB, (b + 1) * TB)
            self.rmsnorm_block(b, CV_OFF["norm_mix"] + 8 * l)

            def evac_qkv(jj, P, pk):
                pp = 0
                slot = jj % 4
                pv = self.pre[:, pp, slot, :]
                pkk = ("pre", pp, slot)
                self.actv(pv[:, 3:3 + TB], P[:, 0:TB], AF.Copy, pk, [pkk])
                self.op("pool", "tensor_copy", [("halo", jj)], [pkk], out=pv[:, 0:3], in_=self.halo[:, jj, :])
                acc = self.tmpf[:, 0:TB] if jj % 2 == 0 else self.tmpf2[:, 0:TB]
                ak = "tmpf" if jj % 2 == 0 else "tmpf2"
                wc = cvo + jj * 4
                self.ts("dve", acc, pv[:, 0:TB], self.cv[:, wc:wc + 1], None, ALU.mult, ALU.bypass, [pkk, "cv"], [ak])
                for tap in range(1, 4):
                    self.stt(acc, pv[:, tap:tap + TB], self.cv[:, wc + tap:wc + tap + 1], acc, ALU.mult, ALU.add, [pkk, "cv", ak], [ak])
                self.op("pool", "tensor_copy", [pkk], [("halo", jj)], out=self.halo[:, jj, :], in_=pv[:, TB:TB + 3])
                self.actv(QKV[:, jj, :], acc, AF.Silu, [ak], ck(jj))
            self.proj_fm(W, 0, 3072, evac_qkv)
            self.proj_fm(W, 3072, 1024, lambda jj, P, pk: self.actv(self.gate[:, jj, :], P[:, 0:TB], AF.Silu, pk, [("gate", jj)]))
            if self.dbg_stop == 1:
                continue

            def evac_ab(si, tt_, P, pk, w_):
                x1 = self.smallf[:, 1, 0:16]
                self.tt("dve", x1[:, 0:8], P[:, 0:8], self.rw[:, 16 + 8 * j:24 + 8 * j], ALU.add, pk + ["rw"], ["x1"])
                self.ts("dve", x1[:, 8:16], P[:, 8:16], -1.0, None, ALU.mult, ALU.bypass, pk, ["x1b"])
                self.actv(x1, x1, AF.Exp, ["x1", "x1b"], ["x1", "x1b"])
                self.actv(x1, x1, AF.Ln, ["x1", "x1b"], ["x1", "x1b"], bias=1.0)
                self.tt("dve", self.Gs[:, tt_, 0:8], x1[:, 0:8], nega, ALU.mult, ["x1", "nega"], [("Gs", tt_)])
                self.ts("dve", self.Gs[:, tt_, 8:16], x1[:, 8:16], -1.0, None, ALU.mult, ALU.bypass, ["x1b"], [("Gsb", tt_)])
            self.proj_tm(W, 4096, 16, evac_ab)
            if self.dbg_stop == 2:
                continue
            def l2n(c, st):
                rs, rk = self.norm1(st, QKV[:, c, :], ck(c), TB, 1.0, EPS)
                if c < 8:
                    self.stt(QKV[:, c, :], QKV[:, c, :], float(128 ** -0.5), rs, ALU.mult, ALU.mult, ck(c) + [rk], ck(c))
                else:
                    self.tt("dve", QKV[:, c, :], QKV[:, c, :], rs, ALU.mult, ck(c) + [rk], ck(c))
            for c0 in range(0, 16, 2):
                self.S.interleave([lambda c0=c0: l2n(c0, 0), lambda c0=c0: l2n(c0 + 1, 1)])
            if self.dbg_stop == 3:
                continue
            for ci in range(NCI):
                self.gdn_chunk(ci, QKV, ck)
            if self.dbg_stop >= 4:
                continue
            oo = CV_OFF["onorm"] + j
            def hnorm(h, st):
                rs, rk = self.norm1(st, self.OT[:, h, :], [("OT", h)], TB, 1.0 / 128, EPS)
                self.stt(self.OT[:, h, :], self.OT[:, h, :], self.cv[:, oo:oo + 1], rs, ALU.mult, ALU.mult,
                         [("OT", h), "cv", rk], [("OT", h)])
                self.tt("pool", self.OT[:, h, :], self.OT[:, h, :], self.gate[:, h, :], ALU.mult, [("OT", h), ("gate", h)], [("OT", h)])
            for h0 in range(0, H, 2):
                self.S.interleave([lambda h0=h0: hnorm(h0, 0), lambda h0=h0: hnorm(h0 + 1, 1)])
            self.out_proj(self.gdn_w_out[j], self.OT, lambda c: ("OT", c), tsl)

    def gdn_chunk(self, ci, QKV, ck):
        mb = self.mxb
        csl = slice(ci * 128, (ci + 1) * 128)
        g = self.Gs[:, ci, 0:8]
        lnb = self.Gs[:, ci, 8:16]
        gk, lk = ("Gs", ci), ("Gsb", ci)
        PSm = self.ps[3]
        p3a = self.psk(3, 0, 128)
        self.mm(PSm[:, 0:8], self.cfa("LT"), g, True, False, [gk, "cf"], p3a)
        self.mm(PSm[:, 0:8], self.cfa("ID"), lnb, False, True, [lk, "cf"], p3a)
        self.mm(PSm[:, 8:16], self.cfa("SUP"), g, True, True, [gk, "cf"], p3a)
        self.mm(PSm[:, 16:24], self.cfa("ONES"), g, True, True, [gk, "cf"], p3a)
        self.mm(PSm[:, 24:32], self.cfa("ID"), lnb, True, True, [lk, "cf"], p3a)
        EG = self.smallf[:, 2, 0:32]
        self.actv(EG, PSm[:, 0:32], AF.Exp, p3a, ["EG"])
        negb = self.smallf[:, 3, 0:8]
        self.ts("dve", negb, EG[:, 24:32], -1.0, None, ALU.mult, ALU.bypass, ["EG"], ["negb"])
        if self.dbg_stop == 4:
            return
        nr, ys = self.nr, self.ysb
        marks = {}

        def head(h):
            hp = h % 2
            qTh, kTh, vTh = QKV[:, h, csl], QKV[:, 8 + h, csl], QKV[:, 16 + h, csl]
            qk_, kk_, vk_ = ck(h), ck(8 + h), ck(16 + h)
            b3 = 0 if hp == 0 else 3
            PT = self.ps[b3][:, :].bitcast(BF16)
            p3b = p3c = p3d = self.psk(b3)
            Gb = self.mxf[:, hp, 0:128]
            Gb2 = self.mxf[:, hp, 128:256]
            gbk, gb2k = self.mxfk(hp, 0, 128), self.mxfk(hp, 128, 256)
            self.ts("dve", Gb, self.cfa("SUP"), g[:, h:h + 1], None, ALU.mult, ALU.bypass, ["cf", gk], gbk)
            self.ts("dve", Gb2, self.cfa("LT"), g[:, h:h + 1], None, ALU.mult, ALU.bypass, ["cf", gk], gb2k)
            PD = self.ps[4 + hp]
            pd0, pd1, pd2 = self.psk(4 + hp, 0, 128), self.psk(4 + hp, 128, 256), self.psk(4 + hp, 256, 384)
            self.mm(PD[:, 0:128], self.cfa("LT"), Gb, True, False, ["cf"] + gbk, pd0)
            self.mm(PD[:, 0:128], self.cba("ID"), self.cba("NEGS"), False, True, ["cb"], pd0)
            self.mm(PD[:, 128:256], self.cfa("SUP"), Gb2, True, False, ["cf"] + gb2k, pd1)
            self.mm(PD[:, 128:256], self.cba("ID"), self.cba("NEGCT"), False, True, ["cb"], pd1)
            self.mm(PD[:, 256:384], self.cfa("ONES"), Gb2, True, True, ["cf"] + gb2k, pd2)
            DEC = mb[:, hp, 0:3, :]
            self.actv(DEC[:, 1:3, :], PD[:, 128:384].rearrange("p (a n) -> p a n", a=2), AF.Exp, pd1 + pd2, self.mbk(hp, 1, 3))
            fz = self.mxf[:, hp, :]
            zk = self.mxfk(hp, 256, 896)
            decS, N, No, U64, T64 = fz[:, 256:384], fz[:, 384:512], fz[:, 512:640], fz[:, 640:768], fz[:, 768:896]
            Pm = fz[:, 896:1024]
            pmk = self.mxfk(hp, 896, 1024)
            self.actv(decS, PD[:, 0:128], AF.Exp, pd0, zk)
            PK = self.ps[6 + hp]
            pq0, pq1, pq2, pq3 = (self.psk(6 + hp, 0, 128), self.psk(6 + hp, 128, 256), self.psk(6 + hp, 256, 384),
                                  self.psk(6 + hp, 384, 512))
            self.mm(PK[:, 0:128], kTh, kTh, True, True, kk_, pq0)
            self.mm(PK[:, 128:256], kTh, qTh, True, True, kk_ + qk_, pq1)
            self.stt(N, PK[:, 0:128], negb[:, h:h + 1], decS, ALU.mult, ALU.mult, pq0 + ["negb"] + zk, zk)
            AT = mb[:, hp, 4, :]
            self.tt("dve", AT, PK[:, 128:256], DEC[:, 1, :], ALU.mult, pq1 + self.mbk(hp, 1), self.mbk(hp, 4))
            Nd = nr[:, hp, 0, 0:128]
            self.tt("pool", Nd, N, self.cba("BD"), ALU.mult, zk + ["cb"], [("nrN", hp, 0)])
            self.tt("pool", No, N, self.cba("OD"), ALU.mult, zk + ["cb"], zk)
            P3 = self.ps[b3]
            self.op("pe", "transpose", [("nrN", hp, 0), "cf"], p3b, P3[:, 128:256], Nd, self.cfa("ID"))
            Yd = ys[:, hp, 0, 0:128]
            self.cp(Yd, P3[:, 128:256], p3b, [("ysY", hp, 0)])
            self.mm(PK[:, 256:384], Nd, Yd, True, True, [("nrN", hp, 0), ("ysY", hp, 0)], pq2)
            self.mm(PK[:, 384:512], Yd, Nd, True, True, [("nrN", hp, 0), ("ysY", hp, 0)], pq3)
            self.cp(ys[:, hp, 1, 0:128], PK[:, 256:384], pq2, [("ysY", hp, 1)])
            self.tt("pool", ys[:, hp, 1, 128:256], Yd, self.cfa("ID"), ALU.add, [("ysY", hp, 0), "cf"], [("ysS", hp, 1)])
            self.cp(nr[:, hp, 1, 0:128], PK[:, 384:512], pq3, [("nrN", hp, 1)])
            self.tt("pool", nr[:, hp, 1, 128:256], Nd, self.cfa("ID"), ALU.add, [("nrN", hp, 0), "cf"], [("nrR", hp, 1)])
            cur = 1
            PA_, PB_ = self.ps[4 + hp], self.ps[6 + hp]
            pa01, pb01 = pd0 + pd1, pq0 + pq1
            for lev in range(1, 5):
                nx = 1 - cur
                rk = [("nrN", hp, cur), ("nrR", hp, cur), ("ysY", hp, cur), ("ysS", hp, cur)]
                self.mm(PA_[:, 0:256], nr[:, hp, cur, 0:128], ys[:, hp, cur, :], True, True, rk, pa01)
                self.mm(PB_[:, 0:256], ys[:, hp, cur, 0:128], nr[:, hp, cur, :], True, True, rk, pb01)
                self.cp(ys[:, hp, nx, 0:128], PA_[:, 0:128], pd0, [("ysY", hp, nx)])
                self.tt("dve", ys[:, hp, nx, 128:256], PA_[:, 128:256], ys[:, hp, cur, 128:256], ALU.add, pd1 + [("ysS", hp, cur)], [("ysS", hp, nx)])
                self.cp(nr[:, hp, nx, 0:128], PB_[:, 0:128], pq0, [("nrN", hp, nx)])
                self.tt("dve", nr[:, hp, nx, 128:256], PB_[:, 128:256], nr[:, hp, cur, 128:256], ALU.add, pq1 + [("nrR", hp, cur)], [("nrR", hp, nx)])
                cur = nx
            rk = [("nrN", hp, cur), ("nrR", hp, cur), ("ysY", hp, cur), ("ysS", hp, cur)]
            self.mm(PA_[:, 0:128], nr[:, hp, cur, 0:128], ys[:, hp, cur, 128:256], True, True, rk, pd0)
            self.mm(PB_[:, 0:128], ys[:, hp, cur, 0:128], nr[:, hp, cur, 128:256], True, True, rk, pq0)
            self.tt("dve", U64, PA_[:, 0:128], ys[:, hp, cur, 128:256], ALU.add, pd0 + [("ysS", hp, cur)], zk)
            self.tt("dve", T64, PB_[:, 0:128], nr[:, hp, cur, 128:256], ALU.add, pq0 + [("nrR", hp, cur)], zk)
            self.mm(PA_[:, 0:128], No, U64, True, True, zk, pd0)
            self.cp(Pm, PA_[:, 0:128], pd0, pmk)
            self.mm(PB_[:, 0:128], T64, Pm, True, True, zk + pmk, pq0)
            U = mb[:, hp, 9, :]
            self.tt("dve", U, PB_[:, 0:128], U64, ALU.add, pq0 + zk, self.mbk(hp, 9))
            if self.dbg_stop == 8:
                return
            self.tr(PT[:, 512:640], kTh, kk_, p3c)
            kbgn = mb[:, hp, 10, :]
            kdec = mb[:, hp, 11, :]
            self.ts("dve", kbgn, PT[:, 512:640], EG[:, h:h + 1], -1.0, ALU.mult, ALU.mult, p3c + ["EG"], self.mbk(hp, 10))
            self.ts("dve", kdec, PT[:, 512:640], EG[:, 8 + h:9 + h], None, ALU.mult, ALU.bypass, p3c + ["EG"], self.mbk(hp, 11))
            self.tr(PT[:, 768:896], vTh, vk_, p3d)
            c_vb, c_w, c_vn, c_qd = hp * 128, 256 + hp * 128, 512 + hp * 128, 768 + hp * 128
            vb = self.kdt[:, c_vb:c_vb + 128]
            self.ts("dve", vb, PT[:, 768:896], EG[:, 24 + h:25 + h], None, ALU.mult, ALU.bypass, p3d + ["EG"], self.kdk(c_vb, c_vb + 128))
            qd = self.kdt[:, c_qd:c_qd + 128]
            qdk = self.kdk(c_qd, c_qd + 128)
            self.tt("pool", qd, qTh, DEC[:, 2, :], ALU.mult, qk_ + self.mbk(hp, 2), qdk)
            marks[h] = len(self.S.recording)
            PO = self.ps[1 + hp]
            po = [self.psk(1 + hp) for q in range(4)]
            self.mm(PO[:, 0:128], kbgn, U, True, True, self.mbk(hp, 10) + self.mbk(hp, 9), po[0])
            wTn = self.kdt[:, c_w:c_w + 128]
            self.cp(wTn, PO[:, 0:128], po[0], self.kdk(c_w, c_w + 128))
            self.mm(PO[:, 128:256], U, vb, True, False, self.mbk(hp, 9) + self.kdk(c_vb, c_vb + 128), po[1])
            self.mm(PO[:, 128:256], wTn, self.Sbf[:, h, :], False, True, self.kdk(c_w, c_w + 128) + [("Sbf", h)], po[1])
            vnew = self.kdt[:, c_vn:c_vn + 128]
            vnk = self.kdk(c_vn, c_vn + 128)
            self.cp(vnew, PO[:, 128:256], po[1], vnk)
            self.mm(PO[:, 256:384], self.Sbf[:, h, :], qd, True, False, [("Sbf", h)] + qdk, po[2])
            self.mm(PO[:, 256:384], vnew, AT, False, True, vnk + self.mbk(hp, 4), po[2])
            self.cp(self.OT[:, h, csl], PO[:, 256:384], po[2], [("OT", h)])
            self.mm(PO[:, 384:512], kdec, vnew, True, True, self.mbk(hp, 11) + vnk, po[3])
            self.stt(self.Sst[:, h, :], self.Sst[:, h, :], EG[:, 16 + h:17 + h], PO[:, 384:512], ALU.mult, ALU.add,
                     [("Sst", h), "EG"] + po[3], [("Sst", h)])
            self.cp(self.Sbf[:, h, :], self.Sst[:, h, :], [("Sst", h)], [("Sbf", h)])

        fronts, tails = [], []
        for h in range(H):
            self.S.recording = []
            head(h)
            L = self.S.recording
            self.S.recording = None
            fronts.append(L[:marks[h]])
            tails.append(L[marks[h]:])
        self.S.merge_lists([fronts[0], fronts[1]])
        for p in range(3):
            self.S.merge_lists([tails[2 * p], tails[2 * p + 1], fronts[2 * p + 2], fronts[2 * p + 3]])
        self.S.merge_lists([tails[6], tails[7]])


def make_in_map(inp, hT, prog):
    m = {"hT_in": np.ascontiguousarray(hT, dtype=np.float32), "cf": _consts(), "cvec": _make_cvec(inp), "rows": _make_rows(inp)}
    need_g, need_h = prog.need_g, prog.need_h
    if need_g:
        m["gdn_w_in"] = inp["gdn_w_in"]
        m["gdn_w_out"] = inp["gdn_w_out"]
    if need_h:
        m["hgrn_w_in"] = inp["hgrn_w_in"]
        m["hgrn_w_out"] = inp["hgrn_w_out"]
        m["lbrows"] = _make_lbrows(inp)
    if prog.layers:
        m["mlp_w_up"] = inp["mlp_w_up"]
        m["mlp_w_down"] = inp["mlp_w_down"]
    return m


N_CORES = 8
_CFG = {"T": 2048, "TB": 512, "nseq": 2, "groups": [[0, 1, 2, 3]], "skip_mixer": False}


def kernel(**inp):
    inp = {k: np.asarray(v) for k, v in inp.items()}
    x = inp["x"]
    B, T, Dm = x.shape
    nseq = B // N_CORES
    hT = [np.ascontiguousarray(x[c * nseq:(c + 1) * nseq].reshape(nseq * T, Dm).T) for c in range(N_CORES)]
    groups = _CFG["groups"]
    for gi, layers in enumerate(groups):
        p = Prog(nseq, T, _CFG["TB"], layers, final_norm=(gi == len(groups) - 1))
        p.skip_mixer = _CFG["skip_mixer"]
        nc = p.build()
        base = make_in_map(inp, hT[0], p)
        in_maps = []
        for c in range(N_CORES):
            m = dict(base)
            m["hT_in"] = hT[c]
            in_maps.append(m)
        res = run_bass_kernel_spmd(nc, in_maps, core_ids=list(range(N_CORES)))
        hT = [np.ascontiguousarray(r["hT_out"]) for r in res.results]
    out = np.stack([h.T.reshape(nseq, T, Dm) for h in hT], axis=0).reshape(B, T, Dm)
    return out.astype(np.float32)
```

```python
import numpy as np
from contextlib import ExitStack
import concourse.bass as bass
import concourse.mybir as mybir
from concourse.bass_utils import run_bass_kernel_spmd

F32 = mybir.dt.float32
BF16 = mybir.dt.bfloat16
AF = mybir.ActivationFunctionType
ALU = mybir.AluOpType

D = 1024
NCH = 8
H = 8
C = 128
EPS = 1e-6
GDN_IN = 4112
MAXV = 30000


class _Op:
    __slots__ = ("eng", "fn", "deps", "kind", "stream", "sig", "upg")

    def __init__(self, eng, fn, deps, kind, stream):
        self.eng, self.fn, self.deps, self.kind, self.stream, self.sig = eng, fn, deps, kind, stream, None


class Sched:
    ENGS = ("pe", "act", "dve", "pool", "sp")

    def __init__(self):
        self.ops = []
        self.lastw = {}
        self.readers = {}
        self.stream_last = {}
        self.recording = None

    def merge_lists(self, lists):
        n = max(len(l) for l in lists)
        for i in range(n):
            for l in lists:
                if i < len(l):
                    self.add(*l[i])

    def interleave(self, bodies):
        lists = []
        for b in bodies:
            self.recording = []
            b()
            lists.append(self.recording)
            self.recording = None
        n = max(len(l) for l in lists)
        for i in range(n):
            for l in lists:
                if i < len(l):
                    self.add(*l[i])

    def add(self, eng, fn, r=(), w=(), kind="c", stream=None):
        if self.recording is not None:
            self.recording.append((eng, fn, tuple(r), tuple(w), kind, stream))
            return -1
        i = len(self.ops)
        deps = {}

        def upd(d, t):
            o = deps.get(d)
            if o is None or (t == "raw") or (t == "waw" and o == "war"):
                deps[d] = t
        for k in r:
            if k in self.lastw:
                upd(self.lastw[k], "raw")
        for k in w:
            if k in self.lastw:
                upd(self.lastw[k], "waw")
            for d in self.readers.get(k, ()):
                if d != i:
                    upd(d, "war")
        op = _Op(eng, fn, deps, kind, stream)
        op.upg = dict(self.stream_last)
        self.ops.append(op)
        if kind == "d":
            self.stream_last[stream] = i
        for k in r:
            self.readers.setdefault(k, []).append(i)
        for k in w:
            self.lastw[k] = i
            self.readers[k] = []
        return i

    def dma(self, eng, fn, r, w, stream):
        return self.add(eng, fn, r, w, kind="d", stream=stream)

    def _needs_sync(self, op, D, t):
        if D.kind == "d":
            return True
        if D.eng != op.eng:
            return True
        if op.kind == "d":
            return True
        if op.eng == "pe":
            return False
        return True

    def emit(self, nc, es):
        ops = self.ops
        need = [False] * len(ops)
        for op in ops:
            for d, t in op.deps.items():
                if self._needs_sync(op, ops[d], t):
                    need[d] = True
        cnt = {}
        semkeys = []

        def nxt(key, inc):
            st = cnt.get(key)
            if st is None:
                st = cnt[key] = [0, 0]
                semkeys.append((key, 0))
            if st[1] + inc > MAXV:
                st[0] += 1
                st[1] = 0
                semkeys.append((key, st[0]))
            st[1] += inc
            return (key, st[0]), st[1]
        for i, op in enumerate(ops):
            if op.kind == "d":
                op.sig = nxt(("dma", op.stream), 16)
            elif need[i]:
                op.sig = nxt(("eng", op.eng), 1)
        sems = {}
        for sk in semkeys:
            nm = "s_%s_%s_%d" % (sk[0][0], str(sk[0][1]), sk[1])
            sems[sk] = es.enter_context(nc.semaphore(nm))
        self.nsems = len(sems)
        block = es.enter_context(nc.Block())
        per = {e: [] for e in self.ENGS}
        for i, op in enumerate(ops):
            per[op.eng].append(i)

        def run(eng_name, engine):
            waited = {}
            for i in per[eng_name]:
                op = ops[i]
                for d, t in sorted(op.deps.items()):
                    Dop = ops[d]
                    if not self._needs_sync(op, Dop, t):
                        continue
                    if Dop.kind == "d":
                        Dop = ops[op.upg[Dop.stream]]
                    sk, val = Dop.sig
                    if waited.get(sk, 0) >= val:
                        continue
                    engine.wait_ge(sems[sk], val)
                    waited[sk] = val
                ins = op.fn(engine)
                if op.sig is not None:
                    assert ins is not None
                    ins.then_inc(sems[op.sig[0]], 16 if op.kind == "d" else 1)

        @block.tensor
        def _(e):
            run("pe", e)

        @block.scalar
        def _(e):
            run("act", e)

        @block.vector
        def _(e):
            run("dve", e)

        @block.gpsimd
        def _(e):
            run("pool", e)

        @block.sync
        def _(e):
            run("sp", e)


def _consts():
    k = np.arange(128)[:, None]
    j = np.arange(128)[None, :]
    f = np.float32
    LT = (k <= j).astype(f)
    SUP = (k > j).astype(f)
    ID = (k == j).astype(f)
    ONES = np.ones((128, 128), f)
    NEG = -30000.0
    NEGS = np.where(k > j, 0.0, NEG).astype(f)
    NEGCT = np.where(j >= k, 0.0, NEG).astype(f)
    BD = ((k // 64) == (j // 64)).astype(f)
    OD = 1.0 - BD
    MASKT = (k <= j).astype(f)
    Ms = []
    for r in range(4):
        lo, hi = 32 * r, 32 * (r + 1)
        M = np.zeros((128, 128), f)
        M += ((j < lo) & (k > j) & (k < lo)) * 1.0
        M -= ((j >= lo) & (j < hi) & (k >= lo) & (k <= j)) * 1.0
        Ms.append(M.astype(f))
    MQ = ((k >= (j // 32) * 32) & (k <= j)).astype(f)
    RH = np.concatenate(Ms + [MQ, LT, ID], axis=1)
    cf = np.concatenate([LT, SUP, ID, ONES, RH, ID, ONES, NEGS, NEGCT, BD, OD, MASKT], axis=1)
    return np.ascontiguousarray(cf.astype(f))


CF_OFF = {"LT": 0, "SUP": 128, "ID": 256, "ONES": 384, "RH": 512}
CF_W = 512 + 896
CB_OFF = {"ID": 0, "ONES": 128, "NEGS": 256, "NEGCT": 384, "BD": 512, "OD": 640, "MASKT": 768}
CB_W = 896


def _cvec_layout():
    off = {}
    n = 0
    for nm, w in (("norm_mix", 32), ("norm_mlp", 32), ("norm_final", 8), ("conv", 2 * 24 * 4),
                  ("onorm", 2), ("gnorm", 16)):
        off[nm] = n
        n += w
    return off, n


CV_OFF, CV_W = _cvec_layout()


def _make_cvec(inp):
    cv = np.zeros((128, CV_W), np.float32)
    cv[:, CV_OFF["norm_mix"]:CV_OFF["norm_mix"] + 32] = inp["norm_mix"].reshape(4, 8, 128).transpose(2, 0, 1).reshape(128, 32)
    cv[:, CV_OFF["norm_mlp"]:CV_OFF["norm_mlp"] + 32] = inp["norm_mlp"].reshape(4, 8, 128).transpose(2, 0, 1).reshape(128, 32)
    cv[:, CV_OFF["norm_final"]:CV_OFF["norm_final"] + 8] = inp["norm_final"].reshape(8, 128).T
    cw = inp["gdn_conv"].reshape(2, 4, 24, 128).transpose(3, 0, 2, 1).reshape(128, 2 * 24 * 4)
    cv[:, CV_OFF["conv"]:CV_OFF["conv"] + 192] = cw
    cv[:, CV_OFF["onorm"]:CV_OFF["onorm"] + 2] = inp["gdn_onorm"].T
    cv[:, CV_OFF["gnorm"]:CV_OFF["gnorm"] + 16] = inp["hgrn_gnorm"].reshape(2, 8, 128).transpose(2, 0, 1).reshape(128, 16)
    return cv


RW_W = 32


def _make_rows(inp):
    r = np.zeros((RW_W,), np.float32)
    r[0:16] = inp["gdn_a_log"].reshape(-1)
    r[16:32] = inp["gdn_dt_bias"].reshape(-1)
    return np.ascontiguousarray(np.broadcast_to(r[None, :], (128, RW_W)))


def _make_lbrows(inp):
    r = inp["hgrn_lb_logits"].reshape(-1).astype(np.float32)
    return np.ascontiguousarray(np.broadcast_to(r[None, :], (128, 4096)))


class Prog:
    def __init__(self, nseq, T, TB, layers, final_norm):
        self.nseq, self.T, self.TB, self.layers, self.final_norm = nseq, T, TB, list(layers), final_norm
        self.NTOK = nseq * T
        self.NCI = TB // 128
        self.S = Sched()
        self.nc = bass.Bass("TRN2", target_bir_lowering=False)
        self.slab_i = 0
        self.pi = 0
        self.skip_mixer = False
        self.dbg_stop = 0
        self.dbg2 = 9
        self.dbg3 = 0

    def dram_in(self, name, shape, dt=F32):
        return self.nc.dram_tensor(name, list(shape), dt, kind="ExternalInput").ap()

    def sb(self, name, shape, dt):
        return self.es.enter_context(self.nc.sbuf_tensor(name, list(shape), dt))

    def op(self, eng, meth, r, w, *a, **kw):
        return self.S.add(eng, lambda e: getattr(e, meth)(*a, **kw), r, w)

    def mm(self, out, lhsT, rhs, start, stop, r, w):
        return self.op("pe", "matmul", r, w, out, lhsT=lhsT, rhs=rhs, start=start, stop=stop)

    def tr(self, out, in_, r, w):
        return self.op("pe", "transpose", r + ["cb"], w, out, in_, self.cba("ID"))

    def actv(self, out, in_, func, r, w, **kw):
        return self.op("act", "activation", r, w, out=out, in_=in_, func=func, **kw)

    def cp(self, out, in_, r, w):
        return self.op("dve", "tensor_copy", r, w, out=out, in_=in_)

    def tt(self, eng, out, in0, in1, op, r, w):
        return self.op(eng, "tensor_tensor", r, w, out=out, in0=in0, in1=in1, op=op)

    def ts(self, eng, out, in0, s1, s2, op0, op1, r, w):
        return self.op(eng, "tensor_scalar", r, w, out=out, in0=in0, scalar1=s1, scalar2=s2, op0=op0, op1=op1)

    def stt(self, out, in0, scalar, in1, op0, op1, r, w):
        return self.op("dve", "scalar_tensor_tensor", r, w, out=out, in0=in0, scalar=scalar, in1=in1, op0=op0, op1=op1)

    def dma(self, eng, out, in_, r, w, stream):
        return self.S.dma(eng, lambda e: e.dma_start(out=out, in_=in_), r, w, stream)

    def cfa(self, name, w=128):
        o = CF_OFF[name]
        return self.cf[:, o:o + w]

    def cba(self, name):
        o = CB_OFF[name]
        return self.cb[:, o:o + 128]

    def psk(self, bank, c0=0, c1=512):
        return [("ps", bank)]

    def bigk(self, lo, hi):
        return [("big", i) for i in range(lo // 512, (hi + 511) // 512)]

    def load_slab(self, src_ap, nc_, ncols):
        i = self.slab_i % 2
        self.slab_i += 1
        t = self.slabs[i]
        view = t[:, 0:nc_ * ncols].rearrange("p (c n) -> p c n", c=nc_)
        self.dma("pool", view, src_ap, [], [("slab", i)], "slab%d" % i)
        return ("slab", i), view

    def build(self):
        nc = self.nc
        NTOK, T, TB = self.NTOK, self.T, self.TB
        with ExitStack() as es:
            self.es = es
            self.hin = self.dram_in("hT_in", [D, NTOK])
            self.hout = nc.dram_tensor("hT_out", [D, NTOK], F32, kind="ExternalOutput").ap()
            self.cf_d = self.dram_in("cf", [128, CF_W + CB_W])
            self.cv_d = self.dram_in("cvec", [128, CV_W])
            self.rw_d = self.dram_in("rows", [128, RW_W])
            need_g = (not self.skip_mixer) and any(l % 2 == 0 for l in self.layers)
            need_h = (not self.skip_mixer) and any(l % 2 == 1 for l in self.layers)
            self.need_g, self.need_h = need_g, need_h
            if need_g:
                self.gdn_w_in = self.dram_in("gdn_w_in", [2, D, GDN_IN])
                self.gdn_w_out = self.dram_in("gdn_w_out", [2, D, D])
            if need_h:
                self.hgrn_w_in = self.dram_in("hgrn_w_in", [2, D, 4 * D])
                self.hgrn_w_out = self.dram_in("hgrn_w_out", [2, D, D])
                self.lb_d = self.dram_in("lbrows", [128, 4096])
            if self.layers:
                self.mlp_w_up = self.dram_in("mlp_w_up", [4, D, 4 * D])
                self.mlp_w_down = self.dram_in("mlp_w_down", [4, 4 * D, D])
            self.hT = self.sb("hT", [128, NCH, T], F32)
            self.yT = self.sb("yT", [128, NCH, TB], BF16)
            self.slabs = [self.sb("slab%d" % i, [128, 4096], BF16) for i in range(2)]
            self.cv = self.sb("cv", [128, CV_W], F32)
            self.rw = self.sb("rw", [128, RW_W], F32)
            self.cf = self.sb("cff", [128, CF_W], F32)
            self.cb = self.sb("cfb", [128, CB_W], BF16)
            self.rstd = self.sb("rstd", [128, 512], F32)
            self.lnt = self.sb("lnt", [128, 512], F32)
            self.rstd2 = self.sb("rstd2", [128, 512], F32)
            self.lnt2 = self.sb("lnt2", [128, 512], F32)
            self.sqs = self.sb("sqs", [128, 2, 512], BF16)
            self.big = self.sb("big", [128, 32 * TB], BF16)
            self.gate = self.sb("gate", [128, H, TB], BF16)
            self.OT = self.sb("OT", [128, H, TB], BF16)
            self.tmpf = self.sb("tmpf", [128, 1024], F32)
            self.tmpf2 = self.sb("tmpf2", [128, 1024], F32)
            self.Sst = self.sb("Sst", [128, H, 128], F32)
            self.Sbf = self.sb("Sbf", [128, H, 128], BF16)
            self.ps = [es.enter_context(nc.psum_tensor("ps%d" % i, [128, 512], F32)) for i in range(8)]
            if need_g or need_h:
                self.mxf = self.sb("mxf", [128, 2, 1024], F32)
                self.mxb = self.sb("mxb", [128, 2, 12, 128], BF16)
                self.lbt = self.sb("lbt", [128, 2, 1024], F32)
                self.kdt = self.sb("kdt", [128, 1024], BF16)
                self.smallf = self.sb("smallf", [128, 4, 64], F32)
            if need_g:
                self.pre = self.sb("pre", [128, 1, 4, 3 + TB], BF16)
                self.halo = self.sb("halo", [128, 24, 3], BF16)
                self.Gs = self.sb("Gs", [128, self.NCI, 16], F32)
                import os as _os
                if int(_os.environ.get("PADT", "0")):
                    self.padt = self.sb("padt", [128, int(_os.environ.get("PADT"))], BF16)
                self.nr = self.sb("nr", [128, 2, 2, 256], F32)
                self.ysb = self.sb("ysb", [128, 2, 2, 256], F32)
            self.emit_all()
            self.S.emit(nc, es)
        return nc

    def emit_all(self):
        T = self.T
        self.dma("sp", self.cf[:], self.cf_d[:, 0:CF_W], [], ["cf"], "c0")
        self.dma("pool", self.cb[:], self.cf_d[:, CF_W:CF_W + CB_W], [], ["cb"], "c3")
        self.dma("sp", self.cv[:], self.cv_d[:, :], [], ["cv"], "c1")
        self.dma("sp", self.rw[:], self.rw_d[:, :], [], ["rw"], "c2")
        for s in range(self.nseq):
            t0 = s * T
            for c in range(NCH):
                self.dma("sp", self.hT[:, c, :], self.hin[c * 128:(c + 1) * 128, t0:t0 + T], [], [("h", c)], "hin")
            for l in self.layers:
                if self.skip_mixer:
                    pass
                elif l % 2 == 0:
                    self.gdn_layer(l)
                else:
                    self.hgrn_layer(l)
                self.mlp_layer(l)
            if self.final_norm:
                for b in range(T // self.TB):
                    self.rmsnorm_block(b, CV_OFF["norm_final"], out_f32=True)
            for c in range(NCH):
                self.dma("sp", self.hout[c * 128:(c + 1) * 128, t0:t0 + T], self.hT[:, c, :], [("h", c)], [("hout", s, c)], "hout")
        self.S.add("sp", lambda e: None, r=[("hout", s, c) for s in range(self.nseq) for c in range(NCH)], w=["done"])

    def sumsq_rstd(self, srcs, keys, n, scale, bias, pk=0):
        P0 = self.ps[pk]
        pkey = self.psk(pk, 0, n)
        ns = len(srcs)
        for i, (sap, k) in enumerate(zip(srcs, keys)):
            sqv = self.sqs[:, i % 2, 0:n]
            self.actv(sqv, sap, AF.Square, [k], [("sqs", i % 2)])
            self.mm(P0[:, 0:n], self.cba("ONES"), sqv, i == 0, i == ns - 1, [("sqs", i % 2), "cb"], pkey)
        self.actv(self.lnt[:, 0:n], P0[:, 0:n], AF.Ln, pkey, ["lnt"], scale=scale, bias=bias)

    def norm1(self, st, src, skeys, n, scale, bias):
        bank = 0 if st == 0 else 3
        P0 = self.ps[bank]
        pkey = self.psk(bank)
        lnt = self.lnt if st == 0 else self.lnt2
        rstd = self.rstd if st == 0 else self.rstd2
        lk, rk = ("lnt" if st == 0 else "lnt2"), ("rstd" if st == 0 else "rstd2")
        sqv = self.sqs[:, st, 0:n]
        self.actv(sqv, src, AF.Square, skeys, [("sqs", st)])
        self.mm(P0[:, 0:n], self.cba("ONES"), sqv, True, True, [("sqs", st), "cb"], pkey)
        self.actv(lnt[:, 0:n], P0[:, 0:n], AF.Ln, pkey, [lk], scale=scale, bias=bias)
        self.actv(rstd[:, 0:n], lnt[:, 0:n], AF.Exp, [lk], [rk], scale=-0.5)
        return rstd[:, 0:n], rk

    def rmsnorm_block(self, b, cvoff, out_f32=False):
        TB = self.TB
        tsl = slice(b * TB, (b + 1) * TB)
        self.sumsq_rstd([self.hT[:, c, tsl] for c in range(NCH)], [("h", c) for c in range(NCH)], TB, 1.0 / D, EPS)
        self.actv(self.rstd[:, 0:TB], self.lnt[:, 0:TB], AF.Exp, ["lnt"], ["rstd"], scale=-0.5)
        for c in range(NCH):
            if out_f32:
                self.stt(self.hT[:, c, tsl], self.hT[:, c, tsl], self.cv[:, cvoff + c:cvoff + c + 1], self.rstd[:, 0:TB],
                         ALU.mult, ALU.mult, [("h", c), "rstd", "cv"], [("h", c)])
            else:
                self.stt(self.yT[:, c, :], self.hT[:, c, tsl], self.cv[:, cvoff + c:cvoff + c + 1], self.rstd[:, 0:TB],
                         ALU.mult, ALU.mult, [("h", c), "rstd", "cv"], [("y", c)])

    def proj_fm(self, W, col0, ncols, evac):
        TB = self.TB
        yk = [("y", c) for c in range(NCH)]
        Wv = W.rearrange("(c p) n -> p c n", p=128)
        for s0 in range(0, ncols, 512):
            w_ = min(512, ncols - s0)
            sk, wv = self.load_slab(Wv[:, :, col0 + s0:col0 + s0 + w_], 8, w_)
            for m in range(w_ // 128):
                pi = self.pi
                self.pi += 1
                P = self.ps[1 + (pi % 2)]
                pk = self.psk(1 + (pi % 2))
                for c in range(NCH):
                    self.mm(P[:, 0:TB], wv[:, c, m * 128:(m + 1) * 128], self.yT[:, c, :], c == 0, c == NCH - 1, [sk] + yk, pk)
                evac((s0 // 128) + m, P, pk)

    def proj_tm(self, W, col0, ncols, evac):
        TB = self.TB
        yk = [("y", c) for c in range(NCH)]
        Wv = W.rearrange("(c p) n -> p c n", p=128)
        for s0 in range(0, ncols, 512):
            w_ = min(512, ncols - s0)
            sk, wv = self.load_slab(Wv[:, :, col0 + s0:col0 + s0 + w_], 8, w_)
            for tt_ in range(TB // 128):
                pi = self.pi
                self.pi += 1
                P = self.ps[1 + (pi % 2)]
                pk = self.psk(1 + (pi % 2))
                for c in range(NCH):
                    self.mm(P[:, 0:w_], self.yT[:, c, tt_ * 128:(tt_ + 1) * 128], wv[:, c, :], c == 0, c == NCH - 1, [sk] + yk, pk)
                evac(s0 // 512, tt_, P, pk, w_)

    def out_proj(self, W, src, srck, tsl):
        TB = self.TB
        Wv = W.rearrange("(c p) n -> p c n", p=128)
        for s0 in range(0, D, 512):
            sk, wv = self.load_slab(Wv[:, :, s0:s0 + 512], 8, 512)
            for m in range(4):
                pi = self.pi
                self.pi += 1
                P = self.ps[1 + (pi % 2)]
                pk = self.psk(1 + (pi % 2))
                for c in range(NCH):
                    self.mm(P[:, 0:TB], wv[:, c, m * 128:(m + 1) * 128], src[:, c, :], c == 0, c == NCH - 1, [sk, srck(c)], pk)
                mo = s0 // 128 + m
                self.tt("dve", self.hT[:, mo, tsl], P[:, 0:TB], self.hT[:, mo, tsl], ALU.add, pk + [("h", mo)], [("h", mo)])

    def mlp_layer(self, l):
        TB = self.TB
        hid = self.big[:, 0:32 * TB].rearrange("p (c t) -> p c t", c=32)
        hk = lambda c: self.bigk(c * TB * 2, (c + 1) * TB * 2)
        for b in range(self.T // TB):
            tsl = slice(b * TB, (b + 1) * TB)
            self.rmsnorm_block(b, CV_OFF["norm_mlp"] + 8 * l)

            def evac(j, P, pk):
                tf = self.tmpf if (j % 2 == 0) else self.tmpf2
                tk = "tmpf" if (j % 2 == 0) else "tmpf2"
                self.actv(tf[:, 0:TB], P[:, 0:TB], AF.Relu, pk, [tk])
                self.tt("pool", hid[:, j, :], tf[:, 0:TB], tf[:, 0:TB], ALU.mult, [tk], hk(j))
            self.proj_fm(self.mlp_w_up[l], 0, 4 * D, evac)
            Wd = self.mlp_w_down[l].rearrange("(c p) n -> p c n", p=128)
            for m in range(NCH):
                sk, wv = self.load_slab(Wd[:, :, m * 128:(m + 1) * 128], 32, 128)
                pi = self.pi
                self.pi += 1
                P = self.ps[1 + (pi % 2)]
                pk = self.psk(1 + (pi % 2))
                for kc in range(32):
                    self.mm(P[:, 0:TB], wv[:, kc, :], hid[:, kc, :], kc == 0, kc == 31, [sk] + hk(kc), pk)
                self.tt("dve", self.hT[:, m, tsl], P[:, 0:TB], self.hT[:, m, tsl], ALU.add, pk + [("h", m)], [("h", m)])

    def mxfk(self, hp, c0, c1):
        bounds = [0, 128, 256, 896, 1024]
        return [("mxf", hp, q) for q in range(4) if c0 < bounds[q + 1] and c1 > bounds[q]]

    def mbk(self, hp, s0, s1=None):
        s1 = s0 + 1 if s1 is None else s1
        return [("mxb", hp, s) for s in range(s0, s1)]

    def kdk(self, c0, c1):
        return [("kdt", q) for q in range(c0 // 128, (c1 + 127) // 128)]

    def hgrn_layer(self, l):
        TB, NCI = self.TB, self.NCI
        j = l // 2
        W = self.hgrn_w_in[j]
        mb = self.mxb
        lg = self.big[:, 0:8192].bitcast(F32)
        lgk = self.bigk(0, 16384)
        self.dma("sp", lg, self.lb_d[:, :], [], lgk, "lb")
        self.actv(lg, lg, AF.Exp, lgk, lgk)
        t1 = self.mxf[:, 0, :]
        t2 = self.mxf[:, 1, :]
        k1 = self.mxfk(0, 0, 1024)
        k2 = self.mxfk(1, 0, 1024)
        self.tt("dve", t1, lg[:, 0:1024], lg[:, 1024:2048], ALU.add, lgk, k1)
        self.tt("dve", t1, t1, lg[:, 2048:3072], ALU.add, lgk + k1, k1)
        self.tt("dve", t1, t1, lg[:, 3072:4096], ALU.add, lgk + k1, k1)
        self.op("dve", "reciprocal", k1, k1, out=t1, in_=t1)
        if l == 1:
            self.op("dve", "tensor_copy", lgk, k2, out=t2, in_=lg[:, 1024:2048])
        else:
            self.tt("dve", t2, lg[:, 1024:2048], lg[:, 2048:3072], ALU.add, lgk, k2)
            self.tt("dve", t2, t2, lg[:, 3072:4096], ALU.add, lgk + k2, k2)
        LB = self.lbt[:, 0, :]
        OML = self.lbt[:, 1, :]
        self.tt("dve", LB, t2, t1, ALU.mult, k1 + k2, ["lbt"])
        self.ts("dve", OML, LB, -1.0, 1.0, ALU.mult, ALU.add, ["lbt"], ["oml"])
        self.op("dve", "memset", [], [("Sst", h) for h in range(H)], self.Sst[:], 0.0)
        self.op("pool", "memset", [], [("Sbf", h) for h in range(H)], self.Sbf[:], 0.0)
        gbytes = NCI * 4096
        G = self.big[:, 0:gbytes // 2].bitcast(F32).rearrange("p (c n) -> p c n", c=NCI)
        V = self.big[:, gbytes // 2:gbytes // 2 + NCI * 1024].rearrange("p (c n) -> p c n", c=NCI)
        q0 = gbytes + NCI * 2048
        qT = self.big[:, q0 // 2:q0 // 2 + H * TB].rearrange("p (h t) -> p h t", h=H)
        Gk = lambda ci, half: self.bigk(ci * 4096 + half * 2048, ci * 4096 + (half + 1) * 2048)
        Vk = lambda ci, half: self.bigk(gbytes + ci * 2048 + half * 1024, gbytes + ci * 2048 + (half + 1) * 1024)
        qk = lambda h: self.bigk(q0 + h * TB * 2, q0 + (h + 1) * TB * 2)
        qsc = float(128 ** -0.5)
        for b in range(self.T // TB):
            tsl = slice(b * TB, (b + 1) * TB)
            self.rmsnorm_block(b, CV_OFF["norm_mix"] + 8 * l)

            def evac_q(jj, P, pk):
                self.actv(qT[:, jj, :], P[:, 0:TB], AF.Silu, pk, qk(jj))
                self.ts("pool", qT[:, jj, :], qT[:, jj, :], qsc, None, ALU.mult, ALU.bypass, qk(jj), qk(jj))
            self.proj_fm(W, 0, 1024, evac_q)

            def evac_f(si, tt_, P, pk, w_):
                cs = slice(si * 512, (si + 1) * 512)
                a = self.tmpf[:, 0:512]
                self.actv(a, P[:, 0:512], AF.Exp, pk, ["tmpf"], scale=-1.0)
                self.ts("dve", a, a, 1.0, None, ALU.add, ALU.bypass, ["tmpf"], ["tmpf"])
                self.op("dve", "reciprocal", ["tmpf"], ["tmpf"], out=a, in_=a)
                self.tt("dve", a, a, OML[:, cs], ALU.mult, ["tmpf", "oml"], ["tmpf"])
                self.tt("dve", a, a, LB[:, cs], ALU.add, ["tmpf", "lbt"], ["tmpf"])
                self.actv(G[:, tt_, cs], a, AF.Ln, ["tmpf"], Gk(tt_, si))
            self.proj_tm(W, 1024, 1024, evac_f)
            self.proj_tm(W, 2048, 1024, lambda si, tt_, P, pk, w_: self.actv(V[:, tt_, si * 512:(si + 1) * 512], P[:, 0:512], AF.Copy, pk, Vk(tt_, si)))
            self.proj_fm(W, 3072, 1024, lambda jj, P, pk: self.actv(self.gate[:, jj, :], P[:, 0:TB], AF.Silu, pk, [("gate", jj)]))
            for ci in range(NCI):
                csl = slice(ci * 128, (ci + 1) * 128)
                for half in range(2):
                    hs = slice(half * 512, (half + 1) * 512)
                    P = self.ps[3]
                    p3 = self.psk(3)
                    self.mm(P[:, :], self.cfa("SUP"), G[:, ci, hs], True, True, Gk(ci, half) + ["cf"], p3)
                    e1 = self.tmpf[:, 0:512]
                    e2 = self.tmpf2[:, 0:512]
                    self.actv(e1, G[:, ci, hs], AF.Exp, Gk(ci, half), ["tmpf"])
                    self.actv(e2, P[:, :], AF.Exp, p3, ["tmpf2"])
                    self.ts("dve", e1, e1, -1.0, 1.0, ALU.mult, ALU.add, ["tmpf"], ["tmpf"])
                    self.tt("dve", self.kdt[:, hs], e1, e2, ALU.mult, ["tmpf", "tmpf2"], self.kdk(half * 512, (half + 1) * 512))
                def head(h, ci=ci, csl=csl):
                    hp = h % 2
                    hsl = slice(h * 128, (h + 1) * 128)
                    PE0, PE1 = self.ps[4 + 2 * hp], self.ps[5 + 2 * hp]
                    pk0, pk1 = self.psk(4 + 2 * hp), self.psk(5 + 2 * hp, 0, 384)
                    RH = self.cfa("RH", 896)
                    self.mm(PE0[:, :], G[:, ci, hsl], RH[:, 0:512], True, True, Gk(ci, h // 4) + ["cf"], pk0)
                    self.mm(PE1[:, 0:384], G[:, ci, hsl], RH[:, 512:896], True, True, Gk(ci, h // 4) + ["cf"], pk1)
                    E = self.mxf[:, hp, 0:896]
                    ek = self.mxfk(hp, 0, 896)
                    self.actv(E[:, 0:512], PE0[:, :], AF.Exp, pk0, self.mxfk(hp, 0, 512))
                    self.actv(E[:, 512:896], PE1[:, 0:384], AF.Exp, pk1, self.mxfk(hp, 512, 896))
                    kTf = self.mxf[:, hp, 896:1024]
                    kfk = self.mxfk(hp, 896, 1024)
                    self.ts("dve", kTf, E[:, 768:896], -1.0, 1.0, ALU.mult, ALU.add, ek, kfk)
                    KT4 = mb[:, hp, 0:4, :]
                    self.tt("dve", KT4, E[:, 0:512].rearrange("p (r n) -> p r n", r=4),
                            kTf.unsqueeze(1).to_broadcast([128, 4, 128]), ALU.mult, ek + kfk, self.mbk(hp, 0, 4))
                    QQ = mb[:, hp, 4:6, :]
                    self.tt("dve", QQ, E[:, 512:768].rearrange("p (r n) -> p r n", r=2),
                            qT[:, h, csl].unsqueeze(1).to_broadcast([128, 2, 128]), ALU.mult, ek + qk(h), self.mbk(hp, 4, 6))
                    PA = self.ps[1 + hp]
                    pa = self.psk(1 + hp)
                    for r4 in range(4):
                        self.mm(PA[:, r4 * 32:(r4 + 1) * 32], KT4[:, r4, :], QQ[:, 0, r4 * 32:(r4 + 1) * 32], True, True,
                                self.mbk(hp, 0, 6), pa)
                    AT = mb[:, hp, 6, :]
                    self.tt("dve", AT, PA[:, 0:128], self.cba("MASKT"), ALU.mult, pa + ["cb"], self.mbk(hp, 6))
                    hmarks[h] = len(self.S.recording)
                    PO = self.ps[0 if hp == 0 else 3]
                    po0 = po1 = self.psk(0 if hp == 0 else 3)
                    self.mm(PO[:, 128:256], self.kdt[:, hsl], V[:, ci, hsl], True, True, self.kdk(h * 128, (h + 1) * 128) + Vk(ci, h // 4), po1)
                    self.stt(self.Sst[:, h, :], self.Sst[:, h, :], E[:, 767:768], PO[:, 128:256], ALU.mult, ALU.add,
                             [("Sst", h)] + ek + po1, [("Sst", h)])
                    self.mm(PO[:, 0:128], V[:, ci, hsl], AT, True, False, Vk(ci, h // 4) + self.mbk(hp, 6), po0)
                    self.mm(PO[:, 0:128], self.Sbf[:, h, :], QQ[:, 1, :], False, True, [("Sbf", h)] + self.mbk(hp, 5), po0)
                    self.actv(self.OT[:, h, csl], PO[:, 0:128], AF.Copy, po0, [("OT", h)])
                    self.actv(self.Sbf[:, h, :], self.Sst[:, h, :], AF.Copy, [("Sst", h)], [("Sbf", h)])

                hmarks = {}
                hf, ht = [], []
                for h in range(H):
                    self.S.recording = []
                    head(h)
                    L = self.S.recording
                    self.S.recording = None
                    hf.append(L[:hmarks[h]])
                    ht.append(L[hmarks[h]:])
                self.S.merge_lists([hf[0], hf[1]])
                for p in range(3):
                    self.S.merge_lists([ht[2 * p], ht[2 * p + 1], hf[2 * p + 2], hf[2 * p + 3]])
                self.S.merge_lists([ht[6], ht[7]])
            self.sumsq_rstd([self.OT[:, h, :] for h in range(H)], [("OT", h) for h in range(H)], TB, 1.0 / D, EPS)
            self.actv(self.rstd[:, 0:TB], self.lnt[:, 0:TB], AF.Exp, ["lnt"], ["rstd"], scale=-0.5)
            go = CV_OFF["gnorm"] + 8 * j
            for h in range(H):
                self.stt(self.OT[:, h, :], self.OT[:, h, :], self.cv[:, go + h:go + h + 1], self.rstd[:, 0:TB], ALU.mult, ALU.mult,
                         [("OT", h), "cv", "rstd"], [("OT", h)])
                self.tt("pool", self.OT[:, h, :], self.OT[:, h, :], self.gate[:, h, :], ALU.mult, [("OT", h), ("gate", h)], [("OT", h)])
            self.out_proj(self.hgrn_w_out[j], self.OT, lambda c: ("OT", c), tsl)

    def gdn_layer(self, l):
        TB, NCI = self.TB, self.NCI
        j = l // 2
        W = self.gdn_w_in[j]
        QKV = self.big[:, 0:24 * TB].rearrange("p (c t) -> p c t", c=24)
        ck = lambda c: self.bigk(c * TB * 2, (c + 1) * TB * 2)
        nega = self.smallf[:, 0, 0:8]
        self.actv(nega, self.rw[:, 8 * j:8 * j + 8], AF.Exp, ["rw"], ["nega"])
        self.ts("dve", nega, nega, -1.0, None, ALU.mult, ALU.bypass, ["nega"], ["nega"])
        self.op("dve", "memset", [], [("Sst", h) for h in range(H)], self.Sst[:], 0.0)
        self.op("pool", "memset", [], [("Sbf", h) for h in range(H)], self.Sbf[:], 0.0)
        self.op("pool", "memset", [], [("halo", c) for c in range(24)], self.halo[:], 0.0)
        cvo = CV_OFF["conv"] + j * 96
        for b in range(self.T // TB):
            tsl = slice(b * TB, (b + 1) * TB)
            self.rmsnorm_block(b, CV_OFF["norm_mix"] + 8 * l)

            def evac_qkv(jj, P, pk):
                pp = 0
                slot = jj % 4
                pv = self.pre[:, pp, slot, :]
                pkk = ("pre", pp, slot)
                self.actv(pv[:, 3:3 + TB], P[:, 0:TB], AF.Copy, pk, [pkk])
                self.op("pool", "tensor_copy", [("halo", jj)], [pkk], out=pv[:, 0:3], in_=self.halo[:, jj, :])
                acc = self.tmpf[:, 0:TB] if jj % 2 == 0 else self.tmpf2[:, 0:TB]
                ak = "tmpf" if jj % 2 == 0 else "tmpf2"
                wc = cvo + jj * 4
                self.ts("dve", acc, pv[:, 0:TB], self.cv[:, wc:wc + 1], None, ALU.mult, ALU.bypass, [pkk, "cv"], [ak])
                for tap in range(1, 4):
                    self.stt(acc, pv[:, tap:tap + TB], self.cv[:, wc + tap:wc + tap + 1], acc, ALU.mult, ALU.add, [pkk, "cv", ak], [ak])
                self.op("pool", "tensor_copy", [pkk], [("halo", jj)], out=self.halo[:, jj, :], in_=pv[:, TB:TB + 3])
                self.actv(QKV[:, jj, :], acc, AF.Silu, [ak], ck(jj))
            self.proj_fm(W, 0, 3072, evac_qkv)
            self.proj_fm(W, 3072, 1024, lambda jj, P, pk: self.actv(self.gate[:, jj, :], P[:, 0:TB], AF.Silu, pk, [("gate", jj)]))
            if self.dbg_stop == 1:
                continue

            def evac_ab(si, tt_, P, pk, w_):
                x1 = self.smallf[:, 1, 0:16]
                self.tt("dve", x1[:, 0:8], P[:, 0:8], self.rw[:, 16 + 8 * j:24 + 8 * j], ALU.add, pk + ["rw"], ["x1"])
                self.ts("dve", x1[:, 8:16], P[:, 8:16], -1.0, None, ALU.mult, ALU.bypass, pk, ["x1b"])
                self.actv(x1, x1, AF.Exp, ["x1", "x1b"], ["x1", "x1b"])
                self.actv(x1, x1, AF.Ln, ["x1", "x1b"], ["x1", "x1b"], bias=1.0)
                self.tt("dve", self.Gs[:, tt_, 0:8], x1[:, 0:8], nega, ALU.mult, ["x1", "nega"], [("Gs", tt_)])
                self.ts("dve", self.Gs[:, tt_, 8:16], x1[:, 8:16], -1.0, None, ALU.mult, ALU.bypass, ["x1b"], [("Gsb", tt_)])
            self.proj_tm(W, 4096, 16, evac_ab)
            if self.dbg_stop == 2:
                continue
            def l2n(c, st):
                rs, rk = self.norm1(st, QKV[:, c, :], ck(c), TB, 1.0, EPS)
                if c < 8:
                    self.stt(QKV[:, c, :], QKV[:, c, :], float(128 ** -0.5), rs, ALU.mult, ALU.mult, ck(c) + [rk], ck(c))
                else:
                    self.tt("dve", QKV[:, c, :], QKV[:, c, :], rs, ALU.mult, ck(c) + [rk], ck(c))
            for c0 in range(0, 16, 2):
                self.S.interleave([lambda c0=c0: l2n(c0, 0), lambda c0=c0: l2n(c0 + 1, 1)])
            if self.dbg_stop == 3:
                continue
            for ci in range(NCI):
                self.gdn_chunk(ci, QKV, ck)
            if self.dbg_stop >= 4:
                continue
            oo = CV_OFF["onorm"] + j
            def hnorm(h, st):
                rs, rk = self.norm1(st, self.OT[:, h, :], [("OT", h)], TB, 1.0 / 128, EPS)
                self.stt(self.OT[:, h, :], self.OT[:, h, :], self.cv[:, oo:oo + 1], rs, ALU.mult, ALU.mult,
                         [("OT", h), "cv", rk], [("OT", h)])
                self.tt("pool", self.OT[:, h, :], self.OT[:, h, :], self.gate[:, h, :], ALU.mult, [("OT", h), ("gate", h)], [("OT", h)])
            for h0 in range(0, H, 2):
                self.S.interleave([lambda h0=h0: hnorm(h0, 0), lambda h0=h0: hnorm(h0 + 1, 1)])
            self.out_proj(self.gdn_w_out[j], self.OT, lambda c: ("OT", c), tsl)

    def gdn_chunk(self, ci, QKV, ck):
        mb = self.mxb
        csl = slice(ci * 128, (ci + 1) * 128)
        g = self.Gs[:, ci, 0:8]
        lnb = self.Gs[:, ci, 8:16]
        gk, lk = ("Gs", ci), ("Gsb", ci)
        PSm = self.ps[3]
        p3a = self.psk(3, 0, 128)
        self.mm(PSm[:, 0:8], self.cfa("LT"), g, True, False, [gk, "cf"], p3a)
        self.mm(PSm[:, 0:8], self.cfa("ID"), lnb, False, True, [lk, "cf"], p3a)
        self.mm(PSm[:, 8:16], self.cfa("SUP"), g, True, True, [gk, "cf"], p3a)
        self.mm(PSm[:, 16:24], self.cfa("ONES"), g, True, True, [gk, "cf"], p3a)
        self.mm(PSm[:, 24:32], self.cfa("ID"), lnb, True, True, [lk, "cf"], p3a)
        EG = self.smallf[:, 2, 0:32]
        self.actv(EG, PSm[:, 0:32], AF.Exp, p3a, ["EG"])
        negb = self.smallf[:, 3, 0:8]
        self.ts("dve", negb, EG[:, 24:32], -1.0, None, ALU.mult, ALU.bypass, ["EG"], ["negb"])
        if self.dbg_stop == 4:
            return
        nr, ys = self.nr, self.ysb
        marks = {}

        def head(h):
            hp = h % 2
            qTh, kTh, vTh = QKV[:, h, csl], QKV[:, 8 + h, csl], QKV[:, 16 + h, csl]
            qk_, kk_, vk_ = ck(h), ck(8 + h), ck(16 + h)
            b3 = 0 if hp == 0 else 3
            PT = self.ps[b3][:, :].bitcast(BF16)
            p3b = p3c = p3d = self.psk(b3)
            Gb = self.mxf[:, hp, 0:128]
            Gb2 = self.mxf[:, hp, 128:256]
            gbk, gb2k = self.mxfk(hp, 0, 128), self.mxfk(hp, 128, 256)
            self.ts("dve", Gb, self.cfa("SUP"), g[:, h:h + 1], None, ALU.mult, ALU.bypass, ["cf", gk], gbk)
            self.ts("dve", Gb2, self.cfa("LT"), g[:, h:h + 1], None, ALU.mult, ALU.bypass, ["cf", gk], gb2k)
            PD = self.ps[4 + hp]
            pd0, pd1, pd2 = self.psk(4 + hp, 0, 128), self.psk(4 + hp, 128, 256), self.psk(4 + hp, 256, 384)
            self.mm(PD[:, 0:128], self.cfa("LT"), Gb, True, False, ["cf"] + gbk, pd0)
            self.mm(PD[:, 0:128], self.cba("ID"), self.cba("NEGS"), False, True, ["cb"], pd0)
            self.mm(PD[:, 128:256], self.cfa("SUP"), Gb2, True, False, ["cf"] + gb2k, pd1)
            self.mm(PD[:, 128:256], self.cba("ID"), self.cba("NEGCT"), False, True, ["cb"], pd1)
            self.mm(PD[:, 256:384], self.cfa("ONES"), Gb2, True, True, ["cf"] + gb2k, pd2)
            DEC = mb[:, hp, 0:3, :]
            self.actv(DEC[:, 1:3, :], PD[:, 128:384].rearrange("p (a n) -> p a n", a=2), AF.Exp, pd1 + pd2, self.mbk(hp, 1, 3))
            fz = self.mxf[:, hp, :]
            zk = self.mxfk(hp, 256, 896)
            decS, N, No, U64, T64 = fz[:, 256:384], fz[:, 384:512], fz[:, 512:640], fz[:, 640:768], fz[:, 768:896]
            Pm = fz[:, 896:1024]
            pmk = self.mxfk(hp, 896, 1024)
            self.actv(decS, PD[:, 0:128], AF.Exp, pd0, zk)
            PK = self.ps[6 + hp]
            pq0, pq1, pq2, pq3 = (self.psk(6 + hp, 0, 128), self.psk(6 + hp, 128, 256), self.psk(6 + hp, 256, 384),
                                  self.psk(6 + hp, 384, 512))
            self.mm(PK[:, 0:128], kTh, kTh, True, True, kk_, pq0)
            self.mm(PK[:, 128:256], kTh, qTh, True, True, kk_ + qk_, pq1)
            self.stt(N, PK[:, 0:128], negb[:, h:h + 1], decS, ALU.mult, ALU.mult, pq0 + ["negb"] + zk, zk)
            AT = mb[:, hp, 4, :]
            self.tt("dve", AT, PK[:, 128:256], DEC[:, 1, :], ALU.mult, pq1 + self.mbk(hp, 1), self.mbk(hp, 4))
            Nd = nr[:, hp, 0, 0:128]
            self.tt("pool", Nd, N, self.cba("BD"), ALU.mult, zk + ["cb"], [("nrN", hp, 0)])
            self.tt("pool", No, N, self.cba("OD"), ALU.mult, zk + ["cb"], zk)
            P3 = self.ps[b3]
            self.op("pe", "transpose", [("nrN", hp, 0), "cf"], p3b, P3[:, 128:256], Nd, self.cfa("ID"))
            Yd = ys[:, hp, 0, 0:128]
            self.cp(Yd, P3[:, 128:256], p3b, [("ysY", hp, 0)])
            self.mm(PK[:, 256:384], Nd, Yd, True, True, [("nrN", hp, 0), ("ysY", hp, 0)], pq2)
            self.mm(PK[:, 384:512], Yd, Nd, True, True, [("nrN", hp, 0), ("ysY", hp, 0)], pq3)
            self.cp(ys[:, hp, 1, 0:128], PK[:, 256:384], pq2, [("ysY", hp, 1)])
            self.tt("pool", ys[:, hp, 1, 128:256], Yd, self.cfa("ID"), ALU.add, [("ysY", hp, 0), "cf"], [("ysS", hp, 1)])
            self.cp(nr[:, hp, 1, 0:128], PK[:, 384:512], pq3, [("nrN", hp, 1)])
            self.tt("pool", nr[:, hp, 1, 128:256], Nd, self.cfa("ID"), ALU.add, [("nrN", hp, 0), "cf"], [("nrR", hp, 1)])
            cur = 1
            PA_, PB_ = self.ps[4 + hp], self.ps[6 + hp]
            pa01, pb01 = pd0 + pd1, pq0 + pq1
            for lev in range(1, 5):
                nx = 1 - cur
                rk = [("nrN", hp, cur), ("nrR", hp, cur), ("ysY", hp, cur), ("ysS", hp, cur)]
                self.mm(PA_[:, 0:256], nr[:, hp, cur, 0:128], ys[:, hp, cur, :], True, True, rk, pa01)
                self.mm(PB_[:, 0:256], ys[:, hp, cur, 0:128], nr[:, hp, cur, :], True, True, rk, pb01)
                self.cp(ys[:, hp, nx, 0:128], PA_[:, 0:128], pd0, [("ysY", hp, nx)])
                self.tt("dve", ys[:, hp, nx, 128:256], PA_[:, 128:256], ys[:, hp, cur, 128:256], ALU.add, pd1 + [("ysS", hp, cur)], [("ysS", hp, nx)])
                self.cp(nr[:, hp, nx, 0:128], PB_[:, 0:128], pq0, [("nrN", hp, nx)])
                self.tt("dve", nr[:, hp, nx, 128:256], PB_[:, 128:256], nr[:, hp, cur, 128:256], ALU.add, pq1 + [("nrR", hp, cur)], [("nrR", hp, nx)])
                cur = nx
            rk = [("nrN", hp, cur), ("nrR", hp, cur), ("ysY", hp, cur), ("ysS", hp, cur)]
            self.mm(PA_[:, 0:128], nr[:, hp, cur, 0:128], ys[:, hp, cur, 128:256], True, True, rk, pd0)
            self.mm(PB_[:, 0:128], ys[:, hp, cur, 0:128], nr[:, hp, cur, 128:256], True, True, rk, pq0)
            self.tt("dve", U64, PA_[:, 0:128], ys[:, hp, cur, 128:256], ALU.add, pd0 + [("ysS", hp, cur)], zk)
            self.tt("dve", T64, PB_[:, 0:128], nr[:, hp, cur, 128:256], ALU.add, pq0 + [("nrR", hp, cur)], zk)
            self.mm(PA_[:, 0:128], No, U64, True, True, zk, pd0)
            self.cp(Pm, PA_[:, 0:128], pd0, pmk)
            self.mm(PB_[:, 0:128], T64, Pm, True, True, zk + pmk, pq0)
            U = mb[:, hp, 9, :]
            self.tt("dve", U, PB_[:, 0:128], U64, ALU.add, pq0 + zk, self.mbk(hp, 9))
            if self.dbg_stop == 8:
                return
            self.tr(PT[:, 512:640], kTh, kk_, p3c)
            kbgn = mb[:, hp, 10, :]
            kdec = mb[:, hp, 11, :]
            self.ts("dve", kbgn, PT[:, 512:640], EG[:, h:h + 1], -1.0, ALU.mult, ALU.mult, p3c + ["EG"], self.mbk(hp, 10))
            self.ts("dve", kdec, PT[:, 512:640], EG[:, 8 + h:9 + h], None, ALU.mult, ALU.bypass, p3c + ["EG"], self.mbk(hp, 11))
            self.tr(PT[:, 768:896], vTh, vk_, p3d)
            c_vb, c_w, c_vn, c_qd = hp * 128, 256 + hp * 128, 512 + hp * 128, 768 + hp * 128
            vb = self.kdt[:, c_vb:c_vb + 128]
            self.ts("dve", vb, PT[:, 768:896], EG[:, 24 + h:25 + h], None, ALU.mult, ALU.bypass, p3d + ["EG"], self.kdk(c_vb, c_vb + 128))
            qd = self.kdt[:, c_qd:c_qd + 128]
            qdk = self.kdk(c_qd, c_qd + 128)
            self.tt("pool", qd, qTh, DEC[:, 2, :], ALU.mult, qk_ + self.mbk(hp, 2), qdk)
            marks[h] = len(self.S.recording)
            PO = self.ps[1 + hp]
            po = [self.psk(1 + hp) for q in range(4)]
            self.mm(PO[:, 0:128], kbgn, U, True, True, self.mbk(hp, 10) + self.mbk(hp, 9), po[0])
            wTn = self.kdt[:, c_w:c_w + 128]
            self.cp(wTn, PO[:, 0:128], po[0], self.kdk(c_w, c_w + 128))
            self.mm(PO[:, 128:256], U, vb, True, False, self.mbk(hp, 9) + self.kdk(c_vb, c_vb + 128), po[1])
            self.mm(PO[:, 128:256], wTn, self.Sbf[:, h, :], False, True, self.kdk(c_w, c_w + 128) + [("Sbf", h)], po[1])
            vnew = self.kdt[:, c_vn:c_vn + 128]
            vnk = self.kdk(c_vn, c_vn + 128)
            self.cp(vnew, PO[:, 128:256], po[1], vnk)
            self.mm(PO[:, 256:384], self.Sbf[:, h, :], qd, True, False, [("Sbf", h)] + qdk, po[2])
            self.mm(PO[:, 256:384], vnew, AT, False, True, vnk + self.mbk(hp, 4), po[2])
            self.cp(self.OT[:, h, csl], PO[:, 256:384], po[2], [("OT", h)])
            self.mm(PO[:, 384:512], kdec, vnew, True, True, self.mbk(hp, 11) + vnk, po[3])
            self.stt(self.Sst[:, h, :], self.Sst[:, h, :], EG[:, 16 + h:17 + h], PO[:, 384:512], ALU.mult, ALU.add,
                     [("Sst", h), "EG"] + po[3], [("Sst", h)])
            self.cp(self.Sbf[:, h, :], self.Sst[:, h, :], [("Sst", h)], [("Sbf", h)])

        fronts, tails = [], []
        for h in range(H):
            self.S.recording = []
            head(h)
            L = self.S.recording
            self.S.recording = None
            fronts.append(L[:marks[h]])
            tails.append(L[marks[h]:])
        self.S.merge_lists([fronts[0], fronts[1]])
        for p in range(3):
            self.S.merge_lists([tails[2 * p], tails[2 * p + 1], fronts[2 * p + 2], fronts[2 * p + 3]])
        self.S.merge_lists([tails[6], tails[7]])


def make_in_map(inp, hT, prog):
    m = {"hT_in": np.ascontiguousarray(hT, dtype=np.float32), "cf": _consts(), "cvec": _make_cvec(inp), "rows": _make_rows(inp)}
    need_g, need_h = prog.need_g, prog.need_h
    if need_g:
        m["gdn_w_in"] = inp["gdn_w_in"]
        m["gdn_w_out"] = inp["gdn_w_out"]
    if need_h:
        m["hgrn_w_in"] = inp["hgrn_w_in"]
        m["hgrn_w_out"] = inp["hgrn_w_out"]
        m["lbrows"] = _make_lbrows(inp)
    if prog.layers:
        m["mlp_w_up"] = inp["mlp_w_up"]
        m["mlp_w_down"] = inp["mlp_w_down"]
    return m


N_CORES = 8
_CFG = {"T": 2048, "TB": 512, "nseq": 2, "groups": [[0, 1, 2, 3]], "skip_mixer": False}


def kernel(**inp):
    inp = {k: np.asarray(v) for k, v in inp.items()}
    x = inp["x"]
    B, T, Dm = x.shape
    nseq = B // N_CORES
    hT = [np.ascontiguousarray(x[c * nseq:(c + 1) * nseq].reshape(nseq * T, Dm).T) for c in range(N_CORES)]
    groups = _CFG["groups"]
    for gi, layers in enumerate(groups):
        p = Prog(nseq, T, _CFG["TB"], layers, final_norm=(gi == len(groups) - 1))
        p.skip_mixer = _CFG["skip_mixer"]
        nc = p.build()
        base = make_in_map(inp, hT[0], p)
        in_maps = []
        for c in range(N_CORES):
            m = dict(base)
            m["hT_in"] = hT[c]
            in_maps.append(m)
        res = run_bass_kernel_spmd(nc, in_maps, core_ids=list(range(N_CORES)))
        hT = [np.ascontiguousarray(r["hT_out"]) for r in res.results]
    out = np.stack([h.T.reshape(nseq, T, Dm) for h in hT], axis=0).reshape(B, T, Dm)
    return out.astype(np.float32)
```
